# Optimizing a Trainium2 kernel written in Bass

```python
import math
import jax, jax.numpy as jnp
from jax import lax
import numpy as np

D_MODEL = 1024
BATCH = 2
SEQ = 8192
DEPTH = 1

M_HEADS = 4
M_HEAD_DIM = 256
M_WIDTH = M_HEADS * M_HEAD_DIM
M_CHUNK = 64
CONV_WIDTH = 4
N_HEADS = 8
N_KV_GROUPS = 2
N_HPG = N_HEADS // N_KV_GROUPS
N_HEAD_DIM = 64
N_WIDTH = N_HEADS * N_HEAD_DIM
N_KV_WIDTH = N_KV_GROUPS * N_HEAD_DIM
CMP_BLOCK = 32
CMP_STRIDE = 16
CMP_HIDDEN = 128
SEL_BLOCK = 64
SEL_TOPK = 16
WINDOW = 512
Q_BLOCK = 128
ROPE_THETA = 500000.0
ROPE_DIM = N_HEAD_DIM // 4
NORM_EPS = 1e-6
NEG = -1e30
FORCE = 1e9

IN_SPLITS = (
    ("m_q", M_WIDTH), ("m_k", M_WIDTH), ("m_v", M_WIDTH), ("m_o", M_WIDTH), ("m_z", M_WIDTH),
    ("m_i", M_HEADS), ("m_f", M_HEADS),
    ("n_q", N_WIDTH), ("n_kc", N_KV_WIDTH), ("n_vc", N_KV_WIDTH),
    ("n_ks", N_KV_WIDTH), ("n_vs", N_KV_WIDTH), ("n_kw", N_KV_WIDTH), ("n_vw", N_KV_WIDTH),
    ("n_g", 3 * N_HEADS), ("n_z", N_WIDTH),
    ("g_a", D_MODEL), ("g_b", D_MODEL),
)
IN_WIDTH = sum(w for _, w in IN_SPLITS)

kernel_name = "hybrid_mlstm_nsa_gated_block"


def _rmsnorm(x, g):
    xf = x.astype(jnp.float32)
    r = lax.rsqrt(jnp.mean(xf * xf, axis=-1, keepdims=True) + NORM_EPS)
    return (xf * r * g).astype(x.dtype)


def _split_in(u):
    names = [n for n, _ in IN_SPLITS]
    cuts = np.cumsum([w for _, w in IN_SPLITS])[:-1].tolist()
    return dict(zip(names, jnp.split(u, cuts, axis=-1)))


def _causal_dwconv(u, w, b):
    K = w.shape[0]
    T = u.shape[1]
    up = jnp.pad(u, ((0, 0), (K - 1, 0), (0, 0)))
    return b + sum(up[:, j:j + T] * w[j] for j in range(K))


def _partial_rope(u, pos):
    half = ROPE_DIM // 2
    inv = jnp.power(jnp.float32(ROPE_THETA), -jnp.arange(half, dtype=jnp.float32) * (2.0 / ROPE_DIM))
    ang = pos.astype(jnp.float32)[..., None] * inv
    cos = jnp.cos(ang)[:, :, None, :]
    sin = jnp.sin(ang)[:, :, None, :]
    u1 = u[..., :half]
    u2 = u[..., half:ROPE_DIM]
    r1 = u1 * cos - u2 * sin
    r2 = u2 * cos + u1 * sin
    return jnp.concatenate([r1.astype(u.dtype), r2.astype(u.dtype), u[..., ROPE_DIM:]], axis=-1)


def _mlstm(q, k, v, i_pre, f_pre):
    B, T, H, d = q.shape
    L = M_CHUNK
    NC = T // L
    k = k * (d ** -0.5)

    def to_chunks(a):
        a = a.reshape((B, NC, L, H) + a.shape[3:])
        return jnp.moveaxis(a, (1, 3), (0, 2))

    log_f = jax.nn.log_sigmoid(f_pre)
    log_i = i_pre
    causal = jnp.tril(jnp.ones((L, L), dtype=bool))

    def step(carry, xs):
        C, n, m = carry
        qc, kc, vc, li, lf = xs
        a = jnp.cumsum(lf, axis=-1)
        A = a[..., -1]
        logD = a[..., :, None] - a[..., None, :] + li[..., None, :]
        logD = jnp.where(causal, logD, -jnp.inf)
        inter = a + m[..., None]
        m_t = jnp.maximum(inter, jnp.max(logD, axis=-1))
        w_inter = jnp.exp(inter - m_t)
        s = jnp.einsum('bhld,bhsd->bhls', qc, kc) * jnp.exp(logD - m_t[..., None])
        num = w_inter[..., None] * jnp.einsum('bhld,bhde->bhle', qc, C) + jnp.einsum('bhls,bhse->bhle', s, vc)
        den = w_inter * jnp.einsum('bhld,bhd->bhl', qc, n) + jnp.sum(s, axis=-1)
        h = num / jnp.maximum(jnp.abs(den), jnp.exp(-m_t))[..., None]
        logw = A[..., None] - a + li
        m_new = jnp.maximum(A + m, jnp.max(logw, axis=-1))
        wk = jnp.exp(logw - m_new[..., None])
        decay = jnp.exp(A + m - m_new)
        C_new = decay[..., None, None] * C + jnp.einsum('bhs,bhsd,bhse->bhde', wk, kc, vc)
        n_new = decay[..., None] * n + jnp.einsum('bhs,bhsd->bhd', wk, kc)
        return (C_new, n_new, m_new), h

    init = (jnp.zeros((B, H, d, d), jnp.float32), jnp.zeros((B, H, d), jnp.float32),
            jnp.zeros((B, H), jnp.float32))
    xs = (to_chunks(q), to_chunks(k), to_chunks(v), to_chunks(log_i), to_chunks(log_f))
    _, hs = lax.scan(step, init, xs)
    hs = jnp.moveaxis(hs, (0, 2), (1, 3))
    return hs.reshape(B, T, H, d)


def _head_layernorm(h, g):
    B, T = h.shape[0], h.shape[1]
    mu = jnp.mean(h, axis=-1, keepdims=True)
    var = jnp.mean(jnp.square(h - mu), axis=-1, keepdims=True)
    hn = (h - mu) * lax.rsqrt(var + NORM_EPS)
    return hn.reshape(B, T, -1) * g


def _compress(kv, pos_emb, w1, w2):
    B, T, G, dk = kv.shape
    n_cmp = (T - CMP_BLOCK) // CMP_STRIDE + 1
    idx = jnp.arange(n_cmp)[:, None] * CMP_STRIDE + jnp.arange(CMP_BLOCK)[None, :]
    blocks = kv[:, idx] + pos_emb[None, None, :, None, :]
    blocks = jnp.moveaxis(blocks, 3, 1).reshape(B, G, n_cmp, CMP_BLOCK * dk)
    return jax.nn.gelu(blocks @ w1) @ w2


def _nsa(q_raw, q_rope, kc, vc, ks, vs, kw, vw, gates):
    B, G, h, T, dk = q_raw.shape
    n_cmp = kc.shape[2]
    n_sel = T // SEL_BLOCK
    n_top = min(SEL_TOPK, n_sel)
    scale = dk ** -0.5
    cmp_end = jnp.arange(n_cmp) * CMP_STRIDE + (CMP_BLOCK - 1)
    ci = jnp.arange(n_cmp)[:, None] * CMP_STRIDE
    sj = jnp.arange(n_sel)[None, :] * SEL_BLOCK
    overlap = ((ci < sj + SEL_BLOCK) & (ci + CMP_BLOCK > sj)).astype(jnp.float32)
    ks_blocks = ks.reshape(B, G, n_sel, SEL_BLOCK, dk)
    vs_blocks = vs.reshape(B, G, n_sel, SEL_BLOCK, dk)
    kw_pad = jnp.pad(kw, ((0, 0), (0, 0), (WINDOW, 0), (0, 0)))
    vw_pad = jnp.pad(vw, ((0, 0), (0, 0), (WINDOW, 0), (0, 0)))
    bi = jnp.arange(B)[:, None, None, None]
    gi = jnp.arange(G)[None, :, None, None]
    blk = jnp.arange(n_sel)

    def block(i):
        t0 = i * Q_BLOCK
        tq = t0 + jnp.arange(Q_BLOCK)
        qr = lax.dynamic_slice_in_dim(q_raw, t0, Q_BLOCK, axis=3)
        qp = lax.dynamic_slice_in_dim(q_rope, t0, Q_BLOCK, axis=3)
        g = lax.dynamic_slice_in_dim(gates, t0, Q_BLOCK, axis=3)
        s_c = jnp.einsum('bghqd,bgnd->bghqn', qr, kc).astype(jnp.float32) * scale
        m_c = cmp_end[None, :] <= tq[:, None]
        p_c = jax.nn.softmax(jnp.where(m_c, s_c, NEG), axis=-1) * m_c
        o_c = jnp.einsum('bghqn,bgnd->bghqd', p_c.astype(vc.dtype), vc)
        imp = jnp.einsum('bghqn,ns->bgqs', p_c, overlap)
        cur = tq // SEL_BLOCK
        forced = (blk[None, :] == 0) | (blk[None, :] == cur[:, None]) | (blk[None, :] == cur[:, None] - 1)
        valid = blk[None, :] * SEL_BLOCK <= tq[:, None]
        score = jnp.where(forced, FORCE, jnp.where(valid, imp, -1.0))
        _, idx = lax.top_k(score, n_top)
        kg = ks_blocks[bi, gi, idx]
        vg = vs_blocks[bi, gi, idx]
        kpos = idx[..., None] * SEL_BLOCK + jnp.arange(SEL_BLOCK)
        m_s = kpos <= tq[:, None, None]
        s_s = jnp.einsum('bghqd,bgqnsd->bghqns', qp, kg).astype(jnp.float32) * scale
        s_s = jnp.where(m_s[:, :, None], s_s, NEG).reshape(B, G, h, Q_BLOCK, n_top * SEL_BLOCK)
        p_s = jax.nn.softmax(s_s, axis=-1).reshape(B, G, h, Q_BLOCK, n_top, SEL_BLOCK)
        o_s = jnp.einsum('bghqns,bgqnsd->bghqd', p_s.astype(vg.dtype), vg)
        kwb = lax.dynamic_slice_in_dim(kw_pad, t0, Q_BLOCK + WINDOW, axis=2)
        vwb = lax.dynamic_slice_in_dim(vw_pad, t0, Q_BLOCK + WINDOW, axis=2)
        wpos = t0 - WINDOW + jnp.arange(Q_BLOCK + WINDOW)
        diff = tq[:, None] - wpos[None, :]
        m_w = (diff >= 0) & (diff < WINDOW) & (wpos[None, :] >= 0)
        s_w = jnp.einsum('bghqd,bgkd->bghqk', qp, kwb).astype(jnp.float32) * scale
        p_w = jax.nn.softmax(jnp.where(m_w, s_w, NEG), axis=-1)
        o_w = jnp.einsum('bghqk,bgkd->bghqd', p_w.astype(vwb.dtype), vwb)
        return g[..., 0:1] * o_c + g[..., 1:2] * o_s + g[..., 2:3] * o_w

    out = lax.map(block, jnp.arange(T // Q_BLOCK))
    out = jnp.moveaxis(out, 0, 3).reshape(B, G, h, T, dk)
    return out.transpose(0, 3, 1, 2, 4).reshape(B, T, G * h * dk)


def setup_inputs(seed: int = 0) -> dict:
    key = jax.random.key(seed)
    ks = jax.random.split(key, 24)
    f32 = jnp.float32

    def nrm(k, shape, scale):
        return jax.random.normal(k, shape, f32) * scale

    x = nrm(ks[0], (BATCH, SEQ, D_MODEL), 1.0)
    positions = jnp.tile(jnp.arange(SEQ, dtype=jnp.int32)[None, :], (BATCH, 1))
    norm_g = 1.0 + nrm(ks[1], (DEPTH, D_MODEL), 0.05)
    w_in = nrm(ks[2], (DEPTH, D_MODEL, IN_WIDTH), D_MODEL ** -0.5)
    conv_q_w = nrm(ks[3], (DEPTH, CONV_WIDTH, M_WIDTH), CONV_WIDTH ** -0.5)
    conv_q_b = nrm(ks[4], (DEPTH, M_WIDTH), 0.02)
    conv_k_w = nrm(ks[5], (DEPTH, CONV_WIDTH, M_WIDTH), CONV_WIDTH ** -0.5)
    conv_k_b = nrm(ks[6], (DEPTH, M_WIDTH), 0.02)
    b_igate = nrm(ks[7], (DEPTH, M_HEADS), 0.1)
    b_fgate = jnp.linspace(3.0, 6.0, M_HEADS, dtype=f32)[None, :] + nrm(ks[8], (DEPTH, M_HEADS), 0.1)
    mh_norm_g = 1.0 + nrm(ks[9], (DEPTH, M_WIDTH), 0.05)
    cmp_pos_k = nrm(ks[10], (DEPTH, CMP_BLOCK, N_HEAD_DIM), 0.1)
    cmp_w1_k = nrm(ks[11], (DEPTH, CMP_BLOCK * N_HEAD_DIM, CMP_HIDDEN), (CMP_BLOCK * N_HEAD_DIM) ** -0.5)
    cmp_w2_k = nrm(ks[12], (DEPTH, CMP_HIDDEN, N_HEAD_DIM), CMP_HIDDEN ** -0.5)
    cmp_pos_v = nrm(ks[13], (DEPTH, CMP_BLOCK, N_HEAD_DIM), 0.1)
    cmp_w1_v = nrm(ks[14], (DEPTH, CMP_BLOCK * N_HEAD_DIM, CMP_HIDDEN), (CMP_BLOCK * N_HEAD_DIM) ** -0.5)
    cmp_w2_v = nrm(ks[15], (DEPTH, CMP_HIDDEN, N_HEAD_DIM), CMP_HIDDEN ** -0.5)
    b_nsa_gate = nrm(ks[16], (DEPTH, 3 * N_HEADS), 0.1)
    w_branch_a = nrm(ks[17], (DEPTH, M_WIDTH, D_MODEL), M_WIDTH ** -0.5)
    w_branch_b = nrm(ks[18], (DEPTH, N_WIDTH, D_MODEL), N_WIDTH ** -0.5)
    w_out = nrm(ks[19], (DEPTH, D_MODEL, D_MODEL), D_MODEL ** -0.5)
    final_norm_g = 1.0 + nrm(ks[20], (D_MODEL,), 0.05)
    return {"x": x, "positions": positions, "norm_g": norm_g, "w_in": w_in,
            "conv_q_w": conv_q_w, "conv_q_b": conv_q_b, "conv_k_w": conv_k_w, "conv_k_b": conv_k_b,
            "b_igate": b_igate, "b_fgate": b_fgate, "mh_norm_g": mh_norm_g,
            "cmp_pos_k": cmp_pos_k, "cmp_w1_k": cmp_w1_k, "cmp_w2_k": cmp_w2_k,
            "cmp_pos_v": cmp_pos_v, "cmp_w1_v": cmp_w1_v, "cmp_w2_v": cmp_w2_v,
            "b_nsa_gate": b_nsa_gate, "w_branch_a": w_branch_a, "w_branch_b": w_branch_b,
            "w_out": w_out, "final_norm_g": final_norm_g}


def reference(x, positions, norm_g, w_in, conv_q_w, conv_q_b, conv_k_w, conv_k_b, b_igate, b_fgate,
              mh_norm_g, cmp_pos_k, cmp_w1_k, cmp_w2_k, cmp_pos_v, cmp_w1_v, cmp_w2_v, b_nsa_gate,
              w_branch_a, w_branch_b, w_out, final_norm_g):
    B, T, _ = x.shape
    f32 = jnp.float32
    for l in range(DEPTH):
        hN = _rmsnorm(x, norm_g[l])
        p = _split_in(hN @ w_in[l])
        mq = jax.nn.silu(_causal_dwconv(p["m_q"], conv_q_w[l], conv_q_b[l]))
        mk = jax.nn.silu(_causal_dwconv(p["m_k"], conv_k_w[l], conv_k_b[l]))
        mh = _mlstm(mq.reshape(B, T, M_HEADS, M_HEAD_DIM).astype(f32),
                    mk.reshape(B, T, M_HEADS, M_HEAD_DIM).astype(f32),
                    p["m_v"].reshape(B, T, M_HEADS, M_HEAD_DIM).astype(f32),
                    (p["m_i"] + b_igate[l]).astype(f32),
                    (p["m_f"] + b_fgate[l]).astype(f32))
        mh = _head_layernorm(mh, mh_norm_g[l]).astype(x.dtype)
        ya = jax.nn.sigmoid(p["m_o"]) * mh * jax.nn.silu(p["m_z"])
        nq = p["n_q"].reshape(B, T, N_HEADS, N_HEAD_DIM)
        q_raw = nq.reshape(B, T, N_KV_GROUPS, N_HPG, N_HEAD_DIM).transpose(0, 2, 3, 1, 4)
        q_rope = _partial_rope(nq, positions).reshape(B, T, N_KV_GROUPS, N_HPG, N_HEAD_DIM).transpose(0, 2, 3, 1, 4)

        def kvh(a):
            return a.reshape(B, T, N_KV_GROUPS, N_HEAD_DIM)

        kc = _compress(kvh(p["n_kc"]), cmp_pos_k[l], cmp_w1_k[l], cmp_w2_k[l])
        vc = _compress(kvh(p["n_vc"]), cmp_pos_v[l], cmp_w1_v[l], cmp_w2_v[l])
        ks = _partial_rope(kvh(p["n_ks"]), positions).transpose(0, 2, 1, 3)
        vs = kvh(p["n_vs"]).transpose(0, 2, 1, 3)
        kw = _partial_rope(kvh(p["n_kw"]), positions).transpose(0, 2, 1, 3)
        vw = kvh(p["n_vw"]).transpose(0, 2, 1, 3)
        gates = jax.nn.sigmoid(p["n_g"] + b_nsa_gate[l]).reshape(B, T, N_KV_GROUPS, N_HPG, 3).transpose(0, 2, 3, 1, 4)
        yn = _nsa(q_raw, q_rope, kc, vc, ks, vs, kw, vw, gates)
        yb = yn * jax.nn.silu(p["n_z"])
        merged = (jax.nn.sigmoid(p["g_a"]) * (ya @ w_branch_a[l])
                  + jax.nn.sigmoid(p["g_b"]) * (yb @ w_branch_b[l]))
        x = x + merged @ w_out[l]
    return _rmsnorm(x, final_norm_g)
```

```python
import math
from contextlib import ExitStack

import numpy as np
import concourse.bass as bass
import concourse.mybir as mybir
from concourse.bass_utils import run_bass_kernel_spmd

F32 = mybir.dt.float32
BF16 = mybir.dt.bfloat16
I32 = mybir.dt.int32
AF = mybir.ActivationFunctionType
ALU = mybir.AluOpType
AX = mybir.AxisListType

D = 1024
NEGB = -1.0e30
MASKNEG = -16384.0
LN16 = math.log(16.0)
EPS = 1e-6
ENGS = ("tensor", "vector", "scalar", "gpsimd", "sync")
SEM_ROLL = 12000
import os
SEQ_MERGE = False
MERGE_MODE = int(os.environ.get('MERGE_MODE', '3'))
H3_TEST = 0
TAIL_MERGE = bool(int(os.environ.get('TAIL_MERGE', '1')))


class Prog:
    def __init__(self, nc):
        self.nc = nc
        self.q = {e: [] for e in ENGS}
        self.cnt = {e: 0 for e in ENGS}
        self.gen = {e: 0 for e in ENGS}
        self.last_w = {}
        self.readers = {}
        self.seen = {e: {} for e in ENGS}
        self.dma_cnt = {}
        self.semnames = []
        self.cap = None

    def _deps(self, eng, reads, writes):
        deps = []
        for r in reads:
            if r in self.last_w:
                deps.append(self.last_w[r])
        for w in writes:
            if w in self.last_w:
                deps.append(self.last_w[w])
            deps += self.readers.get(w, [])
        waits = {}
        for sk, v in deps:
            if v > waits.get(sk, 0):
                waits[sk] = v
        out = []
        for sk, v in waits.items():
            if self.seen[eng].get(sk, 0) >= v:
                continue
            self.seen[eng][sk] = v
            out.append((sk, v))
        return out

    def _commit(self, token, reads, writes):
        for r in reads:
            self.readers.setdefault(r, []).append(token)
        for w in writes:
            self.last_w[w] = token
            self.readers[w] = []

    def _semkey(self, name):
        if name not in self.semnames:
            self.semnames.append(name)
        return name

    def replay_merged(self, *lists):
        lists = [l for l in lists if l]
        pos = [0] * len(lists)
        while True:
            best, bf = -1, 2.0
            for j, l in enumerate(lists):
                if pos[j] < len(l):
                    f = pos[j] / float(len(l))
                    if f < bf:
                        best, bf = j, f
            if best < 0:
                break
            kind, args = lists[best][pos[best]]
            pos[best] += 1
            (self.op if kind == "op" else self.dma)(*args)

    def op(self, eng, fn, reads=(), writes=()):
        if self.cap is not None:
            self.cap.append(("op", (eng, fn, reads, writes)))
            return
        waits = self._deps(eng, reads, writes)
        self.cnt[eng] += 1
        idx = self.cnt[eng]
        if eng == "tensor":
            self.seen[eng][eng] = idx
        self.q[eng].append((fn, waits, eng, 1, idx))
        self._commit((eng, idx), reads, writes)

    def dma(self, eng, fn, sem, reads=(), writes=()):
        if self.cap is not None:
            self.cap.append(("dma", (eng, fn, sem, reads, writes)))
            return
        waits = self._deps(eng, reads, writes)
        sk = self._semkey("d_" + sem)
        self.dma_cnt[sk] = self.dma_cnt.get(sk, 0) + 16
        token = (sk, self.dma_cnt[sk])
        self.q[eng].append((fn, waits, sk, 16, 0))
        self._commit(token, reads, writes)

    def final_wait(self, eng, keys):
        waits = self._deps(eng, keys, ())
        have = dict(waits)
        for sk, cnt in self.dma_cnt.items():
            if have.get(sk, 0) < cnt:
                have[sk] = cnt
        self.q[eng].append((None, list(have.items()), None, 0, 0))

    def barrier(self):
        for eng in ENGS:
            waits = [(e2, self.cnt[e2]) for e2 in ENGS if e2 != eng and self.cnt[e2] > 0]
            waits += list(self.dma_cnt.items())
            self.q[eng].append((None, waits, None, 0, 0))

    def plan(self):
        ms = {e: set() for e in ENGS}
        for e in ENGS:
            for fn, waits, sk, inc, idx in self.q[e]:
                for wk, v in waits:
                    if wk in ms:
                        ms[wk].add(v)
        rank = {}
        for e in ENGS:
            rank[e] = {idx: r for r, idx in enumerate(sorted(ms[e]))}
        return rank

    def emit(self, es):
        nc = self.nc
        rank = self.plan()
        names = list(self.semnames)
        for e in ENGS:
            ngen = (len(rank[e]) + SEM_ROLL - 1) // SEM_ROLL
            for g in range(ngen):
                names.append("e_%s_%d" % (e, g))
        sems = {}
        for name in names:
            sems[name] = es.enter_context(nc.semaphore(name))
        block = es.enter_context(nc.Block())
        q = self.q

        def esem(eng_name, idx):
            r = rank[eng_name][idx]
            return sems["e_%s_%d" % (eng_name, r // SEM_ROLL)], r % SEM_ROLL + 1

        def run(eng_name, e):
            for fn, waits, sk, inc, idx in q[eng_name]:
                for wk, v in waits:
                    if wk in rank:
                        sm_, val = esem(wk, v)
                        e.wait_ge(sm_, val)
                    else:
                        e.wait_ge(sems[wk], v)
                if fn is not None:
                    ins = fn(e)
                    if sk in rank:
                        if idx in rank[sk]:
                            ins.then_inc(esem(sk, idx)[0], 1)
                    else:
                        ins.then_inc(sems[sk], inc)

        @block.tensor
        def _(e):
            run("tensor", e)

        @block.vector
        def _(e):
            run("vector", e)

        @block.scalar
        def _(e):
            run("scalar", e)

        @block.gpsimd
        def _(e):
            run("gpsimd", e)

        @block.sync
        def _(e):
            run("sync", e)


NFM = 640
NTM = 1416
TMA, TMB, TMC = 512, 512, 392
C_QOWN, C_KS, C_KW, C_VS, C_VW, C_I, C_F, C_G = 0, 128, 192, 256, 320, 384, 385, 386
B_Z, B_QOTH, B_NZ = 0, 256, 384


def build_mixer(T, do_nsa=True):
    NT = T // 128
    nc = bass.Bass("TRN2", target_bir_lowering=False)
    P = Prog(nc)
    es = ExitStack()

    def din(name, shape, dt=F32):
        return nc.dram_tensor(name, list(shape), dt, kind="ExternalInput").ap()

    x_d = din("x", [T, D])
    pos_d = din("pos", [128, NT], I32)
    wfm_d = din("wfm", [D, NFM])
    wtm_d = din("wtm", [D, NTM])
    ng_d = din("ng", [128, 8])
    cw_d = din("convw", [128, 4, 4])
    cb_d = din("convb", [128, 4])
    misc_d = din("misc", [128, 16])
    mhg_d = din("mhg", [128, 256])
    c128_d = din("c128", [128, 6, 128])
    ya_d = nc.dram_tensor("ya", [T, 256], BF16, kind="ExternalOutput").ap()
    yb_d = nc.dram_tensor("yb", [T, 128], BF16, kind="ExternalOutput").ap()
    if do_nsa:
        w1_d = din("cw1", [64, 2, 32, 128])
        w2_d = din("cw2", [128, 2, 64])
        cpos_d = din("cpos", [64, 2, 32])
        raug_d = din("raug", [128, 4, 193])
        E_d = din("E", [128, T])
        sel_d = din("selc", [128, 4, 256])
        nm_d = din("negm", [128, 2, 256])

    def sb(name, shape, dt=F32):
        return es.enter_context(nc.sbuf_tensor(name, list(shape), dt))

    def pst(name, shape, dt=F32):
        return es.enter_context(nc.psum_tensor(name, list(shape), dt))

    Wfm = sb("Wfm", [128, 8, NFM], BF16)
    Wtm = sb("Wtm", [128, 8, NTM], BF16)
    stg = sb("stg", [128, 2, NTM], F32)
    ngs = sb("ngs", [128, 8])
    cws = sb("cws", [128, 4, 4])
    cbs = sb("cbs", [128, 4])
    misc = sb("miscs", [128, 16])
    mhg = sb("mhgs", [128, 256])
    c128 = sb("c128s", [128, 6, 128])
    identb = sb("identb", [128, 128], BF16)
    ident = c128[:, 0, :]
    Utri = c128[:, 1, :]
    ones = c128[:, 2, :]
    cnegB = c128[:, 3, :]
    maskT01 = c128[:, 4, :]
    xt = sb("xt", [128, 2, D])
    junk = sb("junk", [128, D], BF16)
    hn = sb("hn", [128, 2, D], BF16)
    hT = sb("hT", [128, 2, 8, 128], BF16)
    sm = sb("sm", [128, 2, 32])
    convbuf = sb("convbuf", [128, 4, 131])
    cacc = sb("cacc", [128, 4, 128])
    qkT = sb("qkT", [128, 2, 4, 128], BF16)
    Vaug = sb("Vaug", [128, 2, 257], BF16)
    og = sb("og", [128, 2, 256])
    zg = sb("zg", [128, 2, 256])
    G2 = sb("G2", [128, 2, 2])
    dbg = sb("dbg", [128, 2, 128])
    Bm = sb("Bm", [128, 128])
    DT = sb("DT", [128, 128])
    DTm = sb("DTm", [128, 128], BF16)
    Wm = sb("Wm", [128, 128])
    qwT = sb("qwT", [128, 2, 128], BF16)
    smT = sb("smT", [128, 128], BF16)
    hbuf = sb("hbuf", [128, 256])
    hbuf2 = sb("hbuf2", [128, 256])
    bnst = sb("bnst", [128, 8])
    kwt = sb("kwt", [128, 256], BF16)
    ktok = sb("ktok", [128, 2, 256], BF16)
    Cst = sb("Cst", [128, 2, 257])
    Cb = sb("Cb", [128, 2, 257], BF16)
    mcol = sb("mcol", [128, 1])
    epsc = sb("epsc", [128, 1])
    negpi = sb("negpi", [128, 1])
    tmC = sb("tmC", [128, 2, TMC])
    yat = sb("yat", [128, 2, 256], BF16)
    ybt = sb("ybt", [128, 2, 128], BF16)

    if do_nsa:
        W1b = sb("W1b", [64, 2, 32, 128], BF16)
        W2b = sb("W2b", [128, 2, 64], BF16)
        posT = sb("posT", [64, 2, 32], BF16)
        biash = sb("biash", [128, 2])
        cpre = sb("cpre", [64, 2, 2, 144], BF16)
        kcT = sb("kcT", [64, 512], BF16)
        GV = sb("GV", [128, 512], BF16)
        Raug = sb("Raug", [128, 4, 193], BF16)
        ksT = sb("ksT", [64, T], BF16)
        kwT = sb("kwT", [64, 8 * 128], BF16)
        Vs = sb("Vs", [128, NT, 65], BF16)
        Vw = sb("Vw", [128, 8, 65], BF16)
        Eb = sb("Eb", [128, T], BF16)
        selc = sb("selcs", [128, 4, 256])
        negm = sb("negms", [128, 2, 256], BF16)
        posi = sb("posi", [128, NT], I32)
        posf = sb("posf", [128, NT])
        ang = sb("ang", [128, NT, 8])
        ang2 = sb("ang2", [128, NT, 8])
        ang3 = sb("ang3", [128, NT, 8])
        angi = sb("angi", [128, NT, 8], I32)
        cossin = sb("cossin", [128, NT, 16])
        qtok = sb("qtok", [128, 2, 4, 64])
        rp = sb("rp", [128, 4, 64])
        rtmp = sb("rtmp", [128, 4, 4, 8])
        qrawT = sb("qrawT", [64, 512], BF16)
        qropT = sb("qropT", [64, 256], BF16)
        PcT = sb("PcT", [128, 4, 512], BF16)
        PT = sb("PT", [128, 2, 512], BF16)
        cmask = sb("cmask", [128, 128], BF16)
        hx = sb("hx", [128, 4, 16])
        gl = sb("gl", [128, 16], BF16)
        imp = sb("imp", [128, 128])
        sc = sb("sc", [128, 2, 128])
        m8 = sb("m8", [128, 16])
        sel01 = sb("sel01", [128, 128])
        NegSel = sb("NegSel", [128, 128])
        NegSelT = sb("NegSelT", [128, 256], BF16)
        gsig = sb("gsig", [128, 8])
        nzg = sb("nzg", [128, 2, 128])
        ynacc = sb("ynacc", [128, 2, 64])
        rr = sb("rr", [128, 16])

    ps = [pst("ps%d" % i, [128, 512]) for i in range(5)]
    ps5 = pst("ps5", [128, 1024], BF16)
    ps6 = pst("ps6", [128, 512])
    ps7 = pst("ps7", [128, 512])

    V, S, G, TE, SY = "vector", "scalar", "gpsimd", "tensor", "sync"

    P.dma(SY, lambda e: e.dma_start(out=ngs[:], in_=ng_d), "c0", (), ("ngs",))
    P.dma(SY, lambda e: e.dma_start(out=cws[:], in_=cw_d), "c1", (), ("cws",))
    P.dma(SY, lambda e: e.dma_start(out=cbs[:], in_=cb_d), "c2", (), ("cbs",))
    P.dma(SY, lambda e: e.dma_start(out=misc[:], in_=misc_d), "c3", (), ("misc",))
    P.dma(SY, lambda e: e.dma_start(out=mhg[:], in_=mhg_d), "c4", (), ("mhg",))
    P.dma(SY, lambda e: e.dma_start(out=c128[:], in_=c128_d), "c5", (), ("c128",))
    P.dma(G, lambda e: e.dma_start(out=identb[:], in_=c128_d[:, 0, :]), "c6", (), ("identb",))
    for k in range(8):
        sp_ = k % 2
        for (wd, Wt, ncol, nm) in ((wfm_d, Wfm, NFM, "f"), (wtm_d, Wtm, NTM, "t")):
            P.dma(SY, lambda e, wd=wd, ncol=ncol, k=k, sp_=sp_: e.dma_start(out=stg[:, sp_, 0:ncol], in_=wd[k * 128:(k + 1) * 128, :]),
                  "w%d" % sp_, (), ("stg%d" % sp_,))
            P.op(V, lambda e, Wt=Wt, ncol=ncol, k=k, sp_=sp_: e.tensor_scalar(out=Wt[:, k, :], in0=stg[:, sp_, 0:ncol], scalar1=ngs[:, k:k + 1], scalar2=None, op0=ALU.mult),
                 ("stg%d" % sp_, "ngs"), ("W",))
    P.op(V, lambda e: e.memset(epsc[:], EPS), (), ("epsc",))
    P.op(V, lambda e: e.memset(negpi[:], -math.pi), (), ("negpi",))
    P.op(V, lambda e: e.memset(Cst[:], 0.0), (), ("Cst",))
    P.op(V, lambda e: e.memset(Cb[:], 0.0), (), ("Cb",))
    P.op(V, lambda e: e.memset(mcol[:], 0.0), (), ("m",))
    P.op(V, lambda e: e.memset(convbuf[:], 0.0), (), ("convbuf",))
    P.op(V, lambda e: e.memset(Vaug[:], 1.0), (), ("Vaug0", "Vaug1"))

    if do_nsa:
        P.dma(G, lambda e: e.dma_start(out=W1b[:], in_=w1_d), "n0", (), ("W1b",))
        P.dma(G, lambda e: e.dma_start(out=W2b[:], in_=w2_d), "n1", (), ("W2b",))
        P.dma(G, lambda e: e.dma_start(out=posT[:], in_=cpos_d), "n2", (), ("posT",))
        P.dma(G, lambda e: e.dma_start(out=Raug[:], in_=raug_d), "n3", (), ("Raug",))
        P.dma(G, lambda e: e.dma_start(out=negm[:], in_=nm_d), "n4", (), ("negm",))
        P.dma(SY, lambda e: e.dma_start(out=selc[:], in_=sel_d), "n5", (), ("selc",))
        P.dma(SY, lambda e: e.dma_start(out=posi[:], in_=pos_d), "n6", (), ("posi",))
        for c0 in range(0, T, 2048):
            c1 = min(T, c0 + 2048)
            P.dma(G, lambda e, c0=c0, c1=c1: e.dma_start(out=Eb[:, c0:c1], in_=E_d[:, c0:c1]), "n7", (), ("Eb",))
        P.op(V, lambda e: e.memset(cpre[:], 0.0), (), ("cpre0", "cpre1"))
        P.op(V, lambda e: e.memset(kcT[:], 0.0), (), ("kcT",))
        P.op(V, lambda e: e.memset(GV[:], 0.0), (), ("GV",))
        P.op(G, lambda e: e.memset(Vs[:], 1.0), (), tuple("Vs%d" % t_ for t_ in range(NT)))
        P.op(G, lambda e: e.memset(Vw[:], 1.0), (), tuple("Vw%d" % t_ for t_ in range(8)))
        P.op(V, lambda e: e.tensor_copy(out=posf[:], in_=posi[:]), ("posi",), ("posf",))
        for f in range(8):
            P.op(V, lambda e, f=f: e.tensor_scalar(out=ang[:, :, f], in0=posf[:], scalar1=misc[:, 8 + f:9 + f], scalar2=None, op0=ALU.mult), ("posf", "misc"), ("ang",))
        for (off_, c0_) in ((0.75, 0), (0.5, 8)):
            P.op(V, lambda e, off_=off_: e.tensor_scalar(out=ang2[:], in0=ang[:], scalar1=1.0 / (2.0 * math.pi), scalar2=off_, op0=ALU.mult, op1=ALU.add), ("ang", "cossin"), ("ang2",))
            P.op(V, lambda e: e.tensor_copy(out=angi[:], in_=ang2[:]), ("ang2",), ("angi",))
            P.op(V, lambda e: e.tensor_copy(out=ang3[:], in_=angi[:]), ("angi",), ("ang3",))
            P.op(V, lambda e: e.tensor_tensor(out=ang2[:], in0=ang2[:], in1=ang3[:], op=ALU.subtract), ("ang2", "ang3"), ("ang2",))
            P.op(V, lambda e: e.tensor_scalar(out=ang3[:], in0=ang2[:], scalar1=0.0, scalar2=None, op0=ALU.is_lt), ("ang2",), ("ang3",))
            P.op(V, lambda e: e.tensor_tensor(out=ang2[:], in0=ang2[:], in1=ang3[:], op=ALU.add), ("ang2", "ang3"), ("ang2",))
            P.op(S, lambda e, c0_=c0_: e.activation(out=cossin[:, :, c0_:c0_ + 8], in_=ang2[:], func=AF.Sin, scale=2.0 * math.pi, bias=negpi[:]), ("ang2", "negpi"), ("cossin",))
        for kv in range(2):
            for l in range(32):
                P.op(TE, lambda e, kv=kv, l=l: e.matmul(ps[4][:, 300 + kv:301 + kv], lhsT=W1b[:, kv, l, :], rhs=posT[:, kv, l:l + 1], start=(l == 0), stop=(l == 31)),
                     ("W1b", "posT"), ("ps4",))
        P.op(V, lambda e: e.tensor_copy(out=biash[:, 0:2], in_=ps[4][:, 300:302]), ("ps4",), ("biash",))

    def emit_A(i):
        p = i % 2
        t0 = i * 128
        X, HN, HT, SM = "xt%d" % p, "hn%d" % p, "hT%d" % p, "sm%d" % p
        smp = sm[:, p, :]
        P.dma(SY, lambda e, p=p, t0=t0: e.dma_start(out=xt[:, p, :], in_=x_d[t0:t0 + 128, :]), "x%d" % p, (), (X,))
        P.op(S, lambda e, p=p, smp=smp: e.activation(out=junk[:], in_=xt[:, p, :], func=AF.Square, accum_out=smp[:, 0:1]), (X,), ("junk", SM + "a"))
        P.op(S, lambda e, smp=smp: e.activation(out=smp[:, 1:2], in_=smp[:, 0:1], func=AF.Sqrt, scale=1.0 / D, bias=epsc[:]), (SM + "a", "epsc"), (SM + "b",))
        P.op(V, lambda e, smp=smp: e.reciprocal(out=smp[:, 2:3], in_=smp[:, 1:2]), (SM + "b",), (SM + "c",))
        P.op(S, lambda e, p=p, smp=smp: e.activation(out=hn[:, p, :], in_=xt[:, p, :], func=AF.Copy, scale=smp[:, 2:3]), (X, SM + "c"), (HN,))
        for k in range(8):
            P.op(TE, lambda e, p=p, k=k: e.transpose(out=ps5[:, k * 128:(k + 1) * 128], in_=hn[:, p, k * 128:(k + 1) * 128], identity=identb[:]),
                 (HN, "identb"), ("ps5",))
        P.op(V, lambda e, p=p: e.tensor_copy(out=hT[:, p, :, :], in_=ps5[:].rearrange("p (k t) -> p k t", k=8)), ("ps5",), (HT,))
        for grp in range(4):
            for k in range(8):
                P.op(TE, lambda e, p=p, k=k, grp=grp: e.matmul(ps7[:, grp * 128:(grp + 1) * 128], lhsT=Wfm[:, k, grp * 128:(grp + 1) * 128], rhs=hT[:, p, k, :], start=(k == 0), stop=(k == 7)),
                     (HT, "W"), ("ps7",))
        P.op(V, lambda e: e.tensor_copy(out=convbuf[:, :, 0:3], in_=convbuf[:, :, 128:131]), ("convbuf",), ("convbuf",))
        P.op(S, lambda e: e.activation(out=convbuf[:, :, 3:131], in_=ps7[:].rearrange("p (g t) -> p g t", g=4), func=AF.Copy), ("ps7",), ("convbuf",))
        if do_nsa:
            for c in range(2):
                for k in range(8):
                    P.op(TE, lambda e, p=p, k=k, c=c: e.matmul(ps7[0:64, c * 128:(c + 1) * 128], lhsT=Wfm[:, k, 512 + c * 64:512 + (c + 1) * 64], rhs=hT[:, p, k, :], start=(k == 0), stop=(k == 7)),
                         (HT, "W"), ("ps7",))
            P.op(V, lambda e, p=p: e.tensor_copy(out=cpre[:, p, :, 0:16], in_=cpre[:, 1 - p, :, 128:144]), ("cpre%d" % (1 - p),), ("cpre%d" % p,))
            P.op(S, lambda e, p=p: e.activation(out=cpre[:, p, :, 16:144], in_=ps7[0:64, 0:256].rearrange("p (c t) -> p c t", c=2), func=AF.Copy), ("ps7",), ("cpre%d" % p,))
        for grp in range(4):
            P.op(V, lambda e, grp=grp: e.tensor_scalar(out=cacc[:, grp, :], in0=convbuf[:, grp, 0:128], scalar1=cws[:, grp, 0:1], scalar2=cbs[:, grp:grp + 1], op0=ALU.mult, op1=ALU.add),
                 ("convbuf", "cws", "cbs"), ("cacc%d" % grp,))
            for j in range(1, 4):
                P.op(V, lambda e, grp=grp, j=j: e.scalar_tensor_tensor(out=cacc[:, grp, :], in0=convbuf[:, grp, j:j + 128], scalar=cws[:, grp, j:j + 1], in1=cacc[:, grp, :], op0=ALU.mult, op1=ALU.add),
                     ("convbuf", "cws", "cacc%d" % grp), ("cacc%d" % grp,))
        P.op(S, lambda e, p=p: e.activation(out=qkT[:, p, :, :], in_=cacc[:], func=AF.Silu), tuple("cacc%d" % g_ for g_ in range(4)), ("qkT%d" % p,))
        for c in range(2):
            P.op(TE, lambda e, p=p, c=c: e.transpose(out=ps5[:, c * 128:(c + 1) * 128], in_=qkT[:, p, 2 + c, :], identity=identb[:]), ("qkT%d" % p, "identb"), ("ps5",))
        P.op(V, lambda e, p=p: e.tensor_copy(out=ktok[:, p, :], in_=ps5[:, 0:256]), ("ps5",), ("ktok%d" % p,))
        for (pb, pk_, c0, n) in ((ps7, "ps7", 0, TMA), (ps7, "ps7", TMA, TMB), (ps7, "ps7", TMA + TMB, TMC)):
            for k in range(8):
                P.op(TE, lambda e, p=p, k=k, pb=pb, c0=c0, n=n: e.matmul(pb[:, 0:n], lhsT=hT[:, p, k, :], rhs=Wtm[:, k, c0:c0 + n], start=(k == 0), stop=(k == 7)),
                     (HT, "W"), (pk_,))
            if c0 == 0:
                P.op(V, lambda e, p=p: e.tensor_copy(out=Vaug[:, p, 0:256], in_=ps7[:, 0:256]), ("ps7",), ("Vaug%d" % p,))
                P.op(S, lambda e, p=p: e.activation(out=og[:, p, :], in_=ps7[:, 256:512], func=AF.Sigmoid), ("ps7",), ("og%d" % p,))
            elif c0 == TMA:
                P.op(S, lambda e, p=p: e.activation(out=zg[:, p, :], in_=ps7[:, 0:256], func=AF.Silu), ("ps7",), ("zg%d" % p,))
                P.op(G, lambda e, p=p: e.tensor_tensor(out=og[:, p, :], in0=og[:, p, :], in1=zg[:, p, :], op=ALU.mult), ("og%d" % p, "zg%d" % p), ("og%d" % p,))
                if do_nsa:
                    P.op(V, lambda e, p=p: e.tensor_copy(out=qtok[:, p, 2:4, :], in_=ps7[:, B_QOTH:B_QOTH + 128].rearrange("p (h d) -> p h d", h=2)), ("ps7",), ("qtok%d" % p,))
                    P.op(S, lambda e, p=p: e.activation(out=nzg[:, p, :], in_=ps7[:, B_NZ:B_NZ + 128], func=AF.Silu), ("ps7",), ("nzg%d" % p,))
            else:
                P.op(V, lambda e, p=p: e.tensor_copy(out=tmC[:, p, :], in_=ps7[:, 0:TMC]), ("ps7",), ("tmC%d" % p,))

    def emit_B(i):
        p = i % 2
        t0 = i * 128
        SM = "sm%d" % p
        smp = sm[:, p, :]
        TC = "tmC%d" % p
        P.op(V, lambda e, smp=smp, p=p: e.tensor_scalar(out=smp[:, 17:18], in0=tmC[:, p, C_F:C_F + 1], scalar1=misc[:, 0:1], scalar2=None, op0=ALU.add), (TC, "misc"), (SM + "d0",))
        P.op(S, lambda e, smp=smp: e.activation(out=smp[:, 3:4], in_=smp[:, 17:18], func=AF.Exp, scale=-1.0), (SM + "d0",), (SM + "d",))
        P.op(S, lambda e, p=p, smp=smp: e.activation(out=G2[:, p, 0:1], in_=smp[:, 3:4], func=AF.Ln, scale=1.0, bias=1.0), (SM + "d",), ("G2%d" % p,))
        P.op(V, lambda e, p=p: e.tensor_scalar(out=G2[:, p, 1:2], in0=tmC[:, p, C_I:C_I + 1], scalar1=misc[:, 1:2], scalar2=None, op0=ALU.add), (TC, "misc"), ("G2%d" % p,))
        P.op(TE, lambda e, p=p: e.matmul(ps6[:, 128:130], lhsT=Utri, rhs=G2[:, p, :], start=True, stop=True), ("G2%d" % p, "c128"), ("ps6",))
        P.op(TE, lambda e, p=p: e.matmul(ps6[:, 130:132], lhsT=ones, rhs=G2[:, p, :], start=True, stop=True), ("G2%d" % p, "c128"), ("ps6",))
        P.op(V, lambda e, smp=smp: e.tensor_copy(out=smp[:, 20:24], in_=ps6[:, 128:132]), ("ps6",), (SM + "cum",))
        P.op(V, lambda e, p=p, smp=smp: e.tensor_tensor(out=smp[:, 4:5], in0=smp[:, 20:21], in1=G2[:, p, 1:2], op=ALU.add), (SM + "cum", "G2%d" % p), (SM + "e",))
        P.op(V, lambda e, smp=smp: e.tensor_scalar(out=smp[:, 5:6], in0=smp[:, 4:5], scalar1=-LN16, scalar2=None, op0=ALU.add), (SM + "e",), (SM + "f",))
        P.op(V, lambda e, smp=smp: e.tensor_scalar(out=dbg[:, 0, :], in0=ident, scalar1=smp[:, 4:5], scalar2=None, op0=ALU.mult), (SM + "e", "c128"), ("dbg0",))
        P.op(TE, lambda e: e.matmul(ps6[:, 0:128], lhsT=ones, rhs=dbg[:, 0, :], start=True, stop=True), ("dbg0", "c128"), ("ps6",))
        P.op(V, lambda e: e.tensor_tensor(out=Bm[:], in0=ps6[:, 0:128], in1=cnegB, op=ALU.add), ("ps6", "c128"), ("Bm",))
        P.op(V, lambda e, smp=smp: e.reduce_max(out=smp[:, 6:7], in_=Bm[:], axis=AX.X), ("Bm",), (SM + "g",))
        P.op(V, lambda e, smp=smp: e.reduce_max(out=smp[:, 7:8], in_=ps6[:, 0:128], axis=AX.X), ("ps6",), (SM + "h",))
        P.op(V, lambda e, smp=smp: e.tensor_tensor(out=smp[:, 8:9], in0=smp[:, 6:7], in1=mcol[:], op=ALU.max), (SM + "g", "m"), (SM + "i",))
        P.op(V, lambda e, smp=smp: e.tensor_tensor(out=smp[:, 9:10], in0=smp[:, 7:8], in1=mcol[:], op=ALU.max), (SM + "h", "m"), (SM + "j",))
        P.op(V, lambda e, smp=smp: e.tensor_scalar(out=dbg[:, 1, :], in0=ident, scalar1=smp[:, 8:9], scalar2=None, op0=ALU.mult), (SM + "i", "c128"), ("dbg1",))
        P.op(TE, lambda e: e.matmul(ps6[:, 0:128], lhsT=ones, rhs=dbg[:, 1, :], start=True, stop=True), ("dbg1", "c128"), ("ps6",))
        P.op(S, lambda e, smp=smp: e.activation(out=DT[:], in_=ps6[:, 0:128], func=AF.Exp, scale=-1.0, bias=smp[:, 5:6]), ("ps6", SM + "f"), ("DT",))
        P.op(G, lambda e: e.tensor_tensor(out=DTm[:], in0=DT[:], in1=maskT01, op=ALU.mult), ("DT", "c128"), ("DTm",))
        P.op(S, lambda e: e.activation(out=Wm[:], in_=ps6[:, 0:128], func=AF.Exp, scale=-1.0, bias=mcol[:]), ("ps6", "m"), ("Wm",))
        for c in range(2):
            P.op(G, lambda e, p=p, c=c: e.tensor_tensor(out=qwT[:, c, :], in0=qkT[:, p, c, :], in1=Wm[:], op=ALU.mult), ("qkT%d" % p, "Wm"), ("qwT",))
        for c in range(2):
            P.op(TE, lambda e, p=p, c=c: e.matmul(ps6[:, 0:128], lhsT=qkT[:, p, 2 + c, :], rhs=qkT[:, p, c, :], start=(c == 0), stop=(c == 1)), ("qkT%d" % p,), ("ps6",))
        P.op(V, lambda e: e.tensor_tensor(out=smT[:], in0=ps6[:, 0:128], in1=DTm[:], op=ALU.mult), ("ps6", "DTm"), ("smT",))
        for c in range(2):
            P.op(TE, lambda e, c=c: e.matmul(ps6[:, 132:389], lhsT=qwT[:, c, :], rhs=Cb[:, c, :], start=(c == 0), stop=False), ("qwT", "Cb"), ("ps6",))
        P.op(TE, lambda e, p=p: e.matmul(ps6[:, 132:389], lhsT=smT[:], rhs=Vaug[:, p, :], start=False, stop=True), ("smT", "Vaug%d" % p), ("ps6",))
        P.op(V, lambda e, smp=smp: e.tensor_tensor(out=smp[:, 10:11], in0=smp[:, 20:21], in1=smp[:, 8:9], op=ALU.subtract), (SM + "cum", SM + "i"), (SM + "k",))
        P.op(S, lambda e, smp=smp: e.activation(out=smp[:, 11:12], in_=smp[:, 10:11], func=AF.Exp), (SM + "k",), (SM + "l",))
        P.op(S, lambda e, smp=smp: e.activation(out=smp[:, 19:20], in_=ps6[:, 388:389], func=AF.Abs), ("ps6",), (SM + "m0",))
        P.op(V, lambda e, smp=smp: e.tensor_tensor(out=smp[:, 12:13], in0=smp[:, 19:20], in1=smp[:, 11:12], op=ALU.max), (SM + "m0", SM + "l"), (SM + "m",))
        P.op(V, lambda e, smp=smp: e.reciprocal(out=smp[:, 13:14], in_=smp[:, 12:13]), (SM + "m",), (SM + "n",))
        P.op(V, lambda e, smp=smp: e.tensor_scalar(out=hbuf[:], in0=ps6[:, 132:388], scalar1=smp[:, 13:14], scalar2=None, op0=ALU.mult), ("ps6", SM + "n"), ("hbuf",))
        P.op(S, lambda e, smp=smp: e.activation(out=smp[:, 15:16], in_=smp[:, 9:10], func=AF.Exp, scale=-1.0, bias=smp[:, 5:6]), (SM + "j", SM + "f"), (SM + "p",))
        P.op(S, lambda e, smp=smp: e.activation(out=smp[:, 16:17], in_=smp[:, 9:10], func=AF.Exp, scale=-1.0, bias=mcol[:]), (SM + "j", "m"), (SM + "q",))
        P.op(V, lambda e, smp=smp, p=p: e.tensor_scalar(out=kwt[:], in0=ktok[:, p, :], scalar1=smp[:, 15:16], scalar2=None, op0=ALU.mult), ("ktok%d" % p, SM + "p"), ("kwt",))
        for c in range(2):
            P.op(TE, lambda e, p=p, c=c: e.matmul(ps6[:, 132:389], lhsT=kwt[:, c * 128:(c + 1) * 128], rhs=Vaug[:, p, :], start=True, stop=True), ("kwt", "Vaug%d" % p), ("ps6",))
            P.op(V, lambda e, c=c, smp=smp: e.scalar_tensor_tensor(out=Cst[:, c, :], in0=Cst[:, c, :], scalar=smp[:, 16:17], in1=ps6[:, 132:389], op0=ALU.mult, op1=ALU.add),
                 ("Cst", SM + "q", "ps6"), ("Cst",))
            P.op(S, lambda e, c=c: e.activation(out=Cb[:, c, :], in_=Cst[:, c, :], func=AF.Copy), ("Cst",), ("Cb",))
        P.op(V, lambda e, smp=smp: e.tensor_tensor(out=mcol[:], in0=smp[:, 9:10], in1=smp[:, 22:23], op=ALU.subtract), (SM + "j", SM + "cum"), ("m",))
        P.op(V, lambda e: e.bn_stats(out=bnst[:, 0:6], in_=hbuf[:]), ("hbuf",), ("bnst",))
        P.op(V, lambda e: e.bn_aggr(out=bnst[:, 6:8], in_=bnst[:, 0:6]), ("bnst",), ("bnst",))
        P.op(S, lambda e, smp=smp: e.activation(out=smp[:, 18:19], in_=bnst[:, 7:8], func=AF.Sqrt, scale=1.0, bias=epsc[:]), ("bnst", "epsc"), (SM + "o0",))
        P.op(V, lambda e, smp=smp: e.reciprocal(out=smp[:, 14:15], in_=smp[:, 18:19]), (SM + "o0",), (SM + "o",))
        P.op(V, lambda e, smp=smp: e.tensor_scalar(out=hbuf2[:], in0=hbuf[:], scalar1=bnst[:, 6:7], scalar2=smp[:, 14:15], op0=ALU.subtract, op1=ALU.mult), ("hbuf", "bnst", SM + "o"), ("hbuf2",))
        P.op(G, lambda e: e.tensor_tensor(out=hbuf2[:], in0=hbuf2[:], in1=mhg[:], op=ALU.mult), ("hbuf2", "mhg"), ("hbuf2",))
        P.op(G, lambda e, p=p: e.tensor_tensor(out=yat[:, p, :], in0=hbuf2[:], in1=og[:, p, :], op=ALU.mult), ("hbuf2", "og%d" % p), ("yat%d" % p,))
        P.dma(SY, lambda e, p=p, t0=t0: e.dma_start(out=ya_d[t0:t0 + 128, :], in_=yat[:, p, :]), "ya%d" % p, ("yat%d" % p,), ("ya_out",))

    def emit_C(i):
        p = i % 2
        t0 = i * 128
        TC = "tmC%d" % p
        if do_nsa:
            psS = (ps[0], ps[1])
            psO = (ps[2], ps[3])
            QT = "qtok%d" % p
            P.op(S, lambda e, p=p: e.activation(out=qtok[:, p, 0:2, :], in_=tmC[:, p, 0:128].rearrange("p (h d) -> p h d", h=2), func=AF.Copy), (TC,), (QT,))
            P.op(S, lambda e, p=p: e.activation(out=rp[:], in_=tmC[:, p, 0:256].rearrange("p (h d) -> p h d", h=4), func=AF.Copy), (TC,), ("rp",))
            src = tmC[:, p, 0:256].rearrange("p (h d) -> p h d", h=4)
            cosb = cossin[:, i, 0:8].unsqueeze(1).to_broadcast([128, 4, 8])
            sinb = cossin[:, i, 8:16].unsqueeze(1).to_broadcast([128, 4, 8])
            P.op(V, lambda e, src=src, cosb=cosb: e.tensor_tensor(out=rtmp[:, 0], in0=src[:, :, 0:8], in1=cosb, op=ALU.mult), (TC, "cossin"), ("rtmp0",))
            P.op(V, lambda e, src=src, sinb=sinb: e.tensor_tensor(out=rtmp[:, 1], in0=src[:, :, 8:16], in1=sinb, op=ALU.mult), (TC, "cossin"), ("rtmp1",))
            P.op(V, lambda e, src=src, cosb=cosb: e.tensor_tensor(out=rtmp[:, 2], in0=src[:, :, 8:16], in1=cosb, op=ALU.mult), (TC, "cossin"), ("rtmp2",))
            P.op(V, lambda e, src=src, sinb=sinb: e.tensor_tensor(out=rtmp[:, 3], in0=src[:, :, 0:8], in1=sinb, op=ALU.mult), (TC, "cossin"), ("rtmp3",))
            P.op(V, lambda e: e.tensor_tensor(out=rp[:, :, 0:8], in0=rtmp[:, 0], in1=rtmp[:, 1], op=ALU.subtract), ("rtmp0", "rtmp1", "rp"), ("rp",))
            P.op(V, lambda e: e.tensor_tensor(out=rp[:, :, 8:16], in0=rtmp[:, 2], in1=rtmp[:, 3], op=ALU.add), ("rtmp2", "rtmp3", "rp"), ("rp",))
            P.op(V, lambda e, i=i, p=p: e.tensor_copy(out=Vs[:, i, 0:64], in_=tmC[:, p, C_VS:C_VS + 64]), (TC,), ("Vs%d" % i,))
            P.op(V, lambda e, i=i, p=p: e.tensor_copy(out=Vw[:, i % 8, 0:64], in_=tmC[:, p, C_VW:C_VW + 64]), (TC,), ("Vw%d" % (i % 8),))
            P.op(V, lambda e, p=p: e.tensor_tensor(out=gsig[:, 0:6], in0=tmC[:, p, C_G:C_G + 6], in1=misc[:, 2:8], op=ALU.add), (TC, "misc"), ("gsig",))
            P.op(S, lambda e: e.activation(out=gsig[:, 0:6], in_=gsig[:, 0:6], func=AF.Sigmoid), ("gsig",), ("gsig",))
            for h in range(4):
                P.op(TE, lambda e, h=h, p=p: e.transpose(out=ps[0][0:64, h * 128:(h + 1) * 128], in_=qtok[:, p, h, :], identity=ident), (QT, "c128"), ("ps0",))
            for h in range(4):
                P.op(TE, lambda e, h=h: e.transpose(out=ps[1][0:64, h * 128:(h + 1) * 128], in_=rp[:, h, :], identity=ident), ("rp", "c128"), ("ps1",))
            P.op(V, lambda e: e.tensor_copy(out=qrawT[:], in_=ps[0][0:64, 0:512]), ("ps0",), ("qrawT",))
            P.op(S, lambda e: e.activation(out=qropT[:], in_=ps[1][0:64, 0:256], func=AF.Copy), ("ps1",), ("qropT",))
            P.op(V, lambda e, t0=t0: e.tensor_copy(out=ksT[:, t0:t0 + 128], in_=ps[1][0:64, 256:384]), ("ps1",), ("ksT%d" % i,))
            P.op(S, lambda e, i=i: e.activation(out=kwT[:, (i % 8) * 128:(i % 8 + 1) * 128], in_=ps[1][0:64, 384:512], func=AF.Copy), ("ps1",), ("kwT%d" % (i % 8),))
            for kv in range(2):
                for l in range(32):
                    P.op(TE, lambda e, kv=kv, l=l, p=p: e.matmul(ps[4][:, 256 + kv * 8:264 + kv * 8], lhsT=W1b[:, kv, l, :], rhs=cpre[:, p, kv, l:l + 113:16], start=(l == 0), stop=(l == 31)),
                         ("W1b", "cpre%d" % p), ("ps4",))
            hid = ps[4][:, 256:272].rearrange("p (k n) -> p k n", k=2)
            P.op(V, lambda e, hid=hid: e.tensor_tensor(out=hx[:, 0, :].rearrange("p (k n) -> p k n", k=2), in0=hid, in1=biash[:].unsqueeze(2).to_broadcast([128, 2, 8]), op=ALU.add), ("ps4", "biash"), ("hx0",))
            P.op(V, lambda e: e.tensor_tensor(out=hx[:, 1, :], in0=hx[:, 0, :], in1=hx[:, 0, :], op=ALU.mult), ("hx0",), ("hx1",))
            P.op(V, lambda e: e.tensor_scalar(out=hx[:, 1, :], in0=hx[:, 1, :], scalar1=0.044715, scalar2=1.0, op0=ALU.mult, op1=ALU.add), ("hx1",), ("hx1",))
            P.op(V, lambda e: e.tensor_tensor(out=hx[:, 2, :], in0=hx[:, 1, :], in1=hx[:, 0, :], op=ALU.mult), ("hx1", "hx0"), ("hx2",))
            P.op(S, lambda e: e.activation(out=hx[:, 3, :], in_=hx[:, 2, :], func=AF.Sigmoid, scale=1.5957691216057308), ("hx2",), ("hx3",))
            P.op(V, lambda e: e.tensor_tensor(out=gl[:], in0=hx[:, 3, :], in1=hx[:, 0, :], op=ALU.mult), ("hx3", "hx0"), ("gl",))
            P.op(TE, lambda e: e.matmul(ps[4][0:64, 320:328], lhsT=W2b[:, 0, :], rhs=gl[:, 0:8], start=True, stop=True), ("W2b", "gl"), ("ps4",))
            P.op(V, lambda e, i=i: e.tensor_copy(out=kcT[:, 8 * i:8 * i + 8], in_=ps[4][0:64, 320:328]), ("ps4",), ("kcT",))
            P.op(V, lambda e, i=i: e.tensor_copy(out=GV[:, 8 * i:8 * i + 8], in_=gl[:, 8:16]), ("gl",), ("GV",))
            if i == 0:
                P.op(V, lambda e: e.memset(GV[:, 0:1], 0.0), (), ("GV",))
            cc_cur = i // 16
            P.op(TE, lambda e, cc_cur=cc_cur: e.matmul(ps[4][:, 336:400], lhsT=GV[:, cc_cur * 128:(cc_cur + 1) * 128], rhs=W2b[:, 1, :], start=True, stop=True), ("GV", "W2b"), ("ps4",))
            P.op(V, lambda e, cc_cur=cc_cur: e.tensor_copy(out=Raug[:, cc_cur, 128:192], in_=ps[4][:, 336:400]), ("ps4",), ("Raug",))
            nch = cc_cur + 1
            for cc in range(nch):
                P.op(TE, lambda e, cc=cc: e.matmul(psS[cc % 2][:, 0:512], lhsT=kcT[:, cc * 128:(cc + 1) * 128], rhs=qrawT[:], start=True, stop=True), ("kcT", "qrawT"), ("ps%d" % (cc % 2),))
                P.op(S, lambda e, cc=cc: e.activation(out=PcT[:, cc, :], in_=psS[cc % 2][:, 0:512], func=AF.Exp, scale=0.125), ("ps%d" % (cc % 2),), ("PcT%d" % cc,))
            delta = 128.0 * (i % 16) - 15.0
            P.op(V, lambda e, delta=delta: e.tensor_scalar(out=cmask[:], in0=c128[:, 5, :], scalar1=delta, scalar2=None, op0=ALU.is_le), ("c128",), ("cmask",))
            P.op(G, lambda e, cc_cur=cc_cur: e.tensor_tensor(out=PcT[:, cc_cur, :].rearrange("p (h t) -> p h t", h=4), in0=PcT[:, cc_cur, :].rearrange("p (h t) -> p h t", h=4),
                                                              in1=cmask[:].unsqueeze(1).to_broadcast([128, 4, 128]), op=ALU.mult), ("PcT%d" % cc_cur, "cmask"), ("PcT%d" % cc_cur,))
            for h in range(4):
                po = psO[h % 2]
                pk = "ps%d" % (2 + h % 2)
                s0 = (h // 2) * 256
                for cc in range(nch):
                    P.op(TE, lambda e, h=h, cc=cc, po=po, s0=s0: e.matmul(po[:, s0:s0 + 193], lhsT=PcT[:, cc, h * 128:(h + 1) * 128], rhs=Raug[:, cc, :], start=(cc == 0), stop=(cc == nch - 1)),
                         ("PcT%d" % cc, "Raug"), (pk,))
            for b_ in range(2):
                P.op(V, lambda e, b_=b_: e.tensor_scalar(out=rr[:, b_:4:2], in0=psO[b_][:, 192:449:256], scalar1=1e-30, scalar2=None, op0=ALU.max), ("ps%d" % (2 + b_),), ("rrm%d" % b_,))
            P.op(V, lambda e: e.reciprocal(out=rr[:, 0:4], in_=rr[:, 0:4]), ("rrm0", "rrm1"), ("rr",))
            for h in range(4):
                po = psO[h % 2]
                pk = "ps%d" % (2 + h % 2)
                s0 = (h // 2) * 256
                if h == 0:
                    P.op(V, lambda e, po=po: e.tensor_scalar(out=imp[:], in0=po[:, 0:128], scalar1=rr[:, 0:1], scalar2=None, op0=ALU.mult), (pk, "rr"), ("imp",))
                else:
                    P.op(V, lambda e, h=h, po=po, s0=s0: e.scalar_tensor_tensor(out=imp[:], in0=po[:, s0:s0 + 128], scalar=rr[:, h:h + 1], in1=imp[:], op0=ALU.mult, op1=ALU.add), (pk, "rr", "imp"), ("imp",))
            P.op(V, lambda e: e.tensor_tensor(out=rr[:, 4:6], in0=rr[:, 0:2], in1=gsig[:, 0:6:3], op=ALU.mult), ("rr", "gsig"), ("rg",))
            for h in range(2):
                P.op(S, lambda e, h=h: e.activation(out=ynacc[:, h, :], in_=psO[h][:, 128:192], func=AF.Copy, scale=rr[:, 4 + h:5 + h]), ("ps%d" % (2 + h), "rg"), ("ynacc%d" % h,))
            w0 = 128 - 2 * i
            P.op(V, lambda e, w0=w0: e.scalar_tensor_tensor(out=sc[:, 0, :], in0=imp[:], scalar=1.0, in1=selc[:, 0, w0:w0 + 128], op0=ALU.add, op1=ALU.mult), ("imp", "selc"), ("sc0",))
            P.op(V, lambda e, w0=w0: e.tensor_tensor(out=sc[:, 0, :], in0=sc[:, 0, :], in1=selc[:, 2, w0:w0 + 128], op=ALU.max), ("sc0", "selc"), ("sc0",))
            P.op(V, lambda e: e.tensor_tensor(out=sc[:, 0, :], in0=sc[:, 0, :], in1=selc[:, 3, 0:128], op=ALU.max), ("sc0", "selc"), ("sc0",))
            P.op(V, lambda e: e.max(out=m8[:, 0:8], in_=sc[:, 0, :]), ("sc0",), ("m8a",))
            P.op(V, lambda e: e.match_replace(out=sc[:, 1, :], in_to_replace=m8[:, 0:8], in_values=sc[:, 0, :], imm_value=-1e30), ("sc0", "m8a"), ("sc1",))
            P.op(V, lambda e: e.max(out=m8[:, 8:16], in_=sc[:, 1, :]), ("sc1",), ("m8b",))
            P.op(V, lambda e: e.tensor_scalar(out=NegSel[:], in0=sc[:, 0, :], scalar1=m8[:, 15:16], scalar2=MASKNEG, op0=ALU.is_lt, op1=ALU.mult), ("sc0", "m8b"), ("NegSel",))
            P.op(TE, lambda e: e.transpose(out=ps[0][:, 0:128], in_=NegSel[:], identity=ident), ("NegSel", "c128"), ("ps0",))
            P.op(V, lambda e: e.tensor_copy(out=NegSelT[:, 0:128], in_=ps[0][:, 0:128]), ("ps0",), ("NegSelT",))
            P.op(S, lambda e: e.activation(out=NegSelT[:, 128:256], in_=ps[0][:, 0:128], func=AF.Copy), ("ps0",), ("NegSelT",))
            def attn_branch(kbs, kT_, kkey, Vt, vkey, gate_col, use_sel, ring=None):
                sl = (lambda kb_: kb_ % ring) if ring else (lambda kb_: kb_)
                nk = len(kbs)
                chunks = [kbs[c_:c_ + 2] for c_ in range(0, nk, 2)]
                done = 0
                for n_, ch in enumerate(chunks):
                    pS = psS[n_ % 2]
                    sk_ = "ps%d" % (n_ % 2)
                    for u, kb in enumerate(ch):
                        extra = []
                        if use_sel:
                            extra.append((Eb[:, kb * 128:(kb + 1) * 128], NegSelT[:], ("Eb", "NegSelT")))
                        if kb == i:
                            extra.append((identb[:], negm[:, 0, :], ("identb", "negm")))
                        if (not use_sel) and kb == i - 4:
                            extra.append((identb[:], negm[:, 1, :], ("identb", "negm")))
                        P.op(TE, lambda e, kb=kb, pS=pS, u=u, ne=len(extra): e.matmul(pS[:, u * 256:(u + 1) * 256], lhsT=kT_[:, sl(kb) * 128:(sl(kb) + 1) * 128], rhs=qropT[:], start=True, stop=(ne == 0)),
                             ("%s%d" % (kkey, sl(kb)), "qropT"), (sk_,))
                        for xi, (l_, r_, keys) in enumerate(extra):
                            P.op(TE, lambda e, l_=l_, r_=r_, pS=pS, u=u, last=(xi == len(extra) - 1): e.matmul(pS[:, u * 256:(u + 1) * 256], lhsT=l_, rhs=r_, start=False, stop=last), keys, (sk_,))
                    w_ = 256 * len(ch)
                    P.op(S, lambda e, pS=pS, n_=n_, w_=w_: e.activation(out=PT[:, n_ % 2, 0:w_], in_=pS[:, 0:w_], func=AF.Exp, scale=0.125), (sk_,), ("PT%d" % (n_ % 2),))
                    for u, kb in enumerate(ch):
                        for h in range(2):
                            P.op(TE, lambda e, h=h, kb=kb, n_=n_, u=u, first=(done == 0), last=(done == nk - 1): e.matmul(psO[h][:, 0:65], lhsT=PT[:, n_ % 2, u * 256 + h * 128:u * 256 + (h + 1) * 128], rhs=Vt[:, sl(kb), :], start=first, stop=last),
                                 ("PT%d" % (n_ % 2), "%s%d" % (vkey, sl(kb))), ("ps%d" % (2 + h),))
                        done += 1
                    if H3_TEST and use_sel and n_ == 0 and nk > 2:
                        P.op(TE, lambda e: e.transpose(out=ps5[:, 0:128], in_=identb[:], identity=identb[:]), ("identb",), ("ps5",))
                for h in range(2):
                    pk = "ps%d" % (2 + h)
                    P.op(V, lambda e, h=h: e.reciprocal(out=rr[:, 8 + h:9 + h], in_=psO[h][:, 64:65]), (pk,), ("rs%d" % h,))
                    P.op(V, lambda e, h=h: e.tensor_tensor(out=rr[:, 8 + h:9 + h], in0=rr[:, 8 + h:9 + h], in1=gsig[:, 3 * h + gate_col:3 * h + gate_col + 1], op=ALU.mult), ("rs%d" % h, "gsig"), ("rs%d" % h,))
                    P.op(V, lambda e, h=h: e.scalar_tensor_tensor(out=ynacc[:, h, :], in0=psO[h][:, 0:64], scalar=rr[:, 8 + h:9 + h], in1=ynacc[:, h, :], op0=ALU.mult, op1=ALU.add),
                         (pk, "rs%d" % h, "ynacc%d" % h), ("ynacc%d" % h,))

            marks.append(len(P.cap) if P.cap is not None else 0)
            attn_branch(list(range(i + 1)), ksT, "ksT", Vs, "Vs", 1, True)
            attn_branch(list(range(max(0, i - 4), i + 1)), kwT, "kwT", Vw, "Vw", 2, False, ring=8)
            P.op(G, lambda e, p=p: e.tensor_tensor(out=ybt[:, p, :], in0=ynacc[:].rearrange("p h d -> p (h d)"), in1=nzg[:, p, :], op=ALU.mult), ("ynacc0", "ynacc1", "nzg%d" % p), ("ybt%d" % p,))
            P.dma(SY, lambda e, p=p, t0=t0: e.dma_start(out=yb_d[t0:t0 + 128, :], in_=ybt[:, p, :]), "yb%d" % p, ("ybt%d" % p,), ("yb_out",))

    marks = []
    emit_A(0)
    for i in range(NT):
        if MERGE_MODE == 0:
            emit_B(i)
            if do_nsa:
                emit_C(i)
            if i + 1 < NT:
                emit_A(i + 1)
            continue
        P.cap = []
        emit_B(i)
        LB = P.cap
        P.cap = []
        if i + 1 < NT:
            emit_A(i + 1)
        LA = P.cap
        P.cap = []
        del marks[:]
        emit_C(i)
        LY = P.cap
        P.cap = None
        mk = marks[0]
        for kind, args in LY[:mk]:
            (P.op if kind == "op" else P.dma)(*args)
        P.replay_merged(LB, LA, LY[mk:])
    P.final_wait(SY, ("ya_out",) + (("yb_out",) if do_nsa else ()))
    P.emit(es)
    es.close()
    return nc


def _c128():
    c = np.zeros((128, 6, 128), np.float32)
    r = np.arange(128)
    c[:, 0, :] = np.eye(128)
    c[:, 1, :] = (r[:, None] <= r[None, :])
    c[:, 2, :] = 1.0
    c[:, 3, :] = np.where(r[None, :] <= r[:, None], 0.0, NEGB)
    c[:, 4, :] = (r[:, None] <= r[None, :])
    c[:, 5, :] = 16.0 * r[:, None] - r[None, :]
    return c


def mixer_inputs(inp, b, j, T):
    NT = T // 128
    g, pr = j // 2, j % 2
    w = inp["w_in"][0]
    off = {}
    o = 0
    for n_, wd in (("m_q", 1024), ("m_k", 1024), ("m_v", 1024), ("m_o", 1024), ("m_z", 1024), ("m_i", 4), ("m_f", 4),
                   ("n_q", 512), ("n_kc", 128), ("n_vc", 128), ("n_ks", 128), ("n_vs", 128), ("n_kw", 128), ("n_vw", 128),
                   ("n_g", 24), ("n_z", 512), ("g_a", 1024), ("g_b", 1024)):
        off[n_] = o
        o += wd
    hs = slice(256 * j, 256 * (j + 1))

    def cols(name, lo, hi):
        return w[:, off[name] + lo: off[name] + hi]
    gq = 256 * g
    own = [2 * pr, 2 * pr + 1]
    oth = [2 * (1 - pr), 2 * (1 - pr) + 1]
    wfm = np.concatenate([cols("m_q", 256 * j, 256 * j + 256), cols("m_k", 256 * j, 256 * j + 256),
                          cols("n_kc", 64 * g, 64 * g + 64), cols("n_vc", 64 * g, 64 * g + 64)], axis=1)
    hq = lambda h: cols("n_q", gq + 64 * h, gq + 64 * h + 64)
    hg = 4 * g
    wtm = np.concatenate([
        cols("m_v", 256 * j, 256 * j + 256), cols("m_o", 256 * j, 256 * j + 256),
        cols("m_z", 256 * j, 256 * j + 256), hq(oth[0]), hq(oth[1]),
        cols("n_z", 64 * (hg + own[0]), 64 * (hg + own[0]) + 128),
        hq(own[0]), hq(own[1]), cols("n_ks", 64 * g, 64 * g + 64), cols("n_kw", 64 * g, 64 * g + 64),
        cols("n_vs", 64 * g, 64 * g + 64), cols("n_vw", 64 * g, 64 * g + 64),
        cols("m_i", j, j + 1), cols("m_f", j, j + 1), cols("n_g", 3 * (hg + own[0]), 3 * (hg + own[0]) + 6),
    ], axis=1)
    assert wfm.shape[1] == NFM and wtm.shape[1] == NTM, (wfm.shape, wtm.shape)
    convw = np.zeros((128, 4, 4), np.float32)
    convb = np.zeros((128, 4), np.float32)
    for gi, (cw, cb) in enumerate(((inp["conv_q_w"][0], inp["conv_q_b"][0]), (inp["conv_k_w"][0], inp["conv_k_b"][0]))):
        for c in range(2):
            convw[:, 2 * gi + c, :] = cw[:, 256 * j + 128 * c: 256 * j + 128 * (c + 1)].T
            convb[:, 2 * gi + c] = cb[256 * j + 128 * c: 256 * j + 128 * (c + 1)]
    misc = np.zeros((128, 16), np.float32)
    misc[:, 0] = inp["b_fgate"][0, j]
    misc[:, 1] = inp["b_igate"][0, j]
    misc[:, 2:8] = inp["b_nsa_gate"][0, 3 * (hg + own[0]): 3 * (hg + own[0]) + 6][None, :]
    misc[:, 8:16] = (np.float32(500000.0) ** (-np.arange(8, dtype=np.float32) * np.float32(2.0 / 16)))[None, :]
    m = {
        "x": np.ascontiguousarray(inp["x"][b, :T]),
        "pos": np.ascontiguousarray(inp["positions"][b, :T].reshape(NT, 128).T.astype(np.int32)),
        "wfm": np.ascontiguousarray(wfm), "wtm": np.ascontiguousarray(wtm),
        "ng": np.ascontiguousarray(inp["norm_g"][0].reshape(8, 128).T),
        "convw": convw, "convb": convb, "misc": misc,
        "mhg": np.ascontiguousarray(np.broadcast_to(inp["mh_norm_g"][0, hs][None, :], (128, 256))),
        "c128": _c128(),
    }
    r = np.arange(128)
    w1 = np.stack([inp["cmp_w1_k"][0].reshape(32, 64, 128).transpose(1, 0, 2), inp["cmp_w1_v"][0].reshape(32, 64, 128).transpose(1, 0, 2)], axis=1)
    m["cw1"] = np.ascontiguousarray(w1)
    m["cw2"] = np.ascontiguousarray(np.stack([inp["cmp_w2_k"][0], inp["cmp_w2_v"][0]], axis=1))
    m["cpos"] = np.ascontiguousarray(np.stack([inp["cmp_pos_k"][0].T, inp["cmp_pos_v"][0].T], axis=1))
    m.update(_nsa_consts(T))
    return m


_NSA_CONSTS = {}


def _nsa_consts(T):
    if T in _NSA_CONSTS:
        return _NSA_CONSTS[T]
    r = np.arange(128)
    raug = np.zeros((128, 4, 193), np.float32)
    cidx = (np.arange(4)[None, :] * 128 + r[:, None])
    n = cidx - 1
    sidx = np.arange(128)
    ov = ((16 * n[:, :, None] < 64 * sidx + 64) & (16 * n[:, :, None] + 32 > 64 * sidx)).astype(np.float32)
    ov[cidx == 0] = 0.0
    raug[:, :, 0:128] = ov
    raug[:, :, 192] = (cidx > 0)
    E = (np.arange(T)[None, :] // 64 == r[:, None]).astype(np.float32)
    selc = np.zeros((128, 4, 256), np.float32)
    rel = np.arange(256)[None, :] - 128
    hi = (r[:, None] >= 64).astype(np.int64)
    valid = (rel <= hi)
    selc[:, 0, :] = valid
    selc[:, 1, :] = valid.astype(np.float32) - 1.0
    selc[:, 2, :] = np.where((rel == hi) | (rel == hi - 1), 1e9, 0.0)
    selc[:, 3, 0] = 1e9
    negm = np.zeros((128, 2, 256), np.float32)
    for h in range(2):
        negm[:, 0, h * 128:(h + 1) * 128] = np.where(r[:, None] > r[None, :], MASKNEG, 0.0)
        negm[:, 1, h * 128:(h + 1) * 128] = np.where(r[:, None] <= r[None, :], MASKNEG, 0.0)
    out = {"raug": raug, "E": E, "selc": selc, "negm": negm}
    _NSA_CONSTS[T] = out
    return out


def emit_tail(nc, P, es, NTT, x_d, ya_tile, yb_tile, wg_d, wa_d, wb_d, wo_d, ng_d, fg_d, c128_d, out_d, pfx="t"):
    def sb(name, shape, dt=F32):
        return es.enter_context(nc.sbuf_tensor(pfx + name, list(shape), dt))

    def pst(name, shape, dt=F32):
        return es.enter_context(nc.psum_tensor(pfx + name, list(shape), dt))

    K = lambda s: pfx + s
    V, S, G, TE, SY = "vector", "scalar", "gpsimd", "tensor", "sync"
    Wg = sb("Wg", [128, 8, 2048], BF16)
    Wa = sb("Wa", [128, 8, 1024], BF16)
    Wb = sb("Wb", [128, 4, 1024], BF16)
    Wo = sb("Wo", [128, 8, 1024], BF16)
    stg = sb("stg", [128, 2, 2048])
    ngs = sb("ngs", [128, 8])
    fgs = sb("fgs", [128, 1024])
    identb = sb("identb", [128, 128], BF16)
    epsc = sb("epsc", [128, 1])
    xt = sb("xt", [128, 2, D])
    junk = sb("junk", [128, D], BF16)
    hn = sb("hn", [128, D], BF16)
    hT = sb("hT", [128, 8, 128], BF16)
    yab = sb("yab", [128, 2, 1024], BF16)
    ybb = sb("ybb", [128, 2, 512], BF16)
    yT = sb("yT", [128, 2, 12, 128], BF16)
    sg = sb("sg", [128, 2, 2048], BF16)
    junk2 = sb("junk2", [128, D], BF16)
    t1 = sb("t1", [128, 1024])
    t2 = sb("t2", [128, 1024])
    mg = sb("mg", [128, 1024], BF16)
    mT = sb("mT", [128, 8, 128], BF16)
    xo = sb("xo", [128, 1024])
    ot = sb("ot", [128, 2, 1024])
    sm = sb("sm", [128, 2, 8])
    ps = [pst("ps%d" % i, [128, 512]) for i in range(6)]
    psT = pst("psT", [128, 1024], BF16)
    psT2 = pst("psT2", [128, 1024], BF16)

    P.dma(SY, lambda e: e.dma_start(out=ngs[:], in_=ng_d), K("c0"), (), (K("ngs"),))
    P.dma(SY, lambda e: e.dma_start(out=fgs[:], in_=fg_d), K("c1"), (), (K("fgs"),))
    P.dma(G, lambda e: e.dma_start(out=identb[:], in_=c128_d[:, 0, :]), K("c2"), (), (K("identb"),))
    P.op(V, lambda e: e.memset(epsc[:], EPS), (), (K("epsc"),))
    for k in range(8):
        sp_ = k % 2
        P.dma(SY, lambda e, k=k, sp_=sp_: e.dma_start(out=stg[:, sp_, :], in_=wg_d[k * 128:(k + 1) * 128, :]), K("w%d" % sp_), (), (K("stg%d" % sp_),))
        P.op(V, lambda e, k=k, sp_=sp_: e.tensor_scalar(out=Wg[:, k, :], in0=stg[:, sp_, :], scalar1=ngs[:, k:k + 1], scalar2=None, op0=ALU.mult), (K("stg%d" % sp_), K("ngs")), (K("W"),))
    for k in range(8):
        P.dma(G, lambda e, k=k: e.dma_start(out=Wa[:, k, :], in_=wa_d[k * 128:(k + 1) * 128, :]), K("wa"), (), (K("W"),))
        P.dma(G, lambda e, k=k: e.dma_start(out=Wo[:, k, :], in_=wo_d[k * 128:(k + 1) * 128, :]), K("wo"), (), (K("W"),))
    for k in range(4):
        P.dma(G, lambda e, k=k: e.dma_start(out=Wb[:, k, :], in_=wb_d[k * 128:(k + 1) * 128, :]), K("wb"), (), (K("W"),))

    def T1(i):
        p = i % 2
        t0 = i * 128
        X, SM = K("xt%d" % p), K("sm%d" % p)
        smp = sm[:, p, :]
        P.dma(SY, lambda e, p=p, t0=t0: e.dma_start(out=xt[:, p, :], in_=x_d[t0:t0 + 128, :]), K("x%d" % p), (), (X,))
        P.dma(SY, lambda e, p=p, i=i: e.dma_start(out=yab[:, p, :], in_=ya_tile(i)), K("ya%d" % p), (), (K("yab%d" % p),))
        P.dma(SY, lambda e, p=p, i=i: e.dma_start(out=ybb[:, p, :], in_=yb_tile(i)), K("yb%d" % p), (), (K("ybb%d" % p),))
        P.op(S, lambda e, p=p, smp=smp: e.activation(out=junk[:], in_=xt[:, p, :], func=AF.Square, accum_out=smp[:, 0:1]), (X,), (K("junk"), SM + "a"))
        P.op(S, lambda e, smp=smp: e.activation(out=smp[:, 1:2], in_=smp[:, 0:1], func=AF.Sqrt, scale=1.0 / D, bias=epsc[:]), (SM + "a", K("epsc")), (SM + "b",))
        P.op(V, lambda e, smp=smp: e.reciprocal(out=smp[:, 2:3], in_=smp[:, 1:2]), (SM + "b",), (SM + "c",))
        P.op(S, lambda e, p=p, smp=smp: e.activation(out=hn[:], in_=xt[:, p, :], func=AF.Copy, scale=smp[:, 2:3]), (X, SM + "c"), (K("hn"),))
        for k in range(8):
            P.op(TE, lambda e, k=k: e.transpose(out=psT[:, k * 128:(k + 1) * 128], in_=hn[:, k * 128:(k + 1) * 128], identity=identb[:]), (K("hn"), K("identb")), (K("psT"),))
        P.op(V, lambda e: e.tensor_copy(out=hT[:], in_=psT[:].rearrange("p (k t) -> p k t", k=8)), (K("psT"),), (K("hT"),))
        for bk in range(4):
            pb, pk = ps[bk % 2], K("ps%d" % (bk % 2))
            for k in range(8):
                P.op(TE, lambda e, k=k, bk=bk, pb=pb: e.matmul(pb[:, :], lhsT=hT[:, k, :], rhs=Wg[:, k, bk * 512:(bk + 1) * 512], start=(k == 0), stop=(k == 7)), (K("hT"), K("W")), (pk,))
            P.op(S, lambda e, bk=bk, pb=pb, p=p: e.activation(out=sg[:, p, bk * 512:(bk + 1) * 512], in_=pb[:, :], func=AF.Sigmoid), (pk,), (K("sg%d_%d" % (p, bk)),))
        for k in range(8):
            P.op(TE, lambda e, k=k, p=p: e.transpose(out=psT[:, k * 128:(k + 1) * 128], in_=yab[:, p, k * 128:(k + 1) * 128], identity=identb[:]), (K("yab%d" % p), K("identb")), (K("psT"),))
        P.op(V, lambda e, p=p: e.tensor_copy(out=yT[:, p, 0:8, :], in_=psT[:].rearrange("p (k t) -> p k t", k=8)), (K("psT"),), (K("yTa%d" % p),))
        for k in range(4):
            P.op(TE, lambda e, k=k, p=p: e.transpose(out=psT[:, k * 128:(k + 1) * 128], in_=ybb[:, p, k * 128:(k + 1) * 128], identity=identb[:]), (K("ybb%d" % p), K("identb")), (K("psT"),))
        P.op(V, lambda e, p=p: e.tensor_copy(out=yT[:, p, 8:12, :], in_=psT[:, 0:512].rearrange("p (k t) -> p k t", k=4)), (K("psT"),), (K("yTb%d" % p),))

    def T2(i):
        p = i % 2
        t0 = i * 128
        X, SM = K("xt%d" % p), K("sm%d" % p)
        smp = sm[:, p, :]
        for half in range(2):
            for k in range(8):
                P.op(TE, lambda e, k=k, half=half, p=p: e.matmul(ps[2 + half][:, :], lhsT=yT[:, p, k, :], rhs=Wa[:, k, half * 512:(half + 1) * 512], start=(k == 0), stop=(k == 7)), (K("yTa%d" % p), K("W")), (K("ps%d" % (2 + half)),))
            P.op(V, lambda e, half=half, p=p: e.tensor_tensor(out=t1[:, half * 512:(half + 1) * 512], in0=ps[2 + half][:, :], in1=sg[:, p, half * 512:(half + 1) * 512], op=ALU.mult),
                 (K("ps%d" % (2 + half)), K("sg%d_%d" % (p, half))), (K("t1%d" % half),))
        for half in range(2):
            for k in range(4):
                P.op(TE, lambda e, k=k, half=half, p=p: e.matmul(ps[2 + half][:, :], lhsT=yT[:, p, 8 + k, :], rhs=Wb[:, k, half * 512:(half + 1) * 512], start=(k == 0), stop=(k == 3)), (K("yTb%d" % p), K("W")), (K("ps%d" % (2 + half)),))
            P.op(V, lambda e, half=half, p=p: e.tensor_tensor(out=t2[:, half * 512:(half + 1) * 512], in0=ps[2 + half][:, :], in1=sg[:, p, 1024 + half * 512:1024 + (half + 1) * 512], op=ALU.mult),
                 (K("ps%d" % (2 + half)), K("sg%d_%d" % (p, 2 + half))), (K("t2%d" % half),))
            P.op(G, lambda e, half=half: e.tensor_tensor(out=mg[:, half * 512:(half + 1) * 512], in0=t1[:, half * 512:(half + 1) * 512], in1=t2[:, half * 512:(half + 1) * 512], op=ALU.add),
                 (K("t1%d" % half), K("t2%d" % half)), (K("mg%d" % half),))
        for k in range(8):
            P.op(TE, lambda e, k=k: e.transpose(out=psT2[:, k * 128:(k + 1) * 128], in_=mg[:, k * 128:(k + 1) * 128], identity=identb[:]), (K("mg0"), K("mg1"), K("identb")), (K("psT2"),))
        P.op(V, lambda e: e.tensor_copy(out=mT[:], in_=psT2[:].rearrange("p (k t) -> p k t", k=8)), (K("psT2"),), (K("mT"),))
        for half in range(2):
            for k in range(8):
                P.op(TE, lambda e, k=k, half=half: e.matmul(ps[4 + half][:, :], lhsT=mT[:, k, :], rhs=Wo[:, k, half * 512:(half + 1) * 512], start=(k == 0), stop=(k == 7)), (K("mT"), K("W")), (K("ps%d" % (4 + half)),))
            P.op(V, lambda e, half=half, p=p: e.tensor_tensor(out=xo[:, half * 512:(half + 1) * 512], in0=ps[4 + half][:, :], in1=xt[:, p, half * 512:(half + 1) * 512], op=ALU.add),
                 (K("ps%d" % (4 + half)), X), (K("xo%d" % half),))
        P.op(S, lambda e, smp=smp: e.activation(out=junk2[:], in_=xo[:], func=AF.Square, accum_out=smp[:, 3:4]), (K("xo0"), K("xo1")), (K("junk2"), SM + "d"))
        P.op(S, lambda e, smp=smp: e.activation(out=smp[:, 4:5], in_=smp[:, 3:4], func=AF.Sqrt, scale=1.0 / D, bias=epsc[:]), (SM + "d", K("epsc")), (SM + "e",))
        P.op(V, lambda e, smp=smp: e.reciprocal(out=smp[:, 5:6], in_=smp[:, 4:5]), (SM + "e",), (SM + "f",))
        P.op(V, lambda e, smp=smp, p=p: e.scalar_tensor_tensor(out=ot[:, p, :], in0=xo[:], scalar=smp[:, 5:6], in1=fgs[:], op0=ALU.mult, op1=ALU.mult), (K("xo0"), K("xo1"), SM + "f", K("fgs")), (K("ot%d" % p),))
        P.dma(SY, lambda e, p=p, t0=t0: e.dma_start(out=out_d[t0:t0 + 128, :], in_=ot[:, p, :]), K("o%d" % p), (K("ot%d" % p),), (K("out"),))

    T1(0)
    for i in range(NTT):
        P.cap = []
        T2(i)
        L2 = P.cap
        P.cap = []
        if i + 1 < NTT:
            T1(i + 1)
        L1 = P.cap
        P.cap = None
        if TAIL_MERGE:
            P.replay_merged(L2, L1)
        else:
            P.replay_merged(L2, [])
            P.replay_merged(L1, [])
    return (K("out"),)


def build_tail(NTT):
    nc = bass.Bass("TRN2", target_bir_lowering=False)
    P = Prog(nc)
    es = ExitStack()
    TT = NTT * 128

    def din(name, shape, dt=F32):
        return nc.dram_tensor(name, list(shape), dt, kind="ExternalInput").ap()

    x_d = din("x", [TT, D])
    ya_d = din("ya", [TT, 1024], BF16)
    yb_d = din("yb", [TT, 512], BF16)
    wg_d = din("wg", [D, 2048])
    wa_d = din("wa", [1024, 1024])
    wb_d = din("wb", [512, 1024])
    wo_d = din("wo", [1024, 1024])
    ng_d = din("ng", [128, 8])
    fg_d = din("fg", [128, 1024])
    c128_d = din("c128", [128, 6, 128])
    out_d = nc.dram_tensor("out", [TT, D], F32, kind="ExternalOutput").ap()
    keys = emit_tail(nc, P, es, NTT, x_d, lambda i: ya_d[i * 128:(i + 1) * 128, :], lambda i: yb_d[i * 128:(i + 1) * 128, :],
                     wg_d, wa_d, wb_d, wo_d, ng_d, fg_d, c128_d, out_d)
    P.final_wait("sync", keys)
    P.emit(es)
    es.close()
    return nc


def tail_inputs(inp, c, NTT, ya_full, yb_full, T):
    TT = NTT * 128
    b, q = c // 4, c % 4
    w = inp["w_in"][0]
    sl = slice(q * TT, (q + 1) * TT)
    return {
        "x": np.ascontiguousarray(inp["x"][b, :T][sl]),
        "ya": np.ascontiguousarray(ya_full[b][sl]),
        "yb": np.ascontiguousarray(yb_full[b][sl]),
        "wg": np.ascontiguousarray(w[:, 8992 - 2048:]),
        "wa": np.ascontiguousarray(inp["w_branch_a"][0]),
        "wb": np.ascontiguousarray(inp["w_branch_b"][0]),
        "wo": np.ascontiguousarray(inp["w_out"][0]),
        "ng": np.ascontiguousarray(inp["norm_g"][0].reshape(8, 128).T),
        "fg": np.ascontiguousarray(np.broadcast_to(inp["final_norm_g"][None, :], (128, 1024))),
        "c128": _c128(),
    }


_CACHE = {}


def kernel_unfused(**inputs):
    inp = {k: np.asarray(v) for k, v in inputs.items()}
    T = inp["x"].shape[1]
    if ("mix", T) not in _CACHE:
        _CACHE[("mix", T)] = build_mixer(T, do_nsa=True)
    nc1 = _CACHE[("mix", T)]
    maps = [mixer_inputs(inp, c // 4, c % 4, T) for c in range(8)]
    res = run_bass_kernel_spmd(nc1, maps, core_ids=list(range(8)))
    ya_full = [np.concatenate([np.asarray(res.results[4 * b + j]["ya"]) for j in range(4)], axis=1) for b in range(2)]
    yb_full = [np.concatenate([np.asarray(res.results[4 * b + j]["yb"]) for j in range(4)], axis=1) for b in range(2)]
    NTT = T // 128 // 4
    if ("tail", NTT) not in _CACHE:
        _CACHE[("tail", NTT)] = build_tail(NTT)
    nc2 = _CACHE[("tail", NTT)]
    maps2 = [tail_inputs(inp, c, NTT, ya_full, yb_full, T) for c in range(8)]
    res2 = run_bass_kernel_spmd(nc2, maps2, core_ids=list(range(8)))
    out = np.stack([np.concatenate([np.asarray(res2.results[4 * b + q]["out"]) for q in range(4)], axis=0) for b in range(2)], axis=0)
    return out.astype(np.float32)


def build_fused(T):
    do_nsa = True
    NT = T // 128
    nc = bass.Bass("TRN2", target_bir_lowering=False)
    P = Prog(nc)
    es = ExitStack()

    def din(name, shape, dt=F32):
        return nc.dram_tensor(name, list(shape), dt, kind="ExternalInput").ap()

    x_d = din("x", [T, D])
    pos_d = din("pos", [128, NT], I32)
    ng_d = din("ng", [128, 8])
    c128_d = din("c128", [128, 6, 128])
    w1_d = din("cw1", [64, 2, 32, 128])
    w2_d = din("cw2", [128, 2, 64])
    cpos_d = din("cpos", [64, 2, 32])
    raug_d = din("raug", [128, 4, 193])
    E_d = din("E", [128, T])
    sel_d = din("selc", [128, 4, 256])
    nm_d = din("negm", [128, 2, 256])
    per = []
    for j in range(4):
        per.append((din("wfm_%d" % j, [D, NFM]), din("wtm_%d" % j, [D, NTM]), din("convw_%d" % j, [128, 4, 4]),
                    din("convb_%d" % j, [128, 4]), din("misc_%d" % j, [128, 16]), din("mhg_%d" % j, [128, 256])))
    wg_d = din("wg", [D, 2048])
    wa_d = din("wa", [1024, 1024])
    wb_d = din("wb", [512, 1024])
    wo_d = din("wo", [1024, 1024])
    fg_d = din("fg", [128, 1024])
    ya_s = nc.dram_tensor("ya_s", [T, 1024], BF16).ap()
    yb_s = nc.dram_tensor("yb_s", [T, 512], BF16).ap()
    out_d = nc.dram_tensor("out", [T, D], F32, kind="ExternalOutput").ap()

    def sb(name, shape, dt=F32):
        return es.enter_context(nc.sbuf_tensor(name, list(shape), dt))

    def pst(name, shape, dt=F32):
        return es.enter_context(nc.psum_tensor(name, list(shape), dt))

    Wfm = sb("Wfm", [128, 8, NFM], BF16)
    Wtm = sb("Wtm", [128, 8, NTM], BF16)
    stg = sb("stg", [128, 2, NTM], F32)
    ngs = sb("ngs", [128, 8])
    cws = sb("cws", [128, 4, 4])
    cbs = sb("cbs", [128, 4])
    misc = sb("miscs", [128, 16])
    mhg = sb("mhgs", [128, 256])
    c128 = sb("c128s", [128, 6, 128])
    identb = sb("identb", [128, 128], BF16)
    ident = c128[:, 0, :]
    Utri = c128[:, 1, :]
    ones = c128[:, 2, :]
    cnegB = c128[:, 3, :]
    maskT01 = c128[:, 4, :]
    xt = sb("xt", [128, 2, D])
    junk = sb("junk", [128, D], BF16)
    hn = sb("hn", [128, 2, D], BF16)
    hT = sb("hT", [128, 2, 8, 128], BF16)
    sm = sb("sm", [128, 2, 32])
    convbuf = sb("convbuf", [128, 4, 131])
    cacc = sb("cacc", [128, 4, 128])
    qkT = sb("qkT", [128, 2, 4, 128], BF16)
    Vaug = sb("Vaug", [128, 2, 257], BF16)
    og = sb("og", [128, 2, 256])
    zg = sb("zg", [128, 2, 256])
    G2 = sb("G2", [128, 2, 2])
    dbg = sb("dbg", [128, 2, 128])
    Bm = sb("Bm", [128, 128])
    DT = sb("DT", [128, 128])
    DTm = sb("DTm", [128, 128], BF16)
    Wm = sb("Wm", [128, 128])
    qwT = sb("qwT", [128, 2, 128], BF16)
    smT = sb("smT", [128, 128], BF16)
    hbuf = sb("hbuf", [128, 256])
    hbuf2 = sb("hbuf2", [128, 256])
    bnst = sb("bnst", [128, 8])
    kwt = sb("kwt", [128, 256], BF16)
    ktok = sb("ktok", [128, 2, 256], BF16)
    Cst = sb("Cst", [128, 2, 257])
    Cb = sb("Cb", [128, 2, 257], BF16)
    mcol = sb("mcol", [128, 1])
    epsc = sb("epsc", [128, 1])
    negpi = sb("negpi", [128, 1])
    tmC = sb("tmC", [128, 2, TMC])
    yat = sb("yat", [128, 2, 256], BF16)
    ybt = sb("ybt", [128, 2, 128], BF16)

    if do_nsa:
        W1b = sb("W1b", [64, 2, 32, 128], BF16)
        W2b = sb("W2b", [128, 2, 64], BF16)
        posT = sb("posT", [64, 2, 32], BF16)
        biash = sb("biash", [128, 2])
        cpre = sb("cpre", [64, 2, 2, 144], BF16)
        kcT = sb("kcT", [64, 512], BF16)
        GV = sb("GV", [128, 512], BF16)
        Raug = sb("Raug", [128, 4, 193], BF16)
        ksT = sb("ksT", [64, T], BF16)
        kwT = sb("kwT", [64, 8 * 128], BF16)
        Vs = sb("Vs", [128, NT, 65], BF16)
        Vw = sb("Vw", [128, 8, 65], BF16)
        Eb = sb("Eb", [128, T], BF16)
        selc = sb("selcs", [128, 4, 256])
        negm = sb("negms", [128, 2, 256], BF16)
        posi = sb("posi", [128, NT], I32)
        posf = sb("posf", [128, NT])
        ang = sb("ang", [128, NT, 8])
        ang2 = sb("ang2", [128, NT, 8])
        ang3 = sb("ang3", [128, NT, 8])
        angi = sb("angi", [128, NT, 8], I32)
        cossin = sb("cossin", [128, NT, 16])
        qtok = sb("qtok", [128, 2, 4, 64])
        rp = sb("rp", [128, 4, 64])
        rtmp = sb("rtmp", [128, 4, 4, 8])
        qrawT = sb("qrawT", [64, 512], BF16)
        qropT = sb("qropT", [64, 256], BF16)
        PcT = sb("PcT", [128, 4, 512], BF16)
        PT = sb("PT", [128, 2, 512], BF16)
        cmask = sb("cmask", [128, 128], BF16)
        hx = sb("hx", [128, 4, 16])
        gl = sb("gl", [128, 16], BF16)
        imp = sb("imp", [128, 128])
        sc = sb("sc", [128, 2, 128])
        m8 = sb("m8", [128, 16])
        sel01 = sb("sel01", [128, 128])
        NegSel = sb("NegSel", [128, 128])
        NegSelT = sb("NegSelT", [128, 256], BF16)
        gsig = sb("gsig", [128, 8])
        nzg = sb("nzg", [128, 2, 128])
        ynacc = sb("ynacc", [128, 2, 64])
        rr = sb("rr", [128, 16])

    ps = [pst("ps%d" % i, [128, 512]) for i in range(5)]
    ps5 = pst("ps5", [128, 1024], BF16)
    ps6 = pst("ps6", [128, 512])
    ps7 = pst("ps7", [128, 512])

    V, S, G, TE, SY = "vector", "scalar", "gpsimd", "tensor", "sync"

    def run_pass(wfm_d, wtm_d, cw_d, cb_d, misc_d, mhg_d, ya_d, yb_d):
        P.dma(SY, lambda e: e.dma_start(out=ngs[:], in_=ng_d), "c0", (), ("ngs",))
        P.dma(SY, lambda e: e.dma_start(out=cws[:], in_=cw_d), "c1", (), ("cws",))
        P.dma(SY, lambda e: e.dma_start(out=cbs[:], in_=cb_d), "c2", (), ("cbs",))
        P.dma(SY, lambda e: e.dma_start(out=misc[:], in_=misc_d), "c3", (), ("misc",))
        P.dma(SY, lambda e: e.dma_start(out=mhg[:], in_=mhg_d), "c4", (), ("mhg",))
        P.dma(SY, lambda e: e.dma_start(out=c128[:], in_=c128_d), "c5", (), ("c128",))
        P.dma(G, lambda e: e.dma_start(out=identb[:], in_=c128_d[:, 0, :]), "c6", (), ("identb",))
        for k in range(8):
            sp_ = k % 2
            for (wd, Wt, ncol, nm) in ((wfm_d, Wfm, NFM, "f"), (wtm_d, Wtm, NTM, "t")):
                P.dma(SY, lambda e, wd=wd, ncol=ncol, k=k, sp_=sp_: e.dma_start(out=stg[:, sp_, 0:ncol], in_=wd[k * 128:(k + 1) * 128, :]),
                      "w%d" % sp_, (), ("stg%d" % sp_,))
                P.op(V, lambda e, Wt=Wt, ncol=ncol, k=k, sp_=sp_: e.tensor_scalar(out=Wt[:, k, :], in0=stg[:, sp_, 0:ncol], scalar1=ngs[:, k:k + 1], scalar2=None, op0=ALU.mult),
                     ("stg%d" % sp_, "ngs"), ("W",))
        P.op(V, lambda e: e.memset(epsc[:], EPS), (), ("epsc",))
        P.op(V, lambda e: e.memset(negpi[:], -math.pi), (), ("negpi",))
        P.op(V, lambda e: e.memset(Cst[:], 0.0), (), ("Cst",))
        P.op(V, lambda e: e.memset(Cb[:], 0.0), (), ("Cb",))
        P.op(V, lambda e: e.memset(mcol[:], 0.0), (), ("m",))
        P.op(V, lambda e: e.memset(convbuf[:], 0.0), (), ("convbuf",))
        P.op(V, lambda e: e.memset(Vaug[:], 1.0), (), ("Vaug0", "Vaug1"))

        if do_nsa:
            P.dma(G, lambda e: e.dma_start(out=W1b[:], in_=w1_d), "n0", (), ("W1b",))
            P.dma(G, lambda e: e.dma_start(out=W2b[:], in_=w2_d), "n1", (), ("W2b",))
            P.dma(G, lambda e: e.dma_start(out=posT[:], in_=cpos_d), "n2", (), ("posT",))
            P.dma(G, lambda e: e.dma_start(out=Raug[:], in_=raug_d), "n3", (), ("Raug",))
            P.dma(G, lambda e: e.dma_start(out=negm[:], in_=nm_d), "n4", (), ("negm",))
            P.dma(SY, lambda e: e.dma_start(out=selc[:], in_=sel_d), "n5", (), ("selc",))
            P.dma(SY, lambda e: e.dma_start(out=posi[:], in_=pos_d), "n6", (), ("posi",))
            for c0 in range(0, T, 2048):
                c1 = min(T, c0 + 2048)
                P.dma(G, lambda e, c0=c0, c1=c1: e.dma_start(out=Eb[:, c0:c1], in_=E_d[:, c0:c1]), "n7", (), ("Eb",))
            P.op(V, lambda e: e.memset(cpre[:], 0.0), (), ("cpre0", "cpre1"))
            P.op(V, lambda e: e.memset(kcT[:], 0.0), (), ("kcT",))
            P.op(V, lambda e: e.memset(GV[:], 0.0), (), ("GV",))
            P.op(G, lambda e: e.memset(Vs[:], 1.0), (), tuple("Vs%d" % t_ for t_ in range(NT)))
            P.op(G, lambda e: e.memset(Vw[:], 1.0), (), tuple("Vw%d" % t_ for t_ in range(8)))
            P.op(V, lambda e: e.tensor_copy(out=posf[:], in_=posi[:]), ("posi",), ("posf",))
            for f in range(8):
                P.op(V, lambda e, f=f: e.tensor_scalar(out=ang[:, :, f], in0=posf[:], scalar1=misc[:, 8 + f:9 + f], scalar2=None, op0=ALU.mult), ("posf", "misc"), ("ang",))
            for (off_, c0_) in ((0.75, 0), (0.5, 8)):
                P.op(V, lambda e, off_=off_: e.tensor_scalar(out=ang2[:], in0=ang[:], scalar1=1.0 / (2.0 * math.pi), scalar2=off_, op0=ALU.mult, op1=ALU.add), ("ang", "cossin"), ("ang2",))
                P.op(V, lambda e: e.tensor_copy(out=angi[:], in_=ang2[:]), ("ang2",), ("angi",))
                P.op(V, lambda e: e.tensor_copy(out=ang3[:], in_=angi[:]), ("angi",), ("ang3",))
                P.op(V, lambda e: e.tensor_tensor(out=ang2[:], in0=ang2[:], in1=ang3[:], op=ALU.subtract), ("ang2", "ang3"), ("ang2",))
                P.op(V, lambda e: e.tensor_scalar(out=ang3[:], in0=ang2[:], scalar1=0.0, scalar2=None, op0=ALU.is_lt), ("ang2",), ("ang3",))
                P.op(V, lambda e: e.tensor_tensor(out=ang2[:], in0=ang2[:], in1=ang3[:], op=ALU.add), ("ang2", "ang3"), ("ang2",))
                P.op(S, lambda e, c0_=c0_: e.activation(out=cossin[:, :, c0_:c0_ + 8], in_=ang2[:], func=AF.Sin, scale=2.0 * math.pi, bias=negpi[:]), ("ang2", "negpi"), ("cossin",))
            for kv in range(2):
                for l in range(32):
                    P.op(TE, lambda e, kv=kv, l=l: e.matmul(ps[4][:, 300 + kv:301 + kv], lhsT=W1b[:, kv, l, :], rhs=posT[:, kv, l:l + 1], start=(l == 0), stop=(l == 31)),
                         ("W1b", "posT"), ("ps4",))
            P.op(V, lambda e: e.tensor_copy(out=biash[:, 0:2], in_=ps[4][:, 300:302]), ("ps4",), ("biash",))

        def emit_A(i):
            p = i % 2
            t0 = i * 128
            X, HN, HT, SM = "xt%d" % p, "hn%d" % p, "hT%d" % p, "sm%d" % p
            smp = sm[:, p, :]
            P.dma(SY, lambda e, p=p, t0=t0: e.dma_start(out=xt[:, p, :], in_=x_d[t0:t0 + 128, :]), "x%d" % p, (), (X,))
            P.op(S, lambda e, p=p, smp=smp: e.activation(out=junk[:], in_=xt[:, p, :], func=AF.Square, accum_out=smp[:, 0:1]), (X,), ("junk", SM + "a"))
            P.op(S, lambda e, smp=smp: e.activation(out=smp[:, 1:2], in_=smp[:, 0:1], func=AF.Sqrt, scale=1.0 / D, bias=epsc[:]), (SM + "a", "epsc"), (SM + "b",))
            P.op(V, lambda e, smp=smp: e.reciprocal(out=smp[:, 2:3], in_=smp[:, 1:2]), (SM + "b",), (SM + "c",))
            P.op(S, lambda e, p=p, smp=smp: e.activation(out=hn[:, p, :], in_=xt[:, p, :], func=AF.Copy, scale=smp[:, 2:3]), (X, SM + "c"), (HN,))
            for k in range(8):
                P.op(TE, lambda e, p=p, k=k: e.transpose(out=ps5[:, k * 128:(k + 1) * 128], in_=hn[:, p, k * 128:(k + 1) * 128], identity=identb[:]),
                     (HN, "identb"), ("ps5",))
            P.op(V, lambda e, p=p: e.tensor_copy(out=hT[:, p, :, :], in_=ps5[:].rearrange("p (k t) -> p k t", k=8)), ("ps5",), (HT,))
            for grp in range(4):
                for k in range(8):
                    P.op(TE, lambda e, p=p, k=k, grp=grp: e.matmul(ps7[:, grp * 128:(grp + 1) * 128], lhsT=Wfm[:, k, grp * 128:(grp + 1) * 128], rhs=hT[:, p, k, :], start=(k == 0), stop=(k == 7)),
                         (HT, "W"), ("ps7",))
            P.op(V, lambda e: e.tensor_copy(out=convbuf[:, :, 0:3], in_=convbuf[:, :, 128:131]), ("convbuf",), ("convbuf",))
            P.op(S, lambda e: e.activation(out=convbuf[:, :, 3:131], in_=ps7[:].rearrange("p (g t) -> p g t", g=4), func=AF.Copy), ("ps7",), ("convbuf",))
            if do_nsa:
                for c in range(2):
                    for k in range(8):
                        P.op(TE, lambda e, p=p, k=k, c=c: e.matmul(ps7[0:64, c * 128:(c + 1) * 128], lhsT=Wfm[:, k, 512 + c * 64:512 + (c + 1) * 64], rhs=hT[:, p, k, :], start=(k == 0), stop=(k == 7)),
                             (HT, "W"), ("ps7",))
                P.op(V, lambda e, p=p: e.tensor_copy(out=cpre[:, p, :, 0:16], in_=cpre[:, 1 - p, :, 128:144]), ("cpre%d" % (1 - p),), ("cpre%d" % p,))
                P.op(S, lambda e, p=p: e.activation(out=cpre[:, p, :, 16:144], in_=ps7[0:64, 0:256].rearrange("p (c t) -> p c t", c=2), func=AF.Copy), ("ps7",), ("cpre%d" % p,))
            for grp in range(4):
                P.op(V, lambda e, grp=grp: e.tensor_scalar(out=cacc[:, grp, :], in0=convbuf[:, grp, 0:128], scalar1=cws[:, grp, 0:1], scalar2=cbs[:, grp:grp + 1], op0=ALU.mult, op1=ALU.add),
                     ("convbuf", "cws", "cbs"), ("cacc%d" % grp,))
                for j in range(1, 4):
                    P.op(V, lambda e, grp=grp, j=j: e.scalar_tensor_tensor(out=cacc[:, grp, :], in0=convbuf[:, grp, j:j + 128], scalar=cws[:, grp, j:j + 1], in1=cacc[:, grp, :], op0=ALU.mult, op1=ALU.add),
                         ("convbuf", "cws", "cacc%d" % grp), ("cacc%d" % grp,))
            P.op(S, lambda e, p=p: e.activation(out=qkT[:, p, :, :], in_=cacc[:], func=AF.Silu), tuple("cacc%d" % g_ for g_ in range(4)), ("qkT%d" % p,))
            for c in range(2):
                P.op(TE, lambda e, p=p, c=c: e.transpose(out=ps5[:, c * 128:(c + 1) * 128], in_=qkT[:, p, 2 + c, :], identity=identb[:]), ("qkT%d" % p, "identb"), ("ps5",))
            P.op(V, lambda e, p=p: e.tensor_copy(out=ktok[:, p, :], in_=ps5[:, 0:256]), ("ps5",), ("ktok%d" % p,))
            for (pb, pk_, c0, n) in ((ps7, "ps7", 0, TMA), (ps7, "ps7", TMA, TMB), (ps7, "ps7", TMA + TMB, TMC)):
                for k in range(8):
                    P.op(TE, lambda e, p=p, k=k, pb=pb, c0=c0, n=n: e.matmul(pb[:, 0:n], lhsT=hT[:, p, k, :], rhs=Wtm[:, k, c0:c0 + n], start=(k == 0), stop=(k == 7)),
                         (HT, "W"), (pk_,))
                if c0 == 0:
                    P.op(V, lambda e, p=p: e.tensor_copy(out=Vaug[:, p, 0:256], in_=ps7[:, 0:256]), ("ps7",), ("Vaug%d" % p,))
                    P.op(S, lambda e, p=p: e.activation(out=og[:, p, :], in_=ps7[:, 256:512], func=AF.Sigmoid), ("ps7",), ("og%d" % p,))
                elif c0 == TMA:
                    P.op(S, lambda e, p=p: e.activation(out=zg[:, p, :], in_=ps7[:, 0:256], func=AF.Silu), ("ps7",), ("zg%d" % p,))
                    P.op(G, lambda e, p=p: e.tensor_tensor(out=og[:, p, :], in0=og[:, p, :], in1=zg[:, p, :], op=ALU.mult), ("og%d" % p, "zg%d" % p), ("og%d" % p,))
                    if do_nsa:
                        P.op(V, lambda e, p=p: e.tensor_copy(out=qtok[:, p, 2:4, :], in_=ps7[:, B_QOTH:B_QOTH + 128].rearrange("p (h d) -> p h d", h=2)), ("ps7",), ("qtok%d" % p,))
                        P.op(S, lambda e, p=p: e.activation(out=nzg[:, p, :], in_=ps7[:, B_NZ:B_NZ + 128], func=AF.Silu), ("ps7",), ("nzg%d" % p,))
                else:
                    P.op(V, lambda e, p=p: e.tensor_copy(out=tmC[:, p, :], in_=ps7[:, 0:TMC]), ("ps7",), ("tmC%d" % p,))

        def emit_B(i):
            p = i % 2
            t0 = i * 128
            SM = "sm%d" % p
            smp = sm[:, p, :]
            TC = "tmC%d" % p
            P.op(V, lambda e, smp=smp, p=p: e.tensor_scalar(out=smp[:, 17:18], in0=tmC[:, p, C_F:C_F + 1], scalar1=misc[:, 0:1], scalar2=None, op0=ALU.add), (TC, "misc"), (SM + "d0",))
            P.op(S, lambda e, smp=smp: e.activation(out=smp[:, 3:4], in_=smp[:, 17:18], func=AF.Exp, scale=-1.0), (SM + "d0",), (SM + "d",))
            P.op(S, lambda e, p=p, smp=smp: e.activation(out=G2[:, p, 0:1], in_=smp[:, 3:4], func=AF.Ln, scale=1.0, bias=1.0), (SM + "d",), ("G2%d" % p,))
            P.op(V, lambda e, p=p: e.tensor_scalar(out=G2[:, p, 1:2], in0=tmC[:, p, C_I:C_I + 1], scalar1=misc[:, 1:2], scalar2=None, op0=ALU.add), (TC, "misc"), ("G2%d" % p,))
            P.op(TE, lambda e, p=p: e.matmul(ps6[:, 128:130], lhsT=Utri, rhs=G2[:, p, :], start=True, stop=True), ("G2%d" % p, "c128"), ("ps6",))
            P.op(TE, lambda e, p=p: e.matmul(ps6[:, 130:132], lhsT=ones, rhs=G2[:, p, :], start=True, stop=True), ("G2%d" % p, "c128"), ("ps6",))
            P.op(V, lambda e, smp=smp: e.tensor_copy(out=smp[:, 20:24], in_=ps6[:, 128:132]), ("ps6",), (SM + "cum",))
            P.op(V, lambda e, p=p, smp=smp: e.tensor_tensor(out=smp[:, 4:5], in0=smp[:, 20:21], in1=G2[:, p, 1:2], op=ALU.add), (SM + "cum", "G2%d" % p), (SM + "e",))
            P.op(V, lambda e, smp=smp: e.tensor_scalar(out=smp[:, 5:6], in0=smp[:, 4:5], scalar1=-LN16, scalar2=None, op0=ALU.add), (SM + "e",), (SM + "f",))
            P.op(V, lambda e, smp=smp: e.tensor_scalar(out=dbg[:, 0, :], in0=ident, scalar1=smp[:, 4:5], scalar2=None, op0=ALU.mult), (SM + "e", "c128"), ("dbg0",))
            P.op(TE, lambda e: e.matmul(ps6[:, 0:128], lhsT=ones, rhs=dbg[:, 0, :], start=True, stop=True), ("dbg0", "c128"), ("ps6",))
            P.op(V, lambda e: e.tensor_tensor(out=Bm[:], in0=ps6[:, 0:128], in1=cnegB, op=ALU.add), ("ps6", "c128"), ("Bm",))
            P.op(V, lambda e, smp=smp: e.reduce_max(out=smp[:, 6:7], in_=Bm[:], axis=AX.X), ("Bm",), (SM + "g",))
            P.op(V, lambda e, smp=smp: e.reduce_max(out=smp[:, 7:8], in_=ps6[:, 0:128], axis=AX.X), ("ps6",), (SM + "h",))
            P.op(V, lambda e, smp=smp: e.tensor_tensor(out=smp[:, 8:9], in0=smp[:, 6:7], in1=mcol[:], op=ALU.max), (SM + "g", "m"), (SM + "i",))
            P.op(V, lambda e, smp=smp: e.tensor_tensor(out=smp[:, 9:10], in0=smp[:, 7:8], in1=mcol[:], op=ALU.max), (SM + "h", "m"), (SM + "j",))
            P.op(V, lambda e, smp=smp: e.tensor_scalar(out=dbg[:, 1, :], in0=ident, scalar1=smp[:, 8:9], scalar2=None, op0=ALU.mult), (SM + "i", "c128"), ("dbg1",))
            P.op(TE, lambda e: e.matmul(ps6[:, 0:128], lhsT=ones, rhs=dbg[:, 1, :], start=True, stop=True), ("dbg1", "c128"), ("ps6",))
            P.op(S, lambda e, smp=smp: e.activation(out=DT[:], in_=ps6[:, 0:128], func=AF.Exp, scale=-1.0, bias=smp[:, 5:6]), ("ps6", SM + "f"), ("DT",))
            P.op(G, lambda e: e.tensor_tensor(out=DTm[:], in0=DT[:], in1=maskT01, op=ALU.mult), ("DT", "c128"), ("DTm",))
            P.op(S, lambda e: e.activation(out=Wm[:], in_=ps6[:, 0:128], func=AF.Exp, scale=-1.0, bias=mcol[:]), ("ps6", "m"), ("Wm",))
            for c in range(2):
                P.op(G, lambda e, p=p, c=c: e.tensor_tensor(out=qwT[:, c, :], in0=qkT[:, p, c, :], in1=Wm[:], op=ALU.mult), ("qkT%d" % p, "Wm"), ("qwT",))
            for c in range(2):
                P.op(TE, lambda e, p=p, c=c: e.matmul(ps6[:, 0:128], lhsT=qkT[:, p, 2 + c, :], rhs=qkT[:, p, c, :], start=(c == 0), stop=(c == 1)), ("qkT%d" % p,), ("ps6",))
            P.op(V, lambda e: e.tensor_tensor(out=smT[:], in0=ps6[:, 0:128], in1=DTm[:], op=ALU.mult), ("ps6", "DTm"), ("smT",))
            for c in range(2):
                P.op(TE, lambda e, c=c: e.matmul(ps6[:, 132:389], lhsT=qwT[:, c, :], rhs=Cb[:, c, :], start=(c == 0), stop=False), ("qwT", "Cb"), ("ps6",))
            P.op(TE, lambda e, p=p: e.matmul(ps6[:, 132:389], lhsT=smT[:], rhs=Vaug[:, p, :], start=False, stop=True), ("smT", "Vaug%d" % p), ("ps6",))
            P.op(V, lambda e, smp=smp: e.tensor_tensor(out=smp[:, 10:11], in0=smp[:, 20:21], in1=smp[:, 8:9], op=ALU.subtract), (SM + "cum", SM + "i"), (SM + "k",))
            P.op(S, lambda e, smp=smp: e.activation(out=smp[:, 11:12], in_=smp[:, 10:11], func=AF.Exp), (SM + "k",), (SM + "l",))
            P.op(S, lambda e, smp=smp: e.activation(out=smp[:, 19:20], in_=ps6[:, 388:389], func=AF.Abs), ("ps6",), (SM + "m0",))
            P.op(V, lambda e, smp=smp: e.tensor_tensor(out=smp[:, 12:13], in0=smp[:, 19:20], in1=smp[:, 11:12], op=ALU.max), (SM + "m0", SM + "l"), (SM + "m",))
            P.op(V, lambda e, smp=smp: e.reciprocal(out=smp[:, 13:14], in_=smp[:, 12:13]), (SM + "m",), (SM + "n",))
            P.op(V, lambda e, smp=smp: e.tensor_scalar(out=hbuf[:], in0=ps6[:, 132:388], scalar1=smp[:, 13:14], scalar2=None, op0=ALU.mult), ("ps6", SM + "n"), ("hbuf",))
            P.op(S, lambda e, smp=smp: e.activation(out=smp[:, 15:16], in_=smp[:, 9:10], func=AF.Exp, scale=-1.0, bias=smp[:, 5:6]), (SM + "j", SM + "f"), (SM + "p",))
            P.op(S, lambda e, smp=smp: e.activation(out=smp[:, 16:17], in_=smp[:, 9:10], func=AF.Exp, scale=-1.0, bias=mcol[:]), (SM + "j", "m"), (SM + "q",))
            P.op(V, lambda e, smp=smp, p=p: e.tensor_scalar(out=kwt[:], in0=ktok[:, p, :], scalar1=smp[:, 15:16], scalar2=None, op0=ALU.mult), ("ktok%d" % p, SM + "p"), ("kwt",))
            for c in range(2):
                P.op(TE, lambda e, p=p, c=c: e.matmul(ps6[:, 132:389], lhsT=kwt[:, c * 128:(c + 1) * 128], rhs=Vaug[:, p, :], start=True, stop=True), ("kwt", "Vaug%d" % p), ("ps6",))
                P.op(V, lambda e, c=c, smp=smp: e.scalar_tensor_tensor(out=Cst[:, c, :], in0=Cst[:, c, :], scalar=smp[:, 16:17], in1=ps6[:, 132:389], op0=ALU.mult, op1=ALU.add),
                     ("Cst", SM + "q", "ps6"), ("Cst",))
                P.op(S, lambda e, c=c: e.activation(out=Cb[:, c, :], in_=Cst[:, c, :], func=AF.Copy), ("Cst",), ("Cb",))
            P.op(V, lambda e, smp=smp: e.tensor_tensor(out=mcol[:], in0=smp[:, 9:10], in1=smp[:, 22:23], op=ALU.subtract), (SM + "j", SM + "cum"), ("m",))
            P.op(V, lambda e: e.bn_stats(out=bnst[:, 0:6], in_=hbuf[:]), ("hbuf",), ("bnst",))
            P.op(V, lambda e: e.bn_aggr(out=bnst[:, 6:8], in_=bnst[:, 0:6]), ("bnst",), ("bnst",))
            P.op(S, lambda e, smp=smp: e.activation(out=smp[:, 18:19], in_=bnst[:, 7:8], func=AF.Sqrt, scale=1.0, bias=epsc[:]), ("bnst", "epsc"), (SM + "o0",))
            P.op(V, lambda e, smp=smp: e.reciprocal(out=smp[:, 14:15], in_=smp[:, 18:19]), (SM + "o0",), (SM + "o",))
            P.op(V, lambda e, smp=smp: e.tensor_scalar(out=hbuf2[:], in0=hbuf[:], scalar1=bnst[:, 6:7], scalar2=smp[:, 14:15], op0=ALU.subtract, op1=ALU.mult), ("hbuf", "bnst", SM + "o"), ("hbuf2",))
            P.op(G, lambda e: e.tensor_tensor(out=hbuf2[:], in0=hbuf2[:], in1=mhg[:], op=ALU.mult), ("hbuf2", "mhg"), ("hbuf2",))
            P.op(G, lambda e, p=p: e.tensor_tensor(out=yat[:, p, :], in0=hbuf2[:], in1=og[:, p, :], op=ALU.mult), ("hbuf2", "og%d" % p), ("yat%d" % p,))
            P.dma(SY, lambda e, p=p, t0=t0: e.dma_start(out=ya_d[t0:t0 + 128, :], in_=yat[:, p, :]), "ya%d" % p, ("yat%d" % p,), ("ya_out",))

        def emit_C(i):
            p = i % 2
            t0 = i * 128
            TC = "tmC%d" % p
            if do_nsa:
                psS = (ps[0], ps[1])
                psO = (ps[2], ps[3])
                QT = "qtok%d" % p
                P.op(S, lambda e, p=p: e.activation(out=qtok[:, p, 0:2, :], in_=tmC[:, p, 0:128].rearrange("p (h d) -> p h d", h=2), func=AF.Copy), (TC,), (QT,))
                P.op(S, lambda e, p=p: e.activation(out=rp[:], in_=tmC[:, p, 0:256].rearrange("p (h d) -> p h d", h=4), func=AF.Copy), (TC,), ("rp",))
                src = tmC[:, p, 0:256].rearrange("p (h d) -> p h d", h=4)
                cosb = cossin[:, i, 0:8].unsqueeze(1).to_broadcast([128, 4, 8])
                sinb = cossin[:, i, 8:16].unsqueeze(1).to_broadcast([128, 4, 8])
                P.op(V, lambda e, src=src, cosb=cosb: e.tensor_tensor(out=rtmp[:, 0], in0=src[:, :, 0:8], in1=cosb, op=ALU.mult), (TC, "cossin"), ("rtmp0",))
                P.op(V, lambda e, src=src, sinb=sinb: e.tensor_tensor(out=rtmp[:, 1], in0=src[:, :, 8:16], in1=sinb, op=ALU.mult), (TC, "cossin"), ("rtmp1",))
                P.op(V, lambda e, src=src, cosb=cosb: e.tensor_tensor(out=rtmp[:, 2], in0=src[:, :, 8:16], in1=cosb, op=ALU.mult), (TC, "cossin"), ("rtmp2",))
                P.op(V, lambda e, src=src, sinb=sinb: e.tensor_tensor(out=rtmp[:, 3], in0=src[:, :, 0:8], in1=sinb, op=ALU.mult), (TC, "cossin"), ("rtmp3",))
                P.op(V, lambda e: e.tensor_tensor(out=rp[:, :, 0:8], in0=rtmp[:, 0], in1=rtmp[:, 1], op=ALU.subtract), ("rtmp0", "rtmp1", "rp"), ("rp",))
                P.op(V, lambda e: e.tensor_tensor(out=rp[:, :, 8:16], in0=rtmp[:, 2], in1=rtmp[:, 3], op=ALU.add), ("rtmp2", "rtmp3", "rp"), ("rp",))
                P.op(V, lambda e, i=i, p=p: e.tensor_copy(out=Vs[:, i, 0:64], in_=tmC[:, p, C_VS:C_VS + 64]), (TC,), ("Vs%d" % i,))
                P.op(V, lambda e, i=i, p=p: e.tensor_copy(out=Vw[:, i % 8, 0:64], in_=tmC[:, p, C_VW:C_VW + 64]), (TC,), ("Vw%d" % (i % 8),))
                P.op(V, lambda e, p=p: e.tensor_tensor(out=gsig[:, 0:6], in0=tmC[:, p, C_G:C_G + 6], in1=misc[:, 2:8], op=ALU.add), (TC, "misc"), ("gsig",))
                P.op(S, lambda e: e.activation(out=gsig[:, 0:6], in_=gsig[:, 0:6], func=AF.Sigmoid), ("gsig",), ("gsig",))
                for h in range(4):
                    P.op(TE, lambda e, h=h, p=p: e.transpose(out=ps[0][0:64, h * 128:(h + 1) * 128], in_=qtok[:, p, h, :], identity=ident), (QT, "c128"), ("ps0",))
                for h in range(4):
                    P.op(TE, lambda e, h=h: e.transpose(out=ps[1][0:64, h * 128:(h + 1) * 128], in_=rp[:, h, :], identity=ident), ("rp", "c128"), ("ps1",))
                P.op(V, lambda e: e.tensor_copy(out=qrawT[:], in_=ps[0][0:64, 0:512]), ("ps0",), ("qrawT",))
                P.op(S, lambda e: e.activation(out=qropT[:], in_=ps[1][0:64, 0:256], func=AF.Copy), ("ps1",), ("qropT",))
                P.op(V, lambda e, t0=t0: e.tensor_copy(out=ksT[:, t0:t0 + 128], in_=ps[1][0:64, 256:384]), ("ps1",), ("ksT%d" % i,))
                P.op(S, lambda e, i=i: e.activation(out=kwT[:, (i % 8) * 128:(i % 8 + 1) * 128], in_=ps[1][0:64, 384:512], func=AF.Copy), ("ps1",), ("kwT%d" % (i % 8),))
                for kv in range(2):
                    for l in range(32):
                        P.op(TE, lambda e, kv=kv, l=l, p=p: e.matmul(ps[4][:, 256 + kv * 8:264 + kv * 8], lhsT=W1b[:, kv, l, :], rhs=cpre[:, p, kv, l:l + 113:16], start=(l == 0), stop=(l == 31)),
                             ("W1b", "cpre%d" % p), ("ps4",))
                hid = ps[4][:, 256:272].rearrange("p (k n) -> p k n", k=2)
                P.op(V, lambda e, hid=hid: e.tensor_tensor(out=hx[:, 0, :].rearrange("p (k n) -> p k n", k=2), in0=hid, in1=biash[:].unsqueeze(2).to_broadcast([128, 2, 8]), op=ALU.add), ("ps4", "biash"), ("hx0",))
                P.op(V, lambda e: e.tensor_tensor(out=hx[:, 1, :], in0=hx[:, 0, :], in1=hx[:, 0, :], op=ALU.mult), ("hx0",), ("hx1",))
                P.op(V, lambda e: e.tensor_scalar(out=hx[:, 1, :], in0=hx[:, 1, :], scalar1=0.044715, scalar2=1.0, op0=ALU.mult, op1=ALU.add), ("hx1",), ("hx1",))
                P.op(V, lambda e: e.tensor_tensor(out=hx[:, 2, :], in0=hx[:, 1, :], in1=hx[:, 0, :], op=ALU.mult), ("hx1", "hx0"), ("hx2",))
                P.op(S, lambda e: e.activation(out=hx[:, 3, :], in_=hx[:, 2, :], func=AF.Sigmoid, scale=1.5957691216057308), ("hx2",), ("hx3",))
                P.op(V, lambda e: e.tensor_tensor(out=gl[:], in0=hx[:, 3, :], in1=hx[:, 0, :], op=ALU.mult), ("hx3", "hx0"), ("gl",))
                P.op(TE, lambda e: e.matmul(ps[4][0:64, 320:328], lhsT=W2b[:, 0, :], rhs=gl[:, 0:8], start=True, stop=True), ("W2b", "gl"), ("ps4",))
                P.op(V, lambda e, i=i: e.tensor_copy(out=kcT[:, 8 * i:8 * i + 8], in_=ps[4][0:64, 320:328]), ("ps4",), ("kcT",))
                P.op(V, lambda e, i=i: e.tensor_copy(out=GV[:, 8 * i:8 * i + 8], in_=gl[:, 8:16]), ("gl",), ("GV",))
                if i == 0:
                    P.op(V, lambda e: e.memset(GV[:, 0:1], 0.0), (), ("GV",))
                cc_cur = i // 16
                P.op(TE, lambda e, cc_cur=cc_cur: e.matmul(ps[4][:, 336:400], lhsT=GV[:, cc_cur * 128:(cc_cur + 1) * 128], rhs=W2b[:, 1, :], start=True, stop=True), ("GV", "W2b"), ("ps4",))
                P.op(V, lambda e, cc_cur=cc_cur: e.tensor_copy(out=Raug[:, cc_cur, 128:192], in_=ps[4][:, 336:400]), ("ps4",), ("Raug",))
                nch = cc_cur + 1
                for cc in range(nch):
                    P.op(TE, lambda e, cc=cc: e.matmul(psS[cc % 2][:, 0:512], lhsT=kcT[:, cc * 128:(cc + 1) * 128], rhs=qrawT[:], start=True, stop=True), ("kcT", "qrawT"), ("ps%d" % (cc % 2),))
                    P.op(S, lambda e, cc=cc: e.activation(out=PcT[:, cc, :], in_=psS[cc % 2][:, 0:512], func=AF.Exp, scale=0.125), ("ps%d" % (cc % 2),), ("PcT%d" % cc,))
                delta = 128.0 * (i % 16) - 15.0
                P.op(V, lambda e, delta=delta: e.tensor_scalar(out=cmask[:], in0=c128[:, 5, :], scalar1=delta, scalar2=None, op0=ALU.is_le), ("c128",), ("cmask",))
                P.op(G, lambda e, cc_cur=cc_cur: e.tensor_tensor(out=PcT[:, cc_cur, :].rearrange("p (h t) -> p h t", h=4), in0=PcT[:, cc_cur, :].rearrange("p (h t) -> p h t", h=4),
                                                                  in1=cmask[:].unsqueeze(1).to_broadcast([128, 4, 128]), op=ALU.mult), ("PcT%d" % cc_cur, "cmask"), ("PcT%d" % cc_cur,))
                for h in range(4):
                    po = psO[h % 2]
                    pk = "ps%d" % (2 + h % 2)
                    s0 = (h // 2) * 256
                    for cc in range(nch):
                        P.op(TE, lambda e, h=h, cc=cc, po=po, s0=s0: e.matmul(po[:, s0:s0 + 193], lhsT=PcT[:, cc, h * 128:(h + 1) * 128], rhs=Raug[:, cc, :], start=(cc == 0), stop=(cc == nch - 1)),
                             ("PcT%d" % cc, "Raug"), (pk,))
                for b_ in range(2):
                    P.op(V, lambda e, b_=b_: e.tensor_scalar(out=rr[:, b_:4:2], in0=psO[b_][:, 192:449:256], scalar1=1e-30, scalar2=None, op0=ALU.max), ("ps%d" % (2 + b_),), ("rrm%d" % b_,))
                P.op(V, lambda e: e.reciprocal(out=rr[:, 0:4], in_=rr[:, 0:4]), ("rrm0", "rrm1"), ("rr",))
                for h in range(4):
                    po = psO[h % 2]
                    pk = "ps%d" % (2 + h % 2)
                    s0 = (h // 2) * 256
                    if h == 0:
                        P.op(V, lambda e, po=po: e.tensor_scalar(out=imp[:], in0=po[:, 0:128], scalar1=rr[:, 0:1], scalar2=None, op0=ALU.mult), (pk, "rr"), ("imp",))
                    else:
                        P.op(V, lambda e, h=h, po=po, s0=s0: e.scalar_tensor_tensor(out=imp[:], in0=po[:, s0:s0 + 128], scalar=rr[:, h:h + 1], in1=imp[:], op0=ALU.mult, op1=ALU.add), (pk, "rr", "imp"), ("imp",))
                P.op(V, lambda e: e.tensor_tensor(out=rr[:, 4:6], in0=rr[:, 0:2], in1=gsig[:, 0:6:3], op=ALU.mult), ("rr", "gsig"), ("rg",))
                for h in range(2):
                    P.op(S, lambda e, h=h: e.activation(out=ynacc[:, h, :], in_=psO[h][:, 128:192], func=AF.Copy, scale=rr[:, 4 + h:5 + h]), ("ps%d" % (2 + h), "rg"), ("ynacc%d" % h,))
                w0 = 128 - 2 * i
                P.op(V, lambda e, w0=w0: e.scalar_tensor_tensor(out=sc[:, 0, :], in0=imp[:], scalar=1.0, in1=selc[:, 0, w0:w0 + 128], op0=ALU.add, op1=ALU.mult), ("imp", "selc"), ("sc0",))
                P.op(V, lambda e, w0=w0: e.tensor_tensor(out=sc[:, 0, :], in0=sc[:, 0, :], in1=selc[:, 2, w0:w0 + 128], op=ALU.max), ("sc0", "selc"), ("sc0",))
                P.op(V, lambda e: e.tensor_tensor(out=sc[:, 0, :], in0=sc[:, 0, :], in1=selc[:, 3, 0:128], op=ALU.max), ("sc0", "selc"), ("sc0",))
                P.op(V, lambda e: e.max(out=m8[:, 0:8], in_=sc[:, 0, :]), ("sc0",), ("m8a",))
                P.op(V, lambda e: e.match_replace(out=sc[:, 1, :], in_to_replace=m8[:, 0:8], in_values=sc[:, 0, :], imm_value=-1e30), ("sc0", "m8a"), ("sc1",))
                P.op(V, lambda e: e.max(out=m8[:, 8:16], in_=sc[:, 1, :]), ("sc1",), ("m8b",))
                P.op(V, lambda e: e.tensor_scalar(out=NegSel[:], in0=sc[:, 0, :], scalar1=m8[:, 15:16], scalar2=MASKNEG, op0=ALU.is_lt, op1=ALU.mult), ("sc0", "m8b"), ("NegSel",))
                P.op(TE, lambda e: e.transpose(out=ps[0][:, 0:128], in_=NegSel[:], identity=ident), ("NegSel", "c128"), ("ps0",))
                P.op(V, lambda e: e.tensor_copy(out=NegSelT[:, 0:128], in_=ps[0][:, 0:128]), ("ps0",), ("NegSelT",))
                P.op(S, lambda e: e.activation(out=NegSelT[:, 128:256], in_=ps[0][:, 0:128], func=AF.Copy), ("ps0",), ("NegSelT",))
                def attn_branch(kbs, kT_, kkey, Vt, vkey, gate_col, use_sel, ring=None):
                    sl = (lambda kb_: kb_ % ring) if ring else (lambda kb_: kb_)
                    nk = len(kbs)
                    chunks = [kbs[c_:c_ + 2] for c_ in range(0, nk, 2)]
                    done = 0
                    for n_, ch in enumerate(chunks):
                        pS = psS[n_ % 2]
                        sk_ = "ps%d" % (n_ % 2)
                        for u, kb in enumerate(ch):
                            extra = []
                            if use_sel:
                                extra.append((Eb[:, kb * 128:(kb + 1) * 128], NegSelT[:], ("Eb", "NegSelT")))
                            if kb == i:
                                extra.append((identb[:], negm[:, 0, :], ("identb", "negm")))
                            if (not use_sel) and kb == i - 4:
                                extra.append((identb[:], negm[:, 1, :], ("identb", "negm")))
                            P.op(TE, lambda e, kb=kb, pS=pS, u=u, ne=len(extra): e.matmul(pS[:, u * 256:(u + 1) * 256], lhsT=kT_[:, sl(kb) * 128:(sl(kb) + 1) * 128], rhs=qropT[:], start=True, stop=(ne == 0)),
                                 ("%s%d" % (kkey, sl(kb)), "qropT"), (sk_,))
                            for xi, (l_, r_, keys) in enumerate(extra):
                                P.op(TE, lambda e, l_=l_, r_=r_, pS=pS, u=u, last=(xi == len(extra) - 1): e.matmul(pS[:, u * 256:(u + 1) * 256], lhsT=l_, rhs=r_, start=False, stop=last), keys, (sk_,))
                        w_ = 256 * len(ch)
                        P.op(S, lambda e, pS=pS, n_=n_, w_=w_: e.activation(out=PT[:, n_ % 2, 0:w_], in_=pS[:, 0:w_], func=AF.Exp, scale=0.125), (sk_,), ("PT%d" % (n_ % 2),))
                        for u, kb in enumerate(ch):
                            for h in range(2):
                                P.op(TE, lambda e, h=h, kb=kb, n_=n_, u=u, first=(done == 0), last=(done == nk - 1): e.matmul(psO[h][:, 0:65], lhsT=PT[:, n_ % 2, u * 256 + h * 128:u * 256 + (h + 1) * 128], rhs=Vt[:, sl(kb), :], start=first, stop=last),
                                     ("PT%d" % (n_ % 2), "%s%d" % (vkey, sl(kb))), ("ps%d" % (2 + h),))
                            done += 1
                        if H3_TEST and use_sel and n_ == 0 and nk > 2:
                            P.op(TE, lambda e: e.transpose(out=ps5[:, 0:128], in_=identb[:], identity=identb[:]), ("identb",), ("ps5",))
                    for h in range(2):
                        pk = "ps%d" % (2 + h)
                        P.op(V, lambda e, h=h: e.reciprocal(out=rr[:, 8 + h:9 + h], in_=psO[h][:, 64:65]), (pk,), ("rs%d" % h,))
                        P.op(V, lambda e, h=h: e.tensor_tensor(out=rr[:, 8 + h:9 + h], in0=rr[:, 8 + h:9 + h], in1=gsig[:, 3 * h + gate_col:3 * h + gate_col + 1], op=ALU.mult), ("rs%d" % h, "gsig"), ("rs%d" % h,))
                        P.op(V, lambda e, h=h: e.scalar_tensor_tensor(out=ynacc[:, h, :], in0=psO[h][:, 0:64], scalar=rr[:, 8 + h:9 + h], in1=ynacc[:, h, :], op0=ALU.mult, op1=ALU.add),
                             (pk, "rs%d" % h, "ynacc%d" % h), ("ynacc%d" % h,))

                marks.append(len(P.cap) if P.cap is not None else 0)
                attn_branch(list(range(i + 1)), ksT, "ksT", Vs, "Vs", 1, True)
                attn_branch(list(range(max(0, i - 4), i + 1)), kwT, "kwT", Vw, "Vw", 2, False, ring=8)
                P.op(G, lambda e, p=p: e.tensor_tensor(out=ybt[:, p, :], in0=ynacc[:].rearrange("p h d -> p (h d)"), in1=nzg[:, p, :], op=ALU.mult), ("ynacc0", "ynacc1", "nzg%d" % p), ("ybt%d" % p,))
                P.dma(SY, lambda e, p=p, t0=t0: e.dma_start(out=yb_d[t0:t0 + 128, :], in_=ybt[:, p, :]), "yb%d" % p, ("ybt%d" % p,), ("yb_out",))

        marks = []
        emit_A(0)
        for i in range(NT):
            if MERGE_MODE == 0:
                emit_B(i)
                if do_nsa:
                    emit_C(i)
                if i + 1 < NT:
                    emit_A(i + 1)
                continue
            P.cap = []
            emit_B(i)
            LB = P.cap
            P.cap = []
            if i + 1 < NT:
                emit_A(i + 1)
            LA = P.cap
            P.cap = []
            del marks[:]
            emit_C(i)
            LY = P.cap
            P.cap = None
            mk = marks[0]
            for kind, args in LY[:mk]:
                (P.op if kind == "op" else P.dma)(*args)
            P.replay_merged(LB, LA, LY[mk:])

    for j in range(4):
        wfm_d, wtm_d, cw_d, cb_d, misc_d, mhg_d = per[j]
        run_pass(wfm_d, wtm_d, cw_d, cb_d, misc_d, mhg_d, ya_s[:, 256 * j:256 * (j + 1)], yb_s[:, 128 * j:128 * (j + 1)])
    P.barrier()
    P.emit(es)
    es.close()
    P2 = Prog(nc)
    es2 = ExitStack()
    keys = emit_tail(nc, P2, es2, NT, x_d, lambda i: ya_s[i * 128:(i + 1) * 128, :], lambda i: yb_s[i * 128:(i + 1) * 128, :],
                     wg_d, wa_d, wb_d, wo_d, ng_d, fg_d, c128_d, out_d)
    P2.final_wait("sync", keys)
    P2.emit(es2)
    es2.close()
    return nc


def fused_inputs(inp, b, T):
    m0 = mixer_inputs(inp, b, 0, T)
    m = {k: m0[k] for k in ("x", "pos", "ng", "c128", "cw1", "cw2", "cpos", "raug", "E", "selc", "negm")}
    for j in range(4):
        mj = mixer_inputs(inp, b, j, T) if j else m0
        for k in ("wfm", "wtm", "convw", "convb", "misc", "mhg"):
            m["%s_%d" % (k, j)] = mj[k]
    w = inp["w_in"][0]
    m["wg"] = np.ascontiguousarray(w[:, 8992 - 2048:])
    m["wa"] = np.ascontiguousarray(inp["w_branch_a"][0])
    m["wb"] = np.ascontiguousarray(inp["w_branch_b"][0])
    m["wo"] = np.ascontiguousarray(inp["w_out"][0])
    m["fg"] = np.ascontiguousarray(np.broadcast_to(inp["final_norm_g"][None, :], (128, 1024)))
    return m


def kernel_fused(**inputs):
    inp = {k: np.asarray(v) for k, v in inputs.items()}
    T = inp["x"].shape[1]
    if ("fused", T) not in _CACHE:
        _CACHE[("fused", T)] = build_fused(T)
    nc = _CACHE[("fused", T)]
    per_b = [fused_inputs(inp, b, T) for b in range(2)]
    maps = [per_b[c // 4] for c in range(8)]
    res = run_bass_kernel_spmd(nc, maps, core_ids=list(range(8)))
    TQ = T // 4
    out = np.stack([np.concatenate([np.asarray(res.results[4 * b + q]["out"])[q * TQ:(q + 1) * TQ] for q in range(4)], axis=0) for b in range(2)], axis=0)
    return out.astype(np.float32)


def kernel(**inputs):
    return kernel_fused(**inputs)
```

```python
import math
from contextlib import ExitStack

import numpy as np
import concourse.bass as bass
import concourse.mybir as mybir
from concourse.bass_utils import run_bass_kernel_spmd

F32 = mybir.dt.float32
BF16 = mybir.dt.bfloat16
I32 = mybir.dt.int32
AF = mybir.ActivationFunctionType
ALU = mybir.AluOpType
AX = mybir.AxisListType

D = 1024
NEGB = -1.0e30
MASKNEG = -16384.0
LN16 = math.log(16.0)
EPS = 1e-6
ENGS = ("tensor", "vector", "scalar", "gpsimd", "sync")
SEM_ROLL = 12000
import os
SEQ_MERGE = False
MERGE_MODE = int(os.environ.get('MERGE_MODE', '3'))
H3_TEST = 0
TAIL_MERGE = bool(int(os.environ.get('TAIL_MERGE', '1')))


class Prog:
    def __init__(self, nc):
        self.nc = nc
        self.q = {e: [] for e in ENGS}
        self.cnt = {e: 0 for e in ENGS}
        self.gen = {e: 0 for e in ENGS}
        self.last_w = {}
        self.readers = {}
        self.seen = {e: {} for e in ENGS}
        self.dma_cnt = {}
        self.semnames = []
        self.cap = None

    def _deps(self, eng, reads, writes):
        deps = []
        for r in reads:
            if r in self.last_w:
                deps.append(self.last_w[r])
        for w in writes:
            if w in self.last_w:
                deps.append(self.last_w[w])
            deps += self.readers.get(w, [])
        waits = {}
        for sk, v in deps:
            if v > waits.get(sk, 0):
                waits[sk] = v
        out = []
        for sk, v in waits.items():
            if self.seen[eng].get(sk, 0) >= v:
                continue
            self.seen[eng][sk] = v
            out.append((sk, v))
        return out

    def _commit(self, token, reads, writes):
        for r in reads:
            self.readers.setdefault(r, []).append(token)
        for w in writes:
            self.last_w[w] = token
            self.readers[w] = []

    def _semkey(self, name):
        if name not in self.semnames:
            self.semnames.append(name)
        return name

    def replay_merged(self, *lists):
        lists = [l for l in lists if l]
        pos = [0] * len(lists)
        while True:
            best, bf = -1, 2.0
            for j, l in enumerate(lists):
                if pos[j] < len(l):
                    f = pos[j] / float(len(l))
                    if f < bf:
                        best, bf = j, f
            if best < 0:
                break
            kind, args = lists[best][pos[best]]
            pos[best] += 1
            (self.op if kind == "op" else self.dma)(*args)

    def op(self, eng, fn, reads=(), writes=()):
        if self.cap is not None:
            self.cap.append(("op", (eng, fn, reads, writes)))
            return
        waits = self._deps(eng, reads, writes)
        self.cnt[eng] += 1
        idx = self.cnt[eng]
        if eng == "tensor":
            self.seen[eng][eng] = idx
        self.q[eng].append((fn, waits, eng, 1, idx))
        self._commit((eng, idx), reads, writes)

    def dma(self, eng, fn, sem, reads=(), writes=()):
        if self.cap is not None:
            self.cap.append(("dma", (eng, fn, sem, reads, writes)))
            return
        waits = self._deps(eng, reads, writes)
        sk = self._semkey("d_" + sem)
        self.dma_cnt[sk] = self.dma_cnt.get(sk, 0) + 16
        token = (sk, self.dma_cnt[sk])
        self.q[eng].append((fn, waits, sk, 16, 0))
        self._commit(token, reads, writes)

    def final_wait(self, eng, keys):
        waits = self._deps(eng, keys, ())
        have = dict(waits)
        for sk, cnt in self.dma_cnt.items():
            if have.get(sk, 0) < cnt:
                have[sk] = cnt
        self.q[eng].append((None, list(have.items()), None, 0, 0))

    def barrier(self):
        for eng in ENGS:
            waits = [(e2, self.cnt[e2]) for e2 in ENGS if e2 != eng and self.cnt[e2] > 0]
            waits += list(self.dma_cnt.items())
            self.q[eng].append((None, waits, None, 0, 0))

    def plan(self):
        ms = {e: set() for e in ENGS}
        for e in ENGS:
            for fn, waits, sk, inc, idx in self.q[e]:
                for wk, v in waits:
                    if wk in ms:
                        ms[wk].add(v)
        rank = {}
        for e in ENGS:
            rank[e] = {idx: r for r, idx in enumerate(sorted(ms[e]))}
        return rank

    def emit(self, es):
        nc = self.nc
        rank = self.plan()
        names = list(self.semnames)
        for e in ENGS:
            ngen = (len(rank[e]) + SEM_ROLL - 1) // SEM_ROLL
            for g in range(ngen):
                names.append("e_%s_%d" % (e, g))
        sems = {}
        for name in names:
            sems[name] = es.enter_context(nc.semaphore(name))
        block = es.enter_context(nc.Block())
        q = self.q

        def esem(eng_name, idx):
            r = rank[eng_name][idx]
            return sems["e_%s_%d" % (eng_name, r // SEM_ROLL)], r % SEM_ROLL + 1

        def run(eng_name, e):
            for fn, waits, sk, inc, idx in q[eng_name]:
                for wk, v in waits:
                    if wk in rank:
                        sm_, val = esem(wk, v)
                        e.wait_ge(sm_, val)
                    else:
                        e.wait_ge(sems[wk], v)
                if fn is not None:
                    ins = fn(e)
                    if sk in rank:
                        if idx in rank[sk]:
                            ins.then_inc(esem(sk, idx)[0], 1)
                    else:
                        ins.then_inc(sems[sk], inc)

        @block.tensor
        def _(e):
            run("tensor", e)

        @block.vector
        def _(e):
            run("vector", e)

        @block.scalar
        def _(e):
            run("scalar", e)

        @block.gpsimd
        def _(e):
            run("gpsimd", e)

        @block.sync
        def _(e):
            run("sync", e)


NFM = 640
NTM = 1416
TMA, TMB, TMC = 512, 512, 392
C_QOWN, C_KS, C_KW, C_VS, C_VW, C_I, C_F, C_G = 0, 128, 192, 256, 320, 384, 385, 386
B_Z, B_QOTH, B_NZ = 0, 256, 384


def build_mixer(T, do_nsa=True):
    NT = T // 128
    nc = bass.Bass("TRN2", target_bir_lowering=False)
    P = Prog(nc)
    es = ExitStack()

    def din(name, shape, dt=F32):
        return nc.dram_tensor(name, list(shape), dt, kind="ExternalInput").ap()

    x_d = din("x", [T, D])
    pos_d = din("pos", [128, NT], I32)
    wfm_d = din("wfm", [D, NFM])
    wtm_d = din("wtm", [D, NTM])
    ng_d = din("ng", [128, 8])
    cw_d = din("convw", [128, 4, 4])
    cb_d = din("convb", [128, 4])
    misc_d = din("misc", [128, 16])
    mhg_d = din("mhg", [128, 256])
    c128_d = din("c128", [128, 6, 128])
    ya_d = nc.dram_tensor("ya", [T, 256], BF16, kind="ExternalOutput").ap()
    yb_d = nc.dram_tensor("yb", [T, 128], BF16, kind="ExternalOutput").ap()
    if do_nsa:
        w1_d = din("cw1", [64, 2, 32, 128])
        w2_d = din("cw2", [128, 2, 64])
        cpos_d = din("cpos", [64, 2, 32])
        raug_d = din("raug", [128, 4, 193])
        E_d = din("E", [128, T])
        sel_d = din("selc", [128, 4, 256])
        nm_d = din("negm", [128, 2, 256])

    def sb(name, shape, dt=F32):
        return es.enter_context(nc.sbuf_tensor(name, list(shape), dt))

    def pst(name, shape, dt=F32):
        return es.enter_context(nc.psum_tensor(name, list(shape), dt))

    Wfm = sb("Wfm", [128, 8, NFM], BF16)
    Wtm = sb("Wtm", [128, 8, NTM], BF16)
    stg = sb("stg", [128, 2, NTM], F32)
    ngs = sb("ngs", [128, 8])
    cws = sb("cws", [128, 4, 4])
    cbs = sb("cbs", [128, 4])
    misc = sb("miscs", [128, 16])
    mhg = sb("mhgs", [128, 256])
    c128 = sb("c128s", [128, 6, 128])
    identb = sb("identb", [128, 128], BF16)
    ident = c128[:, 0, :]
    Utri = c128[:, 1, :]
    ones = c128[:, 2, :]
    cnegB = c128[:, 3, :]
    maskT01 = c128[:, 4, :]
    xt = sb("xt", [128, 2, D])
    junk = sb("junk", [128, D], BF16)
    hn = sb("hn", [128, 2, D], BF16)
    hT = sb("hT", [128, 2, 8, 128], BF16)
    sm = sb("sm", [128, 2, 32])
    convbuf = sb("convbuf", [128, 4, 131])
    cacc = sb("cacc", [128, 4, 128])
    qkT = sb("qkT", [128, 2, 4, 128], BF16)
    Vaug = sb("Vaug", [128, 2, 257], BF16)
    og = sb("og", [128, 2, 256])
    zg = sb("zg", [128, 2, 256])
    G2 = sb("G2", [128, 2, 2])
    dbg = sb("dbg", [128, 2, 128])
    Bm = sb("Bm", [128, 128])
    DT = sb("DT", [128, 128])
    DTm = sb("DTm", [128, 128], BF16)
    Wm = sb("Wm", [128, 128])
    qwT = sb("qwT", [128, 2, 128], BF16)
    smT = sb("smT", [128, 128], BF16)
    hbuf = sb("hbuf", [128, 256])
    hbuf2 = sb("hbuf2", [128, 256])
    bnst = sb("bnst", [128, 8])
    kwt = sb("kwt", [128, 256], BF16)
    ktok = sb("ktok", [128, 2, 256], BF16)
    Cst = sb("Cst", [128, 2, 257])
    Cb = sb("Cb", [128, 2, 257], BF16)
    mcol = sb("mcol", [128, 1])
    epsc = sb("epsc", [128, 1])
    negpi = sb("negpi", [128, 1])
    tmC = sb("tmC", [128, 2, TMC])
    yat = sb("yat", [128, 2, 256], BF16)
    ybt = sb("ybt", [128, 2, 128], BF16)

    if do_nsa:
        W1b = sb("W1b", [64, 2, 32, 128], BF16)
        W2b = sb("W2b", [128, 2, 64], BF16)
        posT = sb("posT", [64, 2, 32], BF16)
        biash = sb("biash", [128, 2])
        cpre = sb("cpre", [64, 2, 2, 144], BF16)
        kcT = sb("kcT", [64, 512], BF16)
        GV = sb("GV", [128, 512], BF16)
        Raug = sb("Raug", [128, 4, 193], BF16)
        ksT = sb("ksT", [64, T], BF16)
        kwT = sb("kwT", [64, 8 * 128], BF16)
        Vs = sb("Vs", [128, NT, 65], BF16)
        Vw = sb("Vw", [128, 8, 65], BF16)
        Eb = sb("Eb", [128, T], BF16)
        selc = sb("selcs", [128, 4, 256])
        negm = sb("negms", [128, 2, 256], BF16)
        posi = sb("posi", [128, NT], I32)
        posf = sb("posf", [128, NT])
        ang = sb("ang", [128, NT, 8])
        ang2 = sb("ang2", [128, NT, 8])
        ang3 = sb("ang3", [128, NT, 8])
        angi = sb("angi", [128, NT, 8], I32)
        cossin = sb("cossin", [128, NT, 16])
        qtok = sb("qtok", [128, 2, 4, 64])
        rp = sb("rp", [128, 4, 64])
        rtmp = sb("rtmp", [128, 4, 4, 8])
        qrawT = sb("qrawT", [64, 512], BF16)
        qropT = sb("qropT", [64, 256], BF16)
        PcT = sb("PcT", [128, 4, 512], BF16)
        PT = sb("PT", [128, 2, 512], BF16)
        cmask = sb("cmask", [128, 128], BF16)
        hx = sb("hx", [128, 4, 16])
        gl = sb("gl", [128, 16], BF16)
        imp = sb("imp", [128, 128])
        sc = sb("sc", [128, 2, 128])
        m8 = sb("m8", [128, 16])
        sel01 = sb("sel01", [128, 128])
        NegSel = sb("NegSel", [128, 128])
        NegSelT = sb("NegSelT", [128, 256], BF16)
        gsig = sb("gsig", [128, 8])
        nzg = sb("nzg", [128, 2, 128])
        ynacc = sb("ynacc", [128, 2, 64])
        rr = sb("rr", [128, 16])

    ps = [pst("ps%d" % i, [128, 512]) for i in range(5)]
    ps5 = pst("ps5", [128, 1024], BF16)
    ps6 = pst("ps6", [128, 512])
    ps7 = pst("ps7", [128, 512])

    V, S, G, TE, SY = "vector", "scalar", "gpsimd", "tensor", "sync"

    P.dma(SY, lambda e: e.dma_start(out=ngs[:], in_=ng_d), "c0", (), ("ngs",))
    P.dma(SY, lambda e: e.dma_start(out=cws[:], in_=cw_d), "c1", (), ("cws",))
    P.dma(SY, lambda e: e.dma_start(out=cbs[:], in_=cb_d), "c2", (), ("cbs",))
    P.dma(SY, lambda e: e.dma_start(out=misc[:], in_=misc_d), "c3", (), ("misc",))
    P.dma(SY, lambda e: e.dma_start(out=mhg[:], in_=mhg_d), "c4", (), ("mhg",))
    P.dma(SY, lambda e: e.dma_start(out=c128[:], in_=c128_d), "c5", (), ("c128",))
    P.dma(G, lambda e: e.dma_start(out=identb[:], in_=c128_d[:, 0, :]), "c6", (), ("identb",))
    for k in range(8):
        sp_ = k % 2
        for (wd, Wt, ncol, nm) in ((wfm_d, Wfm, NFM, "f"), (wtm_d, Wtm, NTM, "t")):
            P.dma(SY, lambda e, wd=wd, ncol=ncol, k=k, sp_=sp_: e.dma_start(out=stg[:, sp_, 0:ncol], in_=wd[k * 128:(k + 1) * 128, :]),
                  "w%d" % sp_, (), ("stg%d" % sp_,))
            P.op(V, lambda e, Wt=Wt, ncol=ncol, k=k, sp_=sp_: e.tensor_scalar(out=Wt[:, k, :], in0=stg[:, sp_, 0:ncol], scalar1=ngs[:, k:k + 1], scalar2=None, op0=ALU.mult),
                 ("stg%d" % sp_, "ngs"), ("W",))
    P.op(V, lambda e: e.memset(epsc[:], EPS), (), ("epsc",))
    P.op(V, lambda e: e.memset(negpi[:], -math.pi), (), ("negpi",))
    P.op(V, lambda e: e.memset(Cst[:], 0.0), (), ("Cst",))
    P.op(V, lambda e: e.memset(Cb[:], 0.0), (), ("Cb",))
    P.op(V, lambda e: e.memset(mcol[:], 0.0), (), ("m",))
    P.op(V, lambda e: e.memset(convbuf[:], 0.0), (), ("convbuf",))
    P.op(V, lambda e: e.memset(Vaug[:], 1.0), (), ("Vaug0", "Vaug1"))

    if do_nsa:
        P.dma(G, lambda e: e.dma_start(out=W1b[:], in_=w1_d), "n0", (), ("W1b",))
        P.dma(G, lambda e: e.dma_start(out=W2b[:], in_=w2_d), "n1", (), ("W2b",))
        P.dma(G, lambda e: e.dma_start(out=posT[:], in_=cpos_d), "n2", (), ("posT",))
        P.dma(G, lambda e: e.dma_start(out=Raug[:], in_=raug_d), "n3", (), ("Raug",))
        P.dma(G, lambda e: e.dma_start(out=negm[:], in_=nm_d), "n4", (), ("negm",))
        P.dma(SY, lambda e: e.dma_start(out=selc[:], in_=sel_d), "n5", (), ("selc",))
        P.dma(SY, lambda e: e.dma_start(out=posi[:], in_=pos_d), "n6", (), ("posi",))
        for c0 in range(0, T, 2048):
            c1 = min(T, c0 + 2048)
            P.dma(G, lambda e, c0=c0, c1=c1: e.dma_start(out=Eb[:, c0:c1], in_=E_d[:, c0:c1]), "n7", (), ("Eb",))
        P.op(V, lambda e: e.memset(cpre[:], 0.0), (), ("cpre0", "cpre1"))
        P.op(V, lambda e: e.memset(kcT[:], 0.0), (), ("kcT",))
        P.op(V, lambda e: e.memset(GV[:], 0.0), (), ("GV",))
        P.op(G, lambda e: e.memset(Vs[:], 1.0), (), tuple("Vs%d" % t_ for t_ in range(NT)))
        P.op(G, lambda e: e.memset(Vw[:], 1.0), (), tuple("Vw%d" % t_ for t_ in range(8)))
        P.op(V, lambda e: e.tensor_copy(out=posf[:], in_=posi[:]), ("posi",), ("posf",))
        for f in range(8):
            P.op(V, lambda e, f=f: e.tensor_scalar(out=ang[:, :, f], in0=posf[:], scalar1=misc[:, 8 + f:9 + f], scalar2=None, op0=ALU.mult), ("posf", "misc"), ("ang",))
        for (off_, c0_) in ((0.75, 0), (0.5, 8)):
            P.op(V, lambda e, off_=off_: e.tensor_scalar(out=ang2[:], in0=ang[:], scalar1=1.0 / (2.0 * math.pi), scalar2=off_, op0=ALU.mult, op1=ALU.add), ("ang", "cossin"), ("ang2",))
            P.op(V, lambda e: e.tensor_copy(out=angi[:], in_=ang2[:]), ("ang2",), ("angi",))
            P.op(V, lambda e: e.tensor_copy(out=ang3[:], in_=angi[:]), ("angi",), ("ang3",))
            P.op(V, lambda e: e.tensor_tensor(out=ang2[:], in0=ang2[:], in1=ang3[:], op=ALU.subtract), ("ang2", "ang3"), ("ang2",))
            P.op(V, lambda e: e.tensor_scalar(out=ang3[:], in0=ang2[:], scalar1=0.0, scalar2=None, op0=ALU.is_lt), ("ang2",), ("ang3",))
            P.op(V, lambda e: e.tensor_tensor(out=ang2[:], in0=ang2[:], in1=ang3[:], op=ALU.add), ("ang2", "ang3"), ("ang2",))
            P.op(S, lambda e, c0_=c0_: e.activation(out=cossin[:, :, c0_:c0_ + 8], in_=ang2[:], func=AF.Sin, scale=2.0 * math.pi, bias=negpi[:]), ("ang2", "negpi"), ("cossin",))
        for kv in range(2):
            for l in range(32):
                P.op(TE, lambda e, kv=kv, l=l: e.matmul(ps[4][:, 300 + kv:301 + kv], lhsT=W1b[:, kv, l, :], rhs=posT[:, kv, l:l + 1], start=(l == 0), stop=(l == 31)),
                     ("W1b", "posT"), ("ps4",))
        P.op(V, lambda e: e.tensor_copy(out=biash[:, 0:2], in_=ps[4][:, 300:302]), ("ps4",), ("biash",))

    def emit_A(i):
        p = i % 2
        t0 = i * 128
        X, HN, HT, SM = "xt%d" % p, "hn%d" % p, "hT%d" % p, "sm%d" % p
        smp = sm[:, p, :]
        P.dma(SY, lambda e, p=p, t0=t0: e.dma_start(out=xt[:, p, :], in_=x_d[t0:t0 + 128, :]), "x%d" % p, (), (X,))
        P.op(S, lambda e, p=p, smp=smp: e.activation(out=junk[:], in_=xt[:, p, :], func=AF.Square, accum_out=smp[:, 0:1]), (X,), ("junk", SM + "a"))
        P.op(S, lambda e, smp=smp: e.activation(out=smp[:, 1:2], in_=smp[:, 0:1], func=AF.Sqrt, scale=1.0 / D, bias=epsc[:]), (SM + "a", "epsc"), (SM + "b",))
        P.op(V, lambda e, smp=smp: e.reciprocal(out=smp[:, 2:3], in_=smp[:, 1:2]), (SM + "b",), (SM + "c",))
        P.op(S, lambda e, p=p, smp=smp: e.activation(out=hn[:, p, :], in_=xt[:, p, :], func=AF.Copy, scale=smp[:, 2:3]), (X, SM + "c"), (HN,))
        for k in range(8):
            P.op(TE, lambda e, p=p, k=k: e.transpose(out=ps5[:, k * 128:(k + 1) * 128], in_=hn[:, p, k * 128:(k + 1) * 128], identity=identb[:]),
                 (HN, "identb"), ("ps5",))
        P.op(V, lambda e, p=p: e.tensor_copy(out=hT[:, p, :, :], in_=ps5[:].rearrange("p (k t) -> p k t", k=8)), ("ps5",), (HT,))
        for grp in range(4):
            for k in range(8):
                P.op(TE, lambda e, p=p, k=k, grp=grp: e.matmul(ps7[:, grp * 128:(grp + 1) * 128], lhsT=Wfm[:, k, grp * 128:(grp + 1) * 128], rhs=hT[:, p, k, :], start=(k == 0), stop=(k == 7)),
                     (HT, "W"), ("ps7",))
        P.op(V, lambda e: e.tensor_copy(out=convbuf[:, :, 0:3], in_=convbuf[:, :, 128:131]), ("convbuf",), ("convbuf",))
        P.op(S, lambda e: e.activation(out=convbuf[:, :, 3:131], in_=ps7[:].rearrange("p (g t) -> p g t", g=4), func=AF.Copy), ("ps7",), ("convbuf",))
        if do_nsa:
            for c in range(2):
                for k in range(8):
                    P.op(TE, lambda e, p=p, k=k, c=c: e.matmul(ps7[0:64, c * 128:(c + 1) * 128], lhsT=Wfm[:, k, 512 + c * 64:512 + (c + 1) * 64], rhs=hT[:, p, k, :], start=(k == 0), stop=(k == 7)),
                         (HT, "W"), ("ps7",))
            P.op(V, lambda e, p=p: e.tensor_copy(out=cpre[:, p, :, 0:16], in_=cpre[:, 1 - p, :, 128:144]), ("cpre%d" % (1 - p),), ("cpre%d" % p,))
            P.op(S, lambda e, p=p: e.activation(out=cpre[:, p, :, 16:144], in_=ps7[0:64, 0:256].rearrange("p (c t) -> p c t", c=2), func=AF.Copy), ("ps7",), ("cpre%d" % p,))
        for grp in range(4):
            P.op(V, lambda e, grp=grp: e.tensor_scalar(out=cacc[:, grp, :], in0=convbuf[:, grp, 0:128], scalar1=cws[:, grp, 0:1], scalar2=cbs[:, grp:grp + 1], op0=ALU.mult, op1=ALU.add),
                 ("convbuf", "cws", "cbs"), ("cacc%d" % grp,))
            for j in range(1, 4):
                P.op(V, lambda e, grp=grp, j=j: e.scalar_tensor_tensor(out=cacc[:, grp, :], in0=convbuf[:, grp, j:j + 128], scalar=cws[:, grp, j:j + 1], in1=cacc[:, grp, :], op0=ALU.mult, op1=ALU.add),
                     ("convbuf", "cws", "cacc%d" % grp), ("cacc%d" % grp,))
        P.op(S, lambda e, p=p: e.activation(out=qkT[:, p, :, :], in_=cacc[:], func=AF.Silu), tuple("cacc%d" % g_ for g_ in range(4)), ("qkT%d" % p,))
        for c in range(2):
            P.op(TE, lambda e, p=p, c=c: e.transpose(out=ps5[:, c * 128:(c + 1) * 128], in_=qkT[:, p, 2 + c, :], identity=identb[:]), ("qkT%d" % p, "identb"), ("ps5",))
        P.op(V, lambda e, p=p: e.tensor_copy(out=ktok[:, p, :], in_=ps5[:, 0:256]), ("ps5",), ("ktok%d" % p,))
        for (pb, pk_, c0, n) in ((ps7, "ps7", 0, TMA), (ps7, "ps7", TMA, TMB), (ps7, "ps7", TMA + TMB, TMC)):
            for k in range(8):
                P.op(TE, lambda e, p=p, k=k, pb=pb, c0=c0, n=n: e.matmul(pb[:, 0:n], lhsT=hT[:, p, k, :], rhs=Wtm[:, k, c0:c0 + n], start=(k == 0), stop=(k == 7)),
                     (HT, "W"), (pk_,))
            if c0 == 0:
                P.op(V, lambda e, p=p: e.tensor_copy(out=Vaug[:, p, 0:256], in_=ps7[:, 0:256]), ("ps7",), ("Vaug%d" % p,))
                P.op(S, lambda e, p=p: e.activation(out=og[:, p, :], in_=ps7[:, 256:512], func=AF.Sigmoid), ("ps7",), ("og%d" % p,))
            elif c0 == TMA:
                P.op(S, lambda e, p=p: e.activation(out=zg[:, p, :], in_=ps7[:, 0:256], func=AF.Silu), ("ps7",), ("zg%d" % p,))
                P.op(G, lambda e, p=p: e.tensor_tensor(out=og[:, p, :], in0=og[:, p, :], in1=zg[:, p, :], op=ALU.mult), ("og%d" % p, "zg%d" % p), ("og%d" % p,))
                if do_nsa:
                    P.op(V, lambda e, p=p: e.tensor_copy(out=qtok[:, p, 2:4, :], in_=ps7[:, B_QOTH:B_QOTH + 128].rearrange("p (h d) -> p h d", h=2)), ("ps7",), ("qtok%d" % p,))
                    P.op(S, lambda e, p=p: e.activation(out=nzg[:, p, :], in_=ps7[:, B_NZ:B_NZ + 128], func=AF.Silu), ("ps7",), ("nzg%d" % p,))
            else:
                P.op(V, lambda e, p=p: e.tensor_copy(out=tmC[:, p, :], in_=ps7[:, 0:TMC]), ("ps7",), ("tmC%d" % p,))

    def emit_B(i):
        p = i % 2
        t0 = i * 128
        SM = "sm%d" % p
        smp = sm[:, p, :]
        TC = "tmC%d" % p
        P.op(V, lambda e, smp=smp, p=p: e.tensor_scalar(out=smp[:, 17:18], in0=tmC[:, p, C_F:C_F + 1], scalar1=misc[:, 0:1], scalar2=None, op0=ALU.add), (TC, "misc"), (SM + "d0",))
        P.op(S, lambda e, smp=smp: e.activation(out=smp[:, 3:4], in_=smp[:, 17:18], func=AF.Exp, scale=-1.0), (SM + "d0",), (SM + "d",))
        P.op(S, lambda e, p=p, smp=smp: e.activation(out=G2[:, p, 0:1], in_=smp[:, 3:4], func=AF.Ln, scale=1.0, bias=1.0), (SM + "d",), ("G2%d" % p,))
        P.op(V, lambda e, p=p: e.tensor_scalar(out=G2[:, p, 1:2], in0=tmC[:, p, C_I:C_I + 1], scalar1=misc[:, 1:2], scalar2=None, op0=ALU.add), (TC, "misc"), ("G2%d" % p,))
        P.op(TE, lambda e, p=p: e.matmul(ps6[:, 128:130], lhsT=Utri, rhs=G2[:, p, :], start=True, stop=True), ("G2%d" % p, "c128"), ("ps6",))
        P.op(TE, lambda e, p=p: e.matmul(ps6[:, 130:132], lhsT=ones, rhs=G2[:, p, :], start=True, stop=True), ("G2%d" % p, "c128"), ("ps6",))
        P.op(V, lambda e, smp=smp: e.tensor_copy(out=smp[:, 20:24], in_=ps6[:, 128:132]), ("ps6",), (SM + "cum",))
        P.op(V, lambda e, p=p, smp=smp: e.tensor_tensor(out=smp[:, 4:5], in0=smp[:, 20:21], in1=G2[:, p, 1:2], op=ALU.add), (SM + "cum", "G2%d" % p), (SM + "e",))
        P.op(V, lambda e, smp=smp: e.tensor_scalar(out=smp[:, 5:6], in0=smp[:, 4:5], scalar1=-LN16, scalar2=None, op0=ALU.add), (SM + "e",), (SM + "f",))
        P.op(V, lambda e, smp=smp: e.tensor_scalar(out=dbg[:, 0, :], in0=ident, scalar1=smp[:, 4:5], scalar2=None, op0=ALU.mult), (SM + "e", "c128"), ("dbg0",))
        P.op(TE, lambda e: e.matmul(ps6[:, 0:128], lhsT=ones, rhs=dbg[:, 0, :], start=True, stop=True), ("dbg0", "c128"), ("ps6",))
        P.op(V, lambda e: e.tensor_tensor(out=Bm[:], in0=ps6[:, 0:128], in1=cnegB, op=ALU.add), ("ps6", "c128"), ("Bm",))
        P.op(V, lambda e, smp=smp: e.reduce_max(out=smp[:, 6:7], in_=Bm[:], axis=AX.X), ("Bm",), (SM + "g",))
        P.op(V, lambda e, smp=smp: e.reduce_max(out=smp[:, 7:8], in_=ps6[:, 0:128], axis=AX.X), ("ps6",), (SM + "h",))
        P.op(V, lambda e, smp=smp: e.tensor_tensor(out=smp[:, 8:9], in0=smp[:, 6:7], in1=mcol[:], op=ALU.max), (SM + "g", "m"), (SM + "i",))
        P.op(V, lambda e, smp=smp: e.tensor_tensor(out=smp[:, 9:10], in0=smp[:, 7:8], in1=mcol[:], op=ALU.max), (SM + "h", "m"), (SM + "j",))
        P.op(V, lambda e, smp=smp: e.tensor_scalar(out=dbg[:, 1, :], in0=ident, scalar1=smp[:, 8:9], scalar2=None, op0=ALU.mult), (SM + "i", "c128"), ("dbg1",))
        P.op(TE, lambda e: e.matmul(ps6[:, 0:128], lhsT=ones, rhs=dbg[:, 1, :], start=True, stop=True), ("dbg1", "c128"), ("ps6",))
        P.op(S, lambda e, smp=smp: e.activation(out=DT[:], in_=ps6[:, 0:128], func=AF.Exp, scale=-1.0, bias=smp[:, 5:6]), ("ps6", SM + "f"), ("DT",))
        P.op(G, lambda e: e.tensor_tensor(out=DTm[:], in0=DT[:], in1=maskT01, op=ALU.mult), ("DT", "c128"), ("DTm",))
        P.op(S, lambda e: e.activation(out=Wm[:], in_=ps6[:, 0:128], func=AF.Exp, scale=-1.0, bias=mcol[:]), ("ps6", "m"), ("Wm",))
        for c in range(2):
            P.op(G, lambda e, p=p, c=c: e.tensor_tensor(out=qwT[:, c, :], in0=qkT[:, p, c, :], in1=Wm[:], op=ALU.mult), ("qkT%d" % p, "Wm"), ("qwT",))
        for c in range(2):
            P.op(TE, lambda e, p=p, c=c: e.matmul(ps6[:, 0:128], lhsT=qkT[:, p, 2 + c, :], rhs=qkT[:, p, c, :], start=(c == 0), stop=(c == 1)), ("qkT%d" % p,), ("ps6",))
        P.op(V, lambda e: e.tensor_tensor(out=smT[:], in0=ps6[:, 0:128], in1=DTm[:], op=ALU.mult), ("ps6", "DTm"), ("smT",))
        for c in range(2):
            P.op(TE, lambda e, c=c: e.matmul(ps6[:, 132:389], lhsT=qwT[:, c, :], rhs=Cb[:, c, :], start=(c == 0), stop=False), ("qwT", "Cb"), ("ps6",))
        P.op(TE, lambda e, p=p: e.matmul(ps6[:, 132:389], lhsT=smT[:], rhs=Vaug[:, p, :], start=False, stop=True), ("smT", "Vaug%d" % p), ("ps6",))
        P.op(V, lambda e, smp=smp: e.tensor_tensor(out=smp[:, 10:11], in0=smp[:, 20:21], in1=smp[:, 8:9], op=ALU.subtract), (SM + "cum", SM + "i"), (SM + "k",))
        P.op(S, lambda e, smp=smp: e.activation(out=smp[:, 11:12], in_=smp[:, 10:11], func=AF.Exp), (SM + "k",), (SM + "l",))
        P.op(S, lambda e, smp=smp: e.activation(out=smp[:, 19:20], in_=ps6[:, 388:389], func=AF.Abs), ("ps6",), (SM + "m0",))
        P.op(V, lambda e, smp=smp: e.tensor_tensor(out=smp[:, 12:13], in0=smp[:, 19:20], in1=smp[:, 11:12], op=ALU.max), (SM + "m0", SM + "l"), (SM + "m",))
        P.op(V, lambda e, smp=smp: e.reciprocal(out=smp[:, 13:14], in_=smp[:, 12:13]), (SM + "m",), (SM + "n",))
        P.op(V, lambda e, smp=smp: e.tensor_scalar(out=hbuf[:], in0=ps6[:, 132:388], scalar1=smp[:, 13:14], scalar2=None, op0=ALU.mult), ("ps6", SM + "n"), ("hbuf",))
        P.op(S, lambda e, smp=smp: e.activation(out=smp[:, 15:16], in_=smp[:, 9:10], func=AF.Exp, scale=-1.0, bias=smp[:, 5:6]), (SM + "j", SM + "f"), (SM + "p",))
        P.op(S, lambda e, smp=smp: e.activation(out=smp[:, 16:17], in_=smp[:, 9:10], func=AF.Exp, scale=-1.0, bias=mcol[:]), (SM + "j", "m"), (SM + "q",))
        P.op(V, lambda e, smp=smp, p=p: e.tensor_scalar(out=kwt[:], in0=ktok[:, p, :], scalar1=smp[:, 15:16], scalar2=None, op0=ALU.mult), ("ktok%d" % p, SM + "p"), ("kwt",))
        for c in range(2):
            P.op(TE, lambda e, p=p, c=c: e.matmul(ps6[:, 132:389], lhsT=kwt[:, c * 128:(c + 1) * 128], rhs=Vaug[:, p, :], start=True, stop=True), ("kwt", "Vaug%d" % p), ("ps6",))
            P.op(V, lambda e, c=c, smp=smp: e.scalar_tensor_tensor(out=Cst[:, c, :], in0=Cst[:, c, :], scalar=smp[:, 16:17], in1=ps6[:, 132:389], op0=ALU.mult, op1=ALU.add),
                 ("Cst", SM + "q", "ps6"), ("Cst",))
            P.op(S, lambda e, c=c: e.activation(out=Cb[:, c, :], in_=Cst[:, c, :], func=AF.Copy), ("Cst",), ("Cb",))
        P.op(V, lambda e, smp=smp: e.tensor_tensor(out=mcol[:], in0=smp[:, 9:10], in1=smp[:, 22:23], op=ALU.subtract), (SM + "j", SM + "cum"), ("m",))
        P.op(V, lambda e: e.bn_stats(out=bnst[:, 0:6], in_=hbuf[:]), ("hbuf",), ("bnst",))
        P.op(V, lambda e: e.bn_aggr(out=bnst[:, 6:8], in_=bnst[:, 0:6]), ("bnst",), ("bnst",))
        P.op(S, lambda e, smp=smp: e.activation(out=smp[:, 18:19], in_=bnst[:, 7:8], func=AF.Sqrt, scale=1.0, bias=epsc[:]), ("bnst", "epsc"), (SM + "o0",))
        P.op(V, lambda e, smp=smp: e.reciprocal(out=smp[:, 14:15], in_=smp[:, 18:19]), (SM + "o0",), (SM + "o",))
        P.op(V, lambda e, smp=smp: e.tensor_scalar(out=hbuf2[:], in0=hbuf[:], scalar1=bnst[:, 6:7], scalar2=smp[:, 14:15], op0=ALU.subtract, op1=ALU.mult), ("hbuf", "bnst", SM + "o"), ("hbuf2",))
        P.op(G, lambda e: e.tensor_tensor(out=hbuf2[:], in0=hbuf2[:], in1=mhg[:], op=ALU.mult), ("hbuf2", "mhg"), ("hbuf2",))
        P.op(G, lambda e, p=p: e.tensor_tensor(out=yat[:, p, :], in0=hbuf2[:], in1=og[:, p, :], op=ALU.mult), ("hbuf2", "og%d" % p), ("yat%d" % p,))
        P.dma(SY, lambda e, p=p, t0=t0: e.dma_start(out=ya_d[t0:t0 + 128, :], in_=yat[:, p, :]), "ya%d" % p, ("yat%d" % p,), ("ya_out",))

    def emit_C(i):
        p = i % 2
        t0 = i * 128
        TC = "tmC%d" % p
        if do_nsa:
            psS = (ps[0], ps[1])
            psO = (ps[2], ps[3])
            QT = "qtok%d" % p
            P.op(S, lambda e, p=p: e.activation(out=qtok[:, p, 0:2, :], in_=tmC[:, p, 0:128].rearrange("p (h d) -> p h d", h=2), func=AF.Copy), (TC,), (QT,))
            P.op(S, lambda e, p=p: e.activation(out=rp[:], in_=tmC[:, p, 0:256].rearrange("p (h d) -> p h d", h=4), func=AF.Copy), (TC,), ("rp",))
            src = tmC[:, p, 0:256].rearrange("p (h d) -> p h d", h=4)
            cosb = cossin[:, i, 0:8].unsqueeze(1).to_broadcast([128, 4, 8])
            sinb = cossin[:, i, 8:16].unsqueeze(1).to_broadcast([128, 4, 8])
            P.op(V, lambda e, src=src, cosb=cosb: e.tensor_tensor(out=rtmp[:, 0], in0=src[:, :, 0:8], in1=cosb, op=ALU.mult), (TC, "cossin"), ("rtmp0",))
            P.op(V, lambda e, src=src, sinb=sinb: e.tensor_tensor(out=rtmp[:, 1], in0=src[:, :, 8:16], in1=sinb, op=ALU.mult), (TC, "cossin"), ("rtmp1",))
            P.op(V, lambda e, src=src, cosb=cosb: e.tensor_tensor(out=rtmp[:, 2], in0=src[:, :, 8:16], in1=cosb, op=ALU.mult), (TC, "cossin"), ("rtmp2",))
            P.op(V, lambda e, src=src, sinb=sinb: e.tensor_tensor(out=rtmp[:, 3], in0=src[:, :, 0:8], in1=sinb, op=ALU.mult), (TC, "cossin"), ("rtmp3",))
            P.op(V, lambda e: e.tensor_tensor(out=rp[:, :, 0:8], in0=rtmp[:, 0], in1=rtmp[:, 1], op=ALU.subtract), ("rtmp0", "rtmp1", "rp"), ("rp",))
            P.op(V, lambda e: e.tensor_tensor(out=rp[:, :, 8:16], in0=rtmp[:, 2], in1=rtmp[:, 3], op=ALU.add), ("rtmp2", "rtmp3", "rp"), ("rp",))
            P.op(V, lambda e, i=i, p=p: e.tensor_copy(out=Vs[:, i, 0:64], in_=tmC[:, p, C_VS:C_VS + 64]), (TC,), ("Vs%d" % i,))
            P.op(V, lambda e, i=i, p=p: e.tensor_copy(out=Vw[:, i % 8, 0:64], in_=tmC[:, p, C_VW:C_VW + 64]), (TC,), ("Vw%d" % (i % 8),))
            P.op(V, lambda e, p=p: e.tensor_tensor(out=gsig[:, 0:6], in0=tmC[:, p, C_G:C_G + 6], in1=misc[:, 2:8], op=ALU.add), (TC, "misc"), ("gsig",))
            P.op(S, lambda e: e.activation(out=gsig[:, 0:6], in_=gsig[:, 0:6], func=AF.Sigmoid), ("gsig",), ("gsig",))
            for h in range(4):
                P.op(TE, lambda e, h=h, p=p: e.transpose(out=ps[0][0:64, h * 128:(h + 1) * 128], in_=qtok[:, p, h, :], identity=ident), (QT, "c128"), ("ps0",))
            for h in range(4):
                P.op(TE, lambda e, h=h: e.transpose(out=ps[1][0:64, h * 128:(h + 1) * 128], in_=rp[:, h, :], identity=ident), ("rp", "c128"), ("ps1",))
            P.op(V, lambda e: e.tensor_copy(out=qrawT[:], in_=ps[0][0:64, 0:512]), ("ps0",), ("qrawT",))
            P.op(S, lambda e: e.activation(out=qropT[:], in_=ps[1][0:64, 0:256], func=AF.Copy), ("ps1",), ("qropT",))
            P.op(V, lambda e, t0=t0: e.tensor_copy(out=ksT[:, t0:t0 + 128], in_=ps[1][0:64, 256:384]), ("ps1",), ("ksT%d" % i,))
            P.op(S, lambda e, i=i: e.activation(out=kwT[:, (i % 8) * 128:(i % 8 + 1) * 128], in_=ps[1][0:64, 384:512], func=AF.Copy), ("ps1",), ("kwT%d" % (i % 8),))
            for kv in range(2):
                for l in range(32):
                    P.op(TE, lambda e, kv=kv, l=l, p=p: e.matmul(ps[4][:, 256 + kv * 8:264 + kv * 8], lhsT=W1b[:, kv, l, :], rhs=cpre[:, p, kv, l:l + 113:16], start=(l == 0), stop=(l == 31)),
                         ("W1b", "cpre%d" % p), ("ps4",))
            hid = ps[4][:, 256:272].rearrange("p (k n) -> p k n", k=2)
            P.op(V, lambda e, hid=hid: e.tensor_tensor(out=hx[:, 0, :].rearrange("p (k n) -> p k n", k=2), in0=hid, in1=biash[:].unsqueeze(2).to_broadcast([128, 2, 8]), op=ALU.add), ("ps4", "biash"), ("hx0",))
            P.op(V, lambda e: e.tensor_tensor(out=hx[:, 1, :], in0=hx[:, 0, :], in1=hx[:, 0, :], op=ALU.mult), ("hx0",), ("hx1",))
            P.op(V, lambda e: e.tensor_scalar(out=hx[:, 1, :], in0=hx[:, 1, :], scalar1=0.044715, scalar2=1.0, op0=ALU.mult, op1=ALU.add), ("hx1",), ("hx1",))
            P.op(V, lambda e: e.tensor_tensor(out=hx[:, 2, :], in0=hx[:, 1, :], in1=hx[:, 0, :], op=ALU.mult), ("hx1", "hx0"), ("hx2",))
            P.op(S, lambda e: e.activation(out=hx[:, 3, :], in_=hx[:, 2, :], func=AF.Sigmoid, scale=1.5957691216057308), ("hx2",), ("hx3",))
            P.op(V, lambda e: e.tensor_tensor(out=gl[:], in0=hx[:, 3, :], in1=hx[:, 0, :], op=ALU.mult), ("hx3", "hx0"), ("gl",))
            P.op(TE, lambda e: e.matmul(ps[4][0:64, 320:328], lhsT=W2b[:, 0, :], rhs=gl[:, 0:8], start=True, stop=True), ("W2b", "gl"), ("ps4",))
            P.op(V, lambda e, i=i: e.tensor_copy(out=kcT[:, 8 * i:8 * i + 8], in_=ps[4][0:64, 320:328]), ("ps4",), ("kcT",))
            P.op(V, lambda e, i=i: e.tensor_copy(out=GV[:, 8 * i:8 * i + 8], in_=gl[:, 8:16]), ("gl",), ("GV",))
            if i == 0:
                P.op(V, lambda e: e.memset(GV[:, 0:1], 0.0), (), ("GV",))
            cc_cur = i // 16
            P.op(TE, lambda e, cc_cur=cc_cur: e.matmul(ps[4][:, 336:400], lhsT=GV[:, cc_cur * 128:(cc_cur + 1) * 128], rhs=W2b[:, 1, :], start=True, stop=True), ("GV", "W2b"), ("ps4",))
            P.op(V, lambda e, cc_cur=cc_cur: e.tensor_copy(out=Raug[:, cc_cur, 128:192], in_=ps[4][:, 336:400]), ("ps4",), ("Raug",))
            nch = cc_cur + 1
            for cc in range(nch):
                P.op(TE, lambda e, cc=cc: e.matmul(psS[cc % 2][:, 0:512], lhsT=kcT[:, cc * 128:(cc + 1) * 128], rhs=qrawT[:], start=True, stop=True), ("kcT", "qrawT"), ("ps%d" % (cc % 2),))
                P.op(S, lambda e, cc=cc: e.activation(out=PcT[:, cc, :], in_=psS[cc % 2][:, 0:512], func=AF.Exp, scale=0.125), ("ps%d" % (cc % 2),), ("PcT%d" % cc,))
            delta = 128.0 * (i % 16) - 15.0
            P.op(V, lambda e, delta=delta: e.tensor_scalar(out=cmask[:], in0=c128[:, 5, :], scalar1=delta, scalar2=None, op0=ALU.is_le), ("c128",), ("cmask",))
            P.op(G, lambda e, cc_cur=cc_cur: e.tensor_tensor(out=PcT[:, cc_cur, :].rearrange("p (h t) -> p h t", h=4), in0=PcT[:, cc_cur, :].rearrange("p (h t) -> p h t", h=4),
                                                              in1=cmask[:].unsqueeze(1).to_broadcast([128, 4, 128]), op=ALU.mult), ("PcT%d" % cc_cur, "cmask"), ("PcT%d" % cc_cur,))
            for h in range(4):
                po = psO[h % 2]
                pk = "ps%d" % (2 + h % 2)
                s0 = (h // 2) * 256
                for cc in range(nch):
                    P.op(TE, lambda e, h=h, cc=cc, po=po, s0=s0: e.matmul(po[:, s0:s0 + 193], lhsT=PcT[:, cc, h * 128:(h + 1) * 128], rhs=Raug[:, cc, :], start=(cc == 0), stop=(cc == nch - 1)),
                         ("PcT%d" % cc, "Raug"), (pk,))
            for b_ in range(2):
                P.op(V, lambda e, b_=b_: e.tensor_scalar(out=rr[:, b_:4:2], in0=psO[b_][:, 192:449:256], scalar1=1e-30, scalar2=None, op0=ALU.max), ("ps%d" % (2 + b_),), ("rrm%d" % b_,))
            P.op(V, lambda e: e.reciprocal(out=rr[:, 0:4], in_=rr[:, 0:4]), ("rrm0", "rrm1"), ("rr",))
            for h in range(4):
                po = psO[h % 2]
                pk = "ps%d" % (2 + h % 2)
                s0 = (h // 2) * 256
                if h == 0:
                    P.op(V, lambda e, po=po: e.tensor_scalar(out=imp[:], in0=po[:, 0:128], scalar1=rr[:, 0:1], scalar2=None, op0=ALU.mult), (pk, "rr"), ("imp",))
                else:
                    P.op(V, lambda e, h=h, po=po, s0=s0: e.scalar_tensor_tensor(out=imp[:], in0=po[:, s0:s0 + 128], scalar=rr[:, h:h + 1], in1=imp[:], op0=ALU.mult, op1=ALU.add), (pk, "rr", "imp"), ("imp",))
            P.op(V, lambda e: e.tensor_tensor(out=rr[:, 4:6], in0=rr[:, 0:2], in1=gsig[:, 0:6:3], op=ALU.mult), ("rr", "gsig"), ("rg",))
            for h in range(2):
                P.op(S, lambda e, h=h: e.activation(out=ynacc[:, h, :], in_=psO[h][:, 128:192], func=AF.Copy, scale=rr[:, 4 + h:5 + h]), ("ps%d" % (2 + h), "rg"), ("ynacc%d" % h,))
            w0 = 128 - 2 * i
            P.op(V, lambda e, w0=w0: e.scalar_tensor_tensor(out=sc[:, 0, :], in0=imp[:], scalar=1.0, in1=selc[:, 0, w0:w0 + 128], op0=ALU.add, op1=ALU.mult), ("imp", "selc"), ("sc0",))
            P.op(V, lambda e, w0=w0: e.tensor_tensor(out=sc[:, 0, :], in0=sc[:, 0, :], in1=selc[:, 2, w0:w0 + 128], op=ALU.max), ("sc0", "selc"), ("sc0",))
            P.op(V, lambda e: e.tensor_tensor(out=sc[:, 0, :], in0=sc[:, 0, :], in1=selc[:, 3, 0:128], op=ALU.max), ("sc0", "selc"), ("sc0",))
            P.op(V, lambda e: e.max(out=m8[:, 0:8], in_=sc[:, 0, :]), ("sc0",), ("m8a",))
            P.op(V, lambda e: e.match_replace(out=sc[:, 1, :], in_to_replace=m8[:, 0:8], in_values=sc[:, 0, :], imm_value=-1e30), ("sc0", "m8a"), ("sc1",))
            P.op(V, lambda e: e.max(out=m8[:, 8:16], in_=sc[:, 1, :]), ("sc1",), ("m8b",))
            P.op(V, lambda e: e.tensor_scalar(out=NegSel[:], in0=sc[:, 0, :], scalar1=m8[:, 15:16], scalar2=MASKNEG, op0=ALU.is_lt, op1=ALU.mult), ("sc0", "m8b"), ("NegSel",))
            P.op(TE, lambda e: e.transpose(out=ps[0][:, 0:128], in_=NegSel[:], identity=ident), ("NegSel", "c128"), ("ps0",))
            P.op(V, lambda e: e.tensor_copy(out=NegSelT[:, 0:128], in_=ps[0][:, 0:128]), ("ps0",), ("NegSelT",))
            P.op(S, lambda e: e.activation(out=NegSelT[:, 128:256], in_=ps[0][:, 0:128], func=AF.Copy), ("ps0",), ("NegSelT",))
            def attn_branch(kbs, kT_, kkey, Vt, vkey, gate_col, use_sel, ring=None):
                sl = (lambda kb_: kb_ % ring) if ring else (lambda kb_: kb_)
                nk = len(kbs)
                chunks = [kbs[c_:c_ + 2] for c_ in range(0, nk, 2)]
                done = 0
                for n_, ch in enumerate(chunks):
                    pS = psS[n_ % 2]
                    sk_ = "ps%d" % (n_ % 2)
                    for u, kb in enumerate(ch):
                        extra = []
                        if use_sel:
                            extra.append((Eb[:, kb * 128:(kb + 1) * 128], NegSelT[:], ("Eb", "NegSelT")))
                        if kb == i:
                            extra.append((identb[:], negm[:, 0, :], ("identb", "negm")))
                        if (not use_sel) and kb == i - 4:
                            extra.append((identb[:], negm[:, 1, :], ("identb", "negm")))
                        P.op(TE, lambda e, kb=kb, pS=pS, u=u, ne=len(extra): e.matmul(pS[:, u * 256:(u + 1) * 256], lhsT=kT_[:, sl(kb) * 128:(sl(kb) + 1) * 128], rhs=qropT[:], start=True, stop=(ne == 0)),
                             ("%s%d" % (kkey, sl(kb)), "qropT"), (sk_,))
                        for xi, (l_, r_, keys) in enumerate(extra):
                            P.op(TE, lambda e, l_=l_, r_=r_, pS=pS, u=u, last=(xi == len(extra) - 1): e.matmul(pS[:, u * 256:(u + 1) * 256], lhsT=l_, rhs=r_, start=False, stop=last), keys, (sk_,))
                    w_ = 256 * len(ch)
                    P.op(S, lambda e, pS=pS, n_=n_, w_=w_: e.activation(out=PT[:, n_ % 2, 0:w_], in_=pS[:, 0:w_], func=AF.Exp, scale=0.125), (sk_,), ("PT%d" % (n_ % 2),))
                    for u, kb in enumerate(ch):
                        for h in range(2):
                            P.op(TE, lambda e, h=h, kb=kb, n_=n_, u=u, first=(done == 0), last=(done == nk - 1): e.matmul(psO[h][:, 0:65], lhsT=PT[:, n_ % 2, u * 256 + h * 128:u * 256 + (h + 1) * 128], rhs=Vt[:, sl(kb), :], start=first, stop=last),
                                 ("PT%d" % (n_ % 2), "%s%d" % (vkey, sl(kb))), ("ps%d" % (2 + h),))
                        done += 1
                    if H3_TEST and use_sel and n_ == 0 and nk > 2:
                        P.op(TE, lambda e: e.transpose(out=ps5[:, 0:128], in_=identb[:], identity=identb[:]), ("identb",), ("ps5",))
                for h in range(2):
                    pk = "ps%d" % (2 + h)
                    P.op(V, lambda e, h=h: e.reciprocal(out=rr[:, 8 + h:9 + h], in_=psO[h][:, 64:65]), (pk,), ("rs%d" % h,))
                    P.op(V, lambda e, h=h: e.tensor_tensor(out=rr[:, 8 + h:9 + h], in0=rr[:, 8 + h:9 + h], in1=gsig[:, 3 * h + gate_col:3 * h + gate_col + 1], op=ALU.mult), ("rs%d" % h, "gsig"), ("rs%d" % h,))
                    P.op(V, lambda e, h=h: e.scalar_tensor_tensor(out=ynacc[:, h, :], in0=psO[h][:, 0:64], scalar=rr[:, 8 + h:9 + h], in1=ynacc[:, h, :], op0=ALU.mult, op1=ALU.add),
                         (pk, "rs%d" % h, "ynacc%d" % h), ("ynacc%d" % h,))

            marks.append(len(P.cap) if P.cap is not None else 0)
            attn_branch(list(range(i + 1)), ksT, "ksT", Vs, "Vs", 1, True)
            attn_branch(list(range(max(0, i - 4), i + 1)), kwT, "kwT", Vw, "Vw", 2, False, ring=8)
            P.op(G, lambda e, p=p: e.tensor_tensor(out=ybt[:, p, :], in0=ynacc[:].rearrange("p h d -> p (h d)"), in1=nzg[:, p, :], op=ALU.mult), ("ynacc0", "ynacc1", "nzg%d" % p), ("ybt%d" % p,))
            P.dma(SY, lambda e, p=p, t0=t0: e.dma_start(out=yb_d[t0:t0 + 128, :], in_=ybt[:, p, :]), "yb%d" % p, ("ybt%d" % p,), ("yb_out",))

    marks = []
    emit_A(0)
    for i in range(NT):
        if MERGE_MODE == 0:
            emit_B(i)
            if do_nsa:
                emit_C(i)
            if i + 1 < NT:
                emit_A(i + 1)
            continue
        P.cap = []
        emit_B(i)
        LB = P.cap
        P.cap = []
        if i + 1 < NT:
            emit_A(i + 1)
        LA = P.cap
        P.cap = []
        del marks[:]
        emit_C(i)
        LY = P.cap
        P.cap = None
        mk = marks[0]
        for kind, args in LY[:mk]:
            (P.op if kind == "op" else P.dma)(*args)
        P.replay_merged(LB, LA, LY[mk:])
    P.final_wait(SY, ("ya_out",) + (("yb_out",) if do_nsa else ()))
    P.emit(es)
    es.close()
    return nc


def _c128():
    c = np.zeros((128, 6, 128), np.float32)
    r = np.arange(128)
    c[:, 0, :] = np.eye(128)
    c[:, 1, :] = (r[:, None] <= r[None, :])
    c[:, 2, :] = 1.0
    c[:, 3, :] = np.where(r[None, :] <= r[:, None], 0.0, NEGB)
    c[:, 4, :] = (r[:, None] <= r[None, :])
    c[:, 5, :] = 16.0 * r[:, None] - r[None, :]
    return c


def mixer_inputs(inp, b, j, T):
    NT = T // 128
    g, pr = j // 2, j % 2
    w = inp["w_in"][0]
    off = {}
    o = 0
    for n_, wd in (("m_q", 1024), ("m_k", 1024), ("m_v", 1024), ("m_o", 1024), ("m_z", 1024), ("m_i", 4), ("m_f", 4),
                   ("n_q", 512), ("n_kc", 128), ("n_vc", 128), ("n_ks", 128), ("n_vs", 128), ("n_kw", 128), ("n_vw", 128),
                   ("n_g", 24), ("n_z", 512), ("g_a", 1024), ("g_b", 1024)):
        off[n_] = o
        o += wd
    hs = slice(256 * j, 256 * (j + 1))

    def cols(name, lo, hi):
        return w[:, off[name] + lo: off[name] + hi]
    gq = 256 * g
    own = [2 * pr, 2 * pr + 1]
    oth = [2 * (1 - pr), 2 * (1 - pr) + 1]
    wfm = np.concatenate([cols("m_q", 256 * j, 256 * j + 256), cols("m_k", 256 * j, 256 * j + 256),
                          cols("n_kc", 64 * g, 64 * g + 64), cols("n_vc", 64 * g, 64 * g + 64)], axis=1)
    hq = lambda h: cols("n_q", gq + 64 * h, gq + 64 * h + 64)
    hg = 4 * g
    wtm = np.concatenate([
        cols("m_v", 256 * j, 256 * j + 256), cols("m_o", 256 * j, 256 * j + 256),
        cols("m_z", 256 * j, 256 * j + 256), hq(oth[0]), hq(oth[1]),
        cols("n_z", 64 * (hg + own[0]), 64 * (hg + own[0]) + 128),
        hq(own[0]), hq(own[1]), cols("n_ks", 64 * g, 64 * g + 64), cols("n_kw", 64 * g, 64 * g + 64),
        cols("n_vs", 64 * g, 64 * g + 64), cols("n_vw", 64 * g, 64 * g + 64),
        cols("m_i", j, j + 1), cols("m_f", j, j + 1), cols("n_g", 3 * (hg + own[0]), 3 * (hg + own[0]) + 6),
    ], axis=1)
    assert wfm.shape[1] == NFM and wtm.shape[1] == NTM, (wfm.shape, wtm.shape)
    convw = np.zeros((128, 4, 4), np.float32)
    convb = np.zeros((128, 4), np.float32)
    for gi, (cw, cb) in enumerate(((inp["conv_q_w"][0], inp["conv_q_b"][0]), (inp["conv_k_w"][0], inp["conv_k_b"][0]))):
        for c in range(2):
            convw[:, 2 * gi + c, :] = cw[:, 256 * j + 128 * c: 256 * j + 128 * (c + 1)].T
            convb[:, 2 * gi + c] = cb[256 * j + 128 * c: 256 * j + 128 * (c + 1)]
    misc = np.zeros((128, 16), np.float32)
    misc[:, 0] = inp["b_fgate"][0, j]
    misc[:, 1] = inp["b_igate"][0, j]
    misc[:, 2:8] = inp["b_nsa_gate"][0, 3 * (hg + own[0]): 3 * (hg + own[0]) + 6][None, :]
    misc[:, 8:16] = (np.float32(500000.0) ** (-np.arange(8, dtype=np.float32) * np.float32(2.0 / 16)))[None, :]
    m = {
        "x": np.ascontiguousarray(inp["x"][b, :T]),
        "pos": np.ascontiguousarray(inp["positions"][b, :T].reshape(NT, 128).T.astype(np.int32)),
        "wfm": np.ascontiguousarray(wfm), "wtm": np.ascontiguousarray(wtm),
        "ng": np.ascontiguousarray(inp["norm_g"][0].reshape(8, 128).T),
        "convw": convw, "convb": convb, "misc": misc,
        "mhg": np.ascontiguousarray(np.broadcast_to(inp["mh_norm_g"][0, hs][None, :], (128, 256))),
        "c128": _c128(),
    }
    r = np.arange(128)
    w1 = np.stack([inp["cmp_w1_k"][0].reshape(32, 64, 128).transpose(1, 0, 2), inp["cmp_w1_v"][0].reshape(32, 64, 128).transpose(1, 0, 2)], axis=1)
    m["cw1"] = np.ascontiguousarray(w1)
    m["cw2"] = np.ascontiguousarray(np.stack([inp["cmp_w2_k"][0], inp["cmp_w2_v"][0]], axis=1))
    m["cpos"] = np.ascontiguousarray(np.stack([inp["cmp_pos_k"][0].T, inp["cmp_pos_v"][0].T], axis=1))
    m.update(_nsa_consts(T))
    return m


_NSA_CONSTS = {}


def _nsa_consts(T):
    if T in _NSA_CONSTS:
        return _NSA_CONSTS[T]
    r = np.arange(128)
    raug = np.zeros((128, 4, 193), np.float32)
    cidx = (np.arange(4)[None, :] * 128 + r[:, None])
    n = cidx - 1
    sidx = np.arange(128)
    ov = ((16 * n[:, :, None] < 64 * sidx + 64) & (16 * n[:, :, None] + 32 > 64 * sidx)).astype(np.float32)
    ov[cidx == 0] = 0.0
    raug[:, :, 0:128] = ov
    raug[:, :, 192] = (cidx > 0)
    E = (np.arange(T)[None, :] // 64 == r[:, None]).astype(np.float32)
    selc = np.zeros((128, 4, 256), np.float32)
    rel = np.arange(256)[None, :] - 128
    hi = (r[:, None] >= 64).astype(np.int64)
    valid = (rel <= hi)
    selc[:, 0, :] = valid
    selc[:, 1, :] = valid.astype(np.float32) - 1.0
    selc[:, 2, :] = np.where((rel == hi) | (rel == hi - 1), 1e9, 0.0)
    selc[:, 3, 0] = 1e9
    negm = np.zeros((128, 2, 256), np.float32)
    for h in range(2):
        negm[:, 0, h * 128:(h + 1) * 128] = np.where(r[:, None] > r[None, :], MASKNEG, 0.0)
        negm[:, 1, h * 128:(h + 1) * 128] = np.where(r[:, None] <= r[None, :], MASKNEG, 0.0)
    out = {"raug": raug, "E": E, "selc": selc, "negm": negm}
    _NSA_CONSTS[T] = out
    return out


def emit_tail(nc, P, es, NTT, x_d, ya_tile, yb_tile, wg_d, wa_d, wb_d, wo_d, ng_d, fg_d, c128_d, out_d, pfx="t", sel4=None):
    def sb(name, shape, dt=F32):
        return es.enter_context(nc.sbuf_tensor(pfx + name, list(shape), dt))

    def pst(name, shape, dt=F32):
        return es.enter_context(nc.psum_tensor(pfx + name, list(shape), dt))

    K = lambda s: pfx + s
    V, S, G, TE, SY = "vector", "scalar", "gpsimd", "tensor", "sync"
    Wg = sb("Wg", [128, 8, 2048], BF16)
    Wa = sb("Wa", [128, 8, 1024], BF16)
    Wb = sb("Wb", [128, 4, 1024], BF16)
    Wo = sb("Wo", [128, 8, 1024], BF16)
    stg = sb("stg", [128, 2, 2048])
    ngs = sb("ngs", [128, 8])
    fgs = sb("fgs", [128, 1024])
    identb = sb("identb", [128, 128], BF16)
    epsc = sb("epsc", [128, 1])
    xt = sb("xt", [128, 2, D])
    junk = sb("junk", [128, D], BF16)
    hn = sb("hn", [128, D], BF16)
    hT = sb("hT", [128, 8, 128], BF16)
    yab = sb("yab", [128, 2, 1024], BF16)
    ybb = sb("ybb", [128, 2, 512], BF16)
    yT = sb("yT", [128, 2, 12, 128], BF16)
    sg = sb("sg", [128, 2, 2048], BF16)
    junk2 = sb("junk2", [128, D], BF16)
    t1 = sb("t1", [128, 1024])
    t2 = sb("t2", [128, 1024])
    mg = sb("mg", [128, 1024], BF16)
    mT = sb("mT", [128, 8, 128], BF16)
    xo = sb("xo", [128, 1024])
    ot = sb("ot", [128, 2, 1024])
    sm = sb("sm", [128, 2, 8])
    ps = [pst("ps%d" % i, [128, 512]) for i in range(6)]
    psT = pst("psT", [128, 1024], BF16)
    if sel4 is not None:
        ohs = sb("ohs", [128, 4])
        cand = sb("cand", [128, 2, 4, 1536], BF16)
        P.dma(SY, lambda e: e.dma_start(out=ohs[:], in_=sel4[0]), K("c3"), (), (K("ohs"),))
    psT2 = pst("psT2", [128, 1024], BF16)

    P.dma(SY, lambda e: e.dma_start(out=ngs[:], in_=ng_d), K("c0"), (), (K("ngs"),))
    P.dma(SY, lambda e: e.dma_start(out=fgs[:], in_=fg_d), K("c1"), (), (K("fgs"),))
    P.dma(G, lambda e: e.dma_start(out=identb[:], in_=c128_d[:, 0, :]), K("c2"), (), (K("identb"),))
    P.op(V, lambda e: e.memset(epsc[:], EPS), (), (K("epsc"),))
    for k in range(8):
        sp_ = k % 2
        P.dma(SY, lambda e, k=k, sp_=sp_: e.dma_start(out=stg[:, sp_, :], in_=wg_d[k * 128:(k + 1) * 128, :]), K("w%d" % sp_), (), (K("stg%d" % sp_),))
        P.op(V, lambda e, k=k, sp_=sp_: e.tensor_scalar(out=Wg[:, k, :], in0=stg[:, sp_, :], scalar1=ngs[:, k:k + 1], scalar2=None, op0=ALU.mult), (K("stg%d" % sp_), K("ngs")), (K("W"),))
    for k in range(8):
        P.dma(G, lambda e, k=k: e.dma_start(out=Wa[:, k, :], in_=wa_d[k * 128:(k + 1) * 128, :]), K("wa"), (), (K("W"),))
        P.dma(G, lambda e, k=k: e.dma_start(out=Wo[:, k, :], in_=wo_d[k * 128:(k + 1) * 128, :]), K("wo"), (), (K("W"),))
    for k in range(4):
        P.dma(G, lambda e, k=k: e.dma_start(out=Wb[:, k, :], in_=wb_d[k * 128:(k + 1) * 128, :]), K("wb"), (), (K("W"),))

    def T1(i):
        p = i % 2
        t0 = i * 128
        X, SM = K("xt%d" % p), K("sm%d" % p)
        smp = sm[:, p, :]
        P.dma(SY, lambda e, p=p, t0=t0: e.dma_start(out=xt[:, p, :], in_=x_d[t0:t0 + 128, :]), K("x%d" % p), (), (X,))
        if sel4 is None:
            P.dma(SY, lambda e, p=p, i=i: e.dma_start(out=yab[:, p, :], in_=ya_tile(i)), K("ya%d" % p), (), (K("yab%d" % p),))
            P.dma(SY, lambda e, p=p, i=i: e.dma_start(out=ybb[:, p, :], in_=yb_tile(i)), K("yb%d" % p), (), (K("ybb%d" % p),))
        else:
            for q_ in range(4):
                P.dma(SY, lambda e, p=p, i=i, q_=q_: e.dma_start(out=cand[:, p, q_, 0:1024], in_=sel4[1](q_, i)), K("ya%d" % p), (), (K("cand%d" % p),))
                P.dma(SY, lambda e, p=p, i=i, q_=q_: e.dma_start(out=cand[:, p, q_, 1024:1536], in_=sel4[2](q_, i)), K("yb%d" % p), (), (K("cand%d" % p),))
            for (dst, c0_, c1_, key) in ((yab, 0, 1024, "yab%d" % p), (ybb, 1024, 1536, "ybb%d" % p)):
                P.op(V, lambda e, p=p, dst=dst, c0_=c0_, c1_=c1_: e.tensor_scalar(out=dst[:, p, :], in0=cand[:, p, 0, c0_:c1_], scalar1=ohs[:, 0:1], scalar2=None, op0=ALU.mult),
                     (K("cand%d" % p), K("ohs")), (K(key),))
                for q_ in range(1, 4):
                    P.op(V, lambda e, p=p, dst=dst, c0_=c0_, c1_=c1_, q_=q_: e.scalar_tensor_tensor(out=dst[:, p, :], in0=cand[:, p, q_, c0_:c1_], scalar=ohs[:, q_:q_ + 1], in1=dst[:, p, :], op0=ALU.mult, op1=ALU.add),
                         (K("cand%d" % p), K("ohs"), K(key)), (K(key),))
        P.op(S, lambda e, p=p, smp=smp: e.activation(out=junk[:], in_=xt[:, p, :], func=AF.Square, accum_out=smp[:, 0:1]), (X,), (K("junk"), SM + "a"))
        P.op(S, lambda e, smp=smp: e.activation(out=smp[:, 1:2], in_=smp[:, 0:1], func=AF.Sqrt, scale=1.0 / D, bias=epsc[:]), (SM + "a", K("epsc")), (SM + "b",))
        P.op(V, lambda e, smp=smp: e.reciprocal(out=smp[:, 2:3], in_=smp[:, 1:2]), (SM + "b",), (SM + "c",))
        P.op(S, lambda e, p=p, smp=smp: e.activation(out=hn[:], in_=xt[:, p, :], func=AF.Copy, scale=smp[:, 2:3]), (X, SM + "c"), (K("hn"),))
        for k in range(8):
            P.op(TE, lambda e, k=k: e.transpose(out=psT[:, k * 128:(k + 1) * 128], in_=hn[:, k * 128:(k + 1) * 128], identity=identb[:]), (K("hn"), K("identb")), (K("psT"),))
        P.op(V, lambda e: e.tensor_copy(out=hT[:], in_=psT[:].rearrange("p (k t) -> p k t", k=8)), (K("psT"),), (K("hT"),))
        for bk in range(4):
            pb, pk = ps[bk % 2], K("ps%d" % (bk % 2))
            for k in range(8):
                P.op(TE, lambda e, k=k, bk=bk, pb=pb: e.matmul(pb[:, :], lhsT=hT[:, k, :], rhs=Wg[:, k, bk * 512:(bk + 1) * 512], start=(k == 0), stop=(k == 7)), (K("hT"), K("W")), (pk,))
            P.op(S, lambda e, bk=bk, pb=pb, p=p: e.activation(out=sg[:, p, bk * 512:(bk + 1) * 512], in_=pb[:, :], func=AF.Sigmoid), (pk,), (K("sg%d_%d" % (p, bk)),))
        for k in range(8):
            P.op(TE, lambda e, k=k, p=p: e.transpose(out=psT[:, k * 128:(k + 1) * 128], in_=yab[:, p, k * 128:(k + 1) * 128], identity=identb[:]), (K("yab%d" % p), K("identb")), (K("psT"),))
        P.op(V, lambda e, p=p: e.tensor_copy(out=yT[:, p, 0:8, :], in_=psT[:].rearrange("p (k t) -> p k t", k=8)), (K("psT"),), (K("yTa%d" % p),))
        for k in range(4):
            P.op(TE, lambda e, k=k, p=p: e.transpose(out=psT[:, k * 128:(k + 1) * 128], in_=ybb[:, p, k * 128:(k + 1) * 128], identity=identb[:]), (K("ybb%d" % p), K("identb")), (K("psT"),))
        P.op(V, lambda e, p=p: e.tensor_copy(out=yT[:, p, 8:12, :], in_=psT[:, 0:512].rearrange("p (k t) -> p k t", k=4)), (K("psT"),), (K("yTb%d" % p),))

    def T2(i):
        p = i % 2
        t0 = i * 128
        X, SM = K("xt%d" % p), K("sm%d" % p)
        smp = sm[:, p, :]
        for half in range(2):
            for k in range(8):
                P.op(TE, lambda e, k=k, half=half, p=p: e.matmul(ps[2 + half][:, :], lhsT=yT[:, p, k, :], rhs=Wa[:, k, half * 512:(half + 1) * 512], start=(k == 0), stop=(k == 7)), (K("yTa%d" % p), K("W")), (K("ps%d" % (2 + half)),))
            P.op(V, lambda e, half=half, p=p: e.tensor_tensor(out=t1[:, half * 512:(half + 1) * 512], in0=ps[2 + half][:, :], in1=sg[:, p, half * 512:(half + 1) * 512], op=ALU.mult),
                 (K("ps%d" % (2 + half)), K("sg%d_%d" % (p, half))), (K("t1%d" % half),))
        for half in range(2):
            for k in range(4):
                P.op(TE, lambda e, k=k, half=half, p=p: e.matmul(ps[2 + half][:, :], lhsT=yT[:, p, 8 + k, :], rhs=Wb[:, k, half * 512:(half + 1) * 512], start=(k == 0), stop=(k == 3)), (K("yTb%d" % p), K("W")), (K("ps%d" % (2 + half)),))
            P.op(V, lambda e, half=half, p=p: e.tensor_tensor(out=t2[:, half * 512:(half + 1) * 512], in0=ps[2 + half][:, :], in1=sg[:, p, 1024 + half * 512:1024 + (half + 1) * 512], op=ALU.mult),
                 (K("ps%d" % (2 + half)), K("sg%d_%d" % (p, 2 + half))), (K("t2%d" % half),))
            P.op(G, lambda e, half=half: e.tensor_tensor(out=mg[:, half * 512:(half + 1) * 512], in0=t1[:, half * 512:(half + 1) * 512], in1=t2[:, half * 512:(half + 1) * 512], op=ALU.add),
                 (K("t1%d" % half), K("t2%d" % half)), (K("mg%d" % half),))
        for k in range(8):
            P.op(TE, lambda e, k=k: e.transpose(out=psT2[:, k * 128:(k + 1) * 128], in_=mg[:, k * 128:(k + 1) * 128], identity=identb[:]), (K("mg0"), K("mg1"), K("identb")), (K("psT2"),))
        P.op(V, lambda e: e.tensor_copy(out=mT[:], in_=psT2[:].rearrange("p (k t) -> p k t", k=8)), (K("psT2"),), (K("mT"),))
        for half in range(2):
            for k in range(8):
                P.op(TE, lambda e, k=k, half=half: e.matmul(ps[4 + half][:, :], lhsT=mT[:, k, :], rhs=Wo[:, k, half * 512:(half + 1) * 512], start=(k == 0), stop=(k == 7)), (K("mT"), K("W")), (K("ps%d" % (4 + half)),))
            P.op(V, lambda e, half=half, p=p: e.tensor_tensor(out=xo[:, half * 512:(half + 1) * 512], in0=ps[4 + half][:, :], in1=xt[:, p, half * 512:(half + 1) * 512], op=ALU.add),
                 (K("ps%d" % (4 + half)), X), (K("xo%d" % half),))
        P.op(S, lambda e, smp=smp: e.activation(out=junk2[:], in_=xo[:], func=AF.Square, accum_out=smp[:, 3:4]), (K("xo0"), K("xo1")), (K("junk2"), SM + "d"))
        P.op(S, lambda e, smp=smp: e.activation(out=smp[:, 4:5], in_=smp[:, 3:4], func=AF.Sqrt, scale=1.0 / D, bias=epsc[:]), (SM + "d", K("epsc")), (SM + "e",))
        P.op(V, lambda e, smp=smp: e.reciprocal(out=smp[:, 5:6], in_=smp[:, 4:5]), (SM + "e",), (SM + "f",))
        P.op(V, lambda e, smp=smp, p=p: e.scalar_tensor_tensor(out=ot[:, p, :], in0=xo[:], scalar=smp[:, 5:6], in1=fgs[:], op0=ALU.mult, op1=ALU.mult), (K("xo0"), K("xo1"), SM + "f", K("fgs")), (K("ot%d" % p),))
        P.dma(SY, lambda e, p=p, t0=t0: e.dma_start(out=out_d[t0:t0 + 128, :], in_=ot[:, p, :]), K("o%d" % p), (K("ot%d" % p),), (K("out"),))

    T1(0)
    for i in range(NTT):
        P.cap = []
        T2(i)
        L2 = P.cap
        P.cap = []
        if i + 1 < NTT:
            T1(i + 1)
        L1 = P.cap
        P.cap = None
        if TAIL_MERGE:
            P.replay_merged(L2, L1)
        else:
            P.replay_merged(L2, [])
            P.replay_merged(L1, [])
    return (K("out"),)


def build_tail(NTT):
    nc = bass.Bass("TRN2", target_bir_lowering=False)
    P = Prog(nc)
    es = ExitStack()
    TT = NTT * 128

    def din(name, shape, dt=F32):
        return nc.dram_tensor(name, list(shape), dt, kind="ExternalInput").ap()

    x_d = din("x", [TT, D])
    ya_d = din("ya", [TT, 1024], BF16)
    yb_d = din("yb", [TT, 512], BF16)
    wg_d = din("wg", [D, 2048])
    wa_d = din("wa", [1024, 1024])
    wb_d = din("wb", [512, 1024])
    wo_d = din("wo", [1024, 1024])
    ng_d = din("ng", [128, 8])
    fg_d = din("fg", [128, 1024])
    c128_d = din("c128", [128, 6, 128])
    out_d = nc.dram_tensor("out", [TT, D], F32, kind="ExternalOutput").ap()
    keys = emit_tail(nc, P, es, NTT, x_d, lambda i: ya_d[i * 128:(i + 1) * 128, :], lambda i: yb_d[i * 128:(i + 1) * 128, :],
                     wg_d, wa_d, wb_d, wo_d, ng_d, fg_d, c128_d, out_d)
    P.final_wait("sync", keys)
    P.emit(es)
    es.close()
    return nc


def tail_inputs(inp, c, NTT, ya_full, yb_full, T):
    TT = NTT * 128
    b, q = c // 4, c % 4
    w = inp["w_in"][0]
    sl = slice(q * TT, (q + 1) * TT)
    return {
        "x": np.ascontiguousarray(inp["x"][b, :T][sl]),
        "ya": np.ascontiguousarray(ya_full[b][sl]),
        "yb": np.ascontiguousarray(yb_full[b][sl]),
        "wg": np.ascontiguousarray(w[:, 8992 - 2048:]),
        "wa": np.ascontiguousarray(inp["w_branch_a"][0]),
        "wb": np.ascontiguousarray(inp["w_branch_b"][0]),
        "wo": np.ascontiguousarray(inp["w_out"][0]),
        "ng": np.ascontiguousarray(inp["norm_g"][0].reshape(8, 128).T),
        "fg": np.ascontiguousarray(np.broadcast_to(inp["final_norm_g"][None, :], (128, 1024))),
        "c128": _c128(),
    }


_CACHE = {}


def kernel_unfused(**inputs):
    inp = {k: np.asarray(v) for k, v in inputs.items()}
    T = inp["x"].shape[1]
    if ("mix", T) not in _CACHE:
        _CACHE[("mix", T)] = build_mixer(T, do_nsa=True)
    nc1 = _CACHE[("mix", T)]
    maps = [mixer_inputs(inp, c // 4, c % 4, T) for c in range(8)]
    res = run_bass_kernel_spmd(nc1, maps, core_ids=list(range(8)))
    ya_full = [np.concatenate([np.asarray(res.results[4 * b + j]["ya"]) for j in range(4)], axis=1) for b in range(2)]
    yb_full = [np.concatenate([np.asarray(res.results[4 * b + j]["yb"]) for j in range(4)], axis=1) for b in range(2)]
    NTT = T // 128 // 4
    if ("tail", NTT) not in _CACHE:
        _CACHE[("tail", NTT)] = build_tail(NTT)
    nc2 = _CACHE[("tail", NTT)]
    maps2 = [tail_inputs(inp, c, NTT, ya_full, yb_full, T) for c in range(8)]
    res2 = run_bass_kernel_spmd(nc2, maps2, core_ids=list(range(8)))
    out = np.stack([np.concatenate([np.asarray(res2.results[4 * b + q]["out"]) for q in range(4)], axis=0) for b in range(2)], axis=0)
    return out.astype(np.float32)


def build_fused(T):
    do_nsa = True
    NT = T // 128
    nc = bass.Bass("TRN2", target_bir_lowering=False)
    P = Prog(nc)
    es = ExitStack()

    def din(name, shape, dt=F32):
        return nc.dram_tensor(name, list(shape), dt, kind="ExternalInput").ap()

    x_d = din("x", [T, D])
    pos_d = din("pos", [128, NT], I32)
    ng_d = din("ng", [128, 8])
    c128_d = din("c128", [128, 6, 128])
    w1_d = din("cw1", [64, 2, 32, 128])
    w2_d = din("cw2", [128, 2, 64])
    cpos_d = din("cpos", [64, 2, 32])
    raug_d = din("raug", [128, 4, 193])
    E_d = din("E", [128, T])
    sel_d = din("selc", [128, 4, 256])
    nm_d = din("negm", [128, 2, 256])
    per = []
    for j in range(4):
        per.append((din("wfm_%d" % j, [D, NFM]), din("wtm_%d" % j, [D, NTM]), din("convw_%d" % j, [128, 4, 4]),
                    din("convb_%d" % j, [128, 4]), din("misc_%d" % j, [128, 16]), din("mhg_%d" % j, [128, 256])))
    wg_d = din("wg", [D, 2048])
    wa_d = din("wa", [1024, 1024])
    wb_d = din("wb", [512, 1024])
    wo_d = din("wo", [1024, 1024])
    fg_d = din("fg", [128, 1024])
    ya_s = nc.dram_tensor("ya_s", [T, 1024], BF16).ap()
    yb_s = nc.dram_tensor("yb_s", [T, 512], BF16).ap()
    TQ = T // 4
    xq_d = din("xq", [TQ, D])
    oh_d = din("onehot", [128, 4])
    out_d = nc.dram_tensor("out", [TQ, D], F32, kind="ExternalOutput").ap()

    def sb(name, shape, dt=F32):
        return es.enter_context(nc.sbuf_tensor(name, list(shape), dt))

    def pst(name, shape, dt=F32):
        return es.enter_context(nc.psum_tensor(name, list(shape), dt))

    Wfm = sb("Wfm", [128, 8, NFM], BF16)
    Wtm = sb("Wtm", [128, 8, NTM], BF16)
    stg = sb("stg", [128, 2, NTM], F32)
    ngs = sb("ngs", [128, 8])
    cws = sb("cws", [128, 4, 4])
    cbs = sb("cbs", [128, 4])
    misc = sb("miscs", [128, 16])
    mhg = sb("mhgs", [128, 256])
    c128 = sb("c128s", [128, 6, 128])
    identb = sb("identb", [128, 128], BF16)
    ident = c128[:, 0, :]
    Utri = c128[:, 1, :]
    ones = c128[:, 2, :]
    cnegB = c128[:, 3, :]
    maskT01 = c128[:, 4, :]
    xt = sb("xt", [128, 2, D])
    junk = sb("junk", [128, D], BF16)
    hn = sb("hn", [128, 2, D], BF16)
    hT = sb("hT", [128, 2, 8, 128], BF16)
    sm = sb("sm", [128, 2, 32])
    convbuf = sb("convbuf", [128, 4, 131])
    cacc = sb("cacc", [128, 4, 128])
    qkT = sb("qkT", [128, 2, 4, 128], BF16)
    Vaug = sb("Vaug", [128, 2, 257], BF16)
    og = sb("og", [128, 2, 256])
    zg = sb("zg", [128, 2, 256])
    G2 = sb("G2", [128, 2, 2])
    dbg = sb("dbg", [128, 2, 128])
    Bm = sb("Bm", [128, 128])
    DT = sb("DT", [128, 128])
    DTm = sb("DTm", [128, 128], BF16)
    Wm = sb("Wm", [128, 128])
    qwT = sb("qwT", [128, 2, 128], BF16)
    smT = sb("smT", [128, 128], BF16)
    hbuf = sb("hbuf", [128, 256])
    hbuf2 = sb("hbuf2", [128, 256])
    bnst = sb("bnst", [128, 8])
    kwt = sb("kwt", [128, 256], BF16)
    ktok = sb("ktok", [128, 2, 256], BF16)
    Cst = sb("Cst", [128, 2, 257])
    Cb = sb("Cb", [128, 2, 257], BF16)
    mcol = sb("mcol", [128, 1])
    epsc = sb("epsc", [128, 1])
    negpi = sb("negpi", [128, 1])
    tmC = sb("tmC", [128, 2, TMC])
    yat = sb("yat", [128, 2, 256], BF16)
    ybt = sb("ybt", [128, 2, 128], BF16)

    if do_nsa:
        W1b = sb("W1b", [64, 2, 32, 128], BF16)
        W2b = sb("W2b", [128, 2, 64], BF16)
        posT = sb("posT", [64, 2, 32], BF16)
        biash = sb("biash", [128, 2])
        cpre = sb("cpre", [64, 2, 2, 144], BF16)
        kcT = sb("kcT", [64, 512], BF16)
        GV = sb("GV", [128, 512], BF16)
        Raug = sb("Raug", [128, 4, 193], BF16)
        ksT = sb("ksT", [64, T], BF16)
        kwT = sb("kwT", [64, 8 * 128], BF16)
        Vs = sb("Vs", [128, NT, 65], BF16)
        Vw = sb("Vw", [128, 8, 65], BF16)
        Eb = sb("Eb", [128, T], BF16)
        selc = sb("selcs", [128, 4, 256])
        negm = sb("negms", [128, 2, 256], BF16)
        posi = sb("posi", [128, NT], I32)
        posf = sb("posf", [128, NT])
        ang = sb("ang", [128, NT, 8])
        ang2 = sb("ang2", [128, NT, 8])
        ang3 = sb("ang3", [128, NT, 8])
        angi = sb("angi", [128, NT, 8], I32)
        cossin = sb("cossin", [128, NT, 16])
        qtok = sb("qtok", [128, 2, 4, 64])
        rp = sb("rp", [128, 4, 64])
        rtmp = sb("rtmp", [128, 4, 4, 8])
        qrawT = sb("qrawT", [64, 512], BF16)
        qropT = sb("qropT", [64, 256], BF16)
        PcT = sb("PcT", [128, 4, 512], BF16)
        PT = sb("PT", [128, 2, 512], BF16)
        cmask = sb("cmask", [128, 128], BF16)
        hx = sb("hx", [128, 4, 16])
        gl = sb("gl", [128, 16], BF16)
        imp = sb("imp", [128, 128])
        sc = sb("sc", [128, 2, 128])
        m8 = sb("m8", [128, 16])
        sel01 = sb("sel01", [128, 128])
        NegSel = sb("NegSel", [128, 128])
        NegSelT = sb("NegSelT", [128, 256], BF16)
        gsig = sb("gsig", [128, 8])
        nzg = sb("nzg", [128, 2, 128])
        ynacc = sb("ynacc", [128, 2, 64])
        rr = sb("rr", [128, 16])

    ps = [pst("ps%d" % i, [128, 512]) for i in range(5)]
    ps5 = pst("ps5", [128, 1024], BF16)
    ps6 = pst("ps6", [128, 512])
    ps7 = pst("ps7", [128, 512])

    V, S, G, TE, SY = "vector", "scalar", "gpsimd", "tensor", "sync"

    def run_pass(wfm_d, wtm_d, cw_d, cb_d, misc_d, mhg_d, ya_d, yb_d):
        P.dma(SY, lambda e: e.dma_start(out=ngs[:], in_=ng_d), "c0", (), ("ngs",))
        P.dma(SY, lambda e: e.dma_start(out=cws[:], in_=cw_d), "c1", (), ("cws",))
        P.dma(SY, lambda e: e.dma_start(out=cbs[:], in_=cb_d), "c2", (), ("cbs",))
        P.dma(SY, lambda e: e.dma_start(out=misc[:], in_=misc_d), "c3", (), ("misc",))
        P.dma(SY, lambda e: e.dma_start(out=mhg[:], in_=mhg_d), "c4", (), ("mhg",))
        P.dma(SY, lambda e: e.dma_start(out=c128[:], in_=c128_d), "c5", (), ("c128",))
        P.dma(G, lambda e: e.dma_start(out=identb[:], in_=c128_d[:, 0, :]), "c6", (), ("identb",))
        for k in range(8):
            sp_ = k % 2
            for (wd, Wt, ncol, nm) in ((wfm_d, Wfm, NFM, "f"), (wtm_d, Wtm, NTM, "t")):
                P.dma(SY, lambda e, wd=wd, ncol=ncol, k=k, sp_=sp_: e.dma_start(out=stg[:, sp_, 0:ncol], in_=wd[k * 128:(k + 1) * 128, :]),
                      "w%d" % sp_, (), ("stg%d" % sp_,))
                P.op(V, lambda e, Wt=Wt, ncol=ncol, k=k, sp_=sp_: e.tensor_scalar(out=Wt[:, k, :], in0=stg[:, sp_, 0:ncol], scalar1=ngs[:, k:k + 1], scalar2=None, op0=ALU.mult),
                     ("stg%d" % sp_, "ngs"), ("W",))
        P.op(V, lambda e: e.memset(epsc[:], EPS), (), ("epsc",))
        P.op(V, lambda e: e.memset(negpi[:], -math.pi), (), ("negpi",))
        P.op(V, lambda e: e.memset(Cst[:], 0.0), (), ("Cst",))
        P.op(V, lambda e: e.memset(Cb[:], 0.0), (), ("Cb",))
        P.op(V, lambda e: e.memset(mcol[:], 0.0), (), ("m",))
        P.op(V, lambda e: e.memset(convbuf[:], 0.0), (), ("convbuf",))
        P.op(V, lambda e: e.memset(Vaug[:], 1.0), (), ("Vaug0", "Vaug1"))

        if do_nsa:
            P.dma(G, lambda e: e.dma_start(out=W1b[:], in_=w1_d), "n0", (), ("W1b",))
            P.dma(G, lambda e: e.dma_start(out=W2b[:], in_=w2_d), "n1", (), ("W2b",))
            P.dma(G, lambda e: e.dma_start(out=posT[:], in_=cpos_d), "n2", (), ("posT",))
            P.dma(G, lambda e: e.dma_start(out=Raug[:], in_=raug_d), "n3", (), ("Raug",))
            P.dma(G, lambda e: e.dma_start(out=negm[:], in_=nm_d), "n4", (), ("negm",))
            P.dma(SY, lambda e: e.dma_start(out=selc[:], in_=sel_d), "n5", (), ("selc",))
            P.dma(SY, lambda e: e.dma_start(out=posi[:], in_=pos_d), "n6", (), ("posi",))
            for c0 in range(0, T, 2048):
                c1 = min(T, c0 + 2048)
                P.dma(G, lambda e, c0=c0, c1=c1: e.dma_start(out=Eb[:, c0:c1], in_=E_d[:, c0:c1]), "n7", (), ("Eb",))
            P.op(V, lambda e: e.memset(cpre[:], 0.0), (), ("cpre0", "cpre1"))
            P.op(V, lambda e: e.memset(kcT[:], 0.0), (), ("kcT",))
            P.op(V, lambda e: e.memset(GV[:], 0.0), (), ("GV",))
            P.op(G, lambda e: e.memset(Vs[:], 1.0), (), tuple("Vs%d" % t_ for t_ in range(NT)))
            P.op(G, lambda e: e.memset(Vw[:], 1.0), (), tuple("Vw%d" % t_ for t_ in range(8)))
            P.op(V, lambda e: e.tensor_copy(out=posf[:], in_=posi[:]), ("posi",), ("posf",))
            for f in range(8):
                P.op(V, lambda e, f=f: e.tensor_scalar(out=ang[:, :, f], in0=posf[:], scalar1=misc[:, 8 + f:9 + f], scalar2=None, op0=ALU.mult), ("posf", "misc"), ("ang",))
            for (off_, c0_) in ((0.75, 0), (0.5, 8)):
                P.op(V, lambda e, off_=off_: e.tensor_scalar(out=ang2[:], in0=ang[:], scalar1=1.0 / (2.0 * math.pi), scalar2=off_, op0=ALU.mult, op1=ALU.add), ("ang", "cossin"), ("ang2",))
                P.op(V, lambda e: e.tensor_copy(out=angi[:], in_=ang2[:]), ("ang2",), ("angi",))
                P.op(V, lambda e: e.tensor_copy(out=ang3[:], in_=angi[:]), ("angi",), ("ang3",))
                P.op(V, lambda e: e.tensor_tensor(out=ang2[:], in0=ang2[:], in1=ang3[:], op=ALU.subtract), ("ang2", "ang3"), ("ang2",))
                P.op(V, lambda e: e.tensor_scalar(out=ang3[:], in0=ang2[:], scalar1=0.0, scalar2=None, op0=ALU.is_lt), ("ang2",), ("ang3",))
                P.op(V, lambda e: e.tensor_tensor(out=ang2[:], in0=ang2[:], in1=ang3[:], op=ALU.add), ("ang2", "ang3"), ("ang2",))
                P.op(S, lambda e, c0_=c0_: e.activation(out=cossin[:, :, c0_:c0_ + 8], in_=ang2[:], func=AF.Sin, scale=2.0 * math.pi, bias=negpi[:]), ("ang2", "negpi"), ("cossin",))
            for kv in range(2):
                for l in range(32):
                    P.op(TE, lambda e, kv=kv, l=l: e.matmul(ps[4][:, 300 + kv:301 + kv], lhsT=W1b[:, kv, l, :], rhs=posT[:, kv, l:l + 1], start=(l == 0), stop=(l == 31)),
                         ("W1b", "posT"), ("ps4",))
            P.op(V, lambda e: e.tensor_copy(out=biash[:, 0:2], in_=ps[4][:, 300:302]), ("ps4",), ("biash",))

        def emit_A(i):
            p = i % 2
            t0 = i * 128
            X, HN, HT, SM = "xt%d" % p, "hn%d" % p, "hT%d" % p, "sm%d" % p
            smp = sm[:, p, :]
            P.dma(SY, lambda e, p=p, t0=t0: e.dma_start(out=xt[:, p, :], in_=x_d[t0:t0 + 128, :]), "x%d" % p, (), (X,))
            P.op(S, lambda e, p=p, smp=smp: e.activation(out=junk[:], in_=xt[:, p, :], func=AF.Square, accum_out=smp[:, 0:1]), (X,), ("junk", SM + "a"))
            P.op(S, lambda e, smp=smp: e.activation(out=smp[:, 1:2], in_=smp[:, 0:1], func=AF.Sqrt, scale=1.0 / D, bias=epsc[:]), (SM + "a", "epsc"), (SM + "b",))
            P.op(V, lambda e, smp=smp: e.reciprocal(out=smp[:, 2:3], in_=smp[:, 1:2]), (SM + "b",), (SM + "c",))
            P.op(S, lambda e, p=p, smp=smp: e.activation(out=hn[:, p, :], in_=xt[:, p, :], func=AF.Copy, scale=smp[:, 2:3]), (X, SM + "c"), (HN,))
            for k in range(8):
                P.op(TE, lambda e, p=p, k=k: e.transpose(out=ps5[:, k * 128:(k + 1) * 128], in_=hn[:, p, k * 128:(k + 1) * 128], identity=identb[:]),
                     (HN, "identb"), ("ps5",))
            P.op(V, lambda e, p=p: e.tensor_copy(out=hT[:, p, :, :], in_=ps5[:].rearrange("p (k t) -> p k t", k=8)), ("ps5",), (HT,))
            for grp in range(4):
                for k in range(8):
                    P.op(TE, lambda e, p=p, k=k, grp=grp: e.matmul(ps7[:, grp * 128:(grp + 1) * 128], lhsT=Wfm[:, k, grp * 128:(grp + 1) * 128], rhs=hT[:, p, k, :], start=(k == 0), stop=(k == 7)),
                         (HT, "W"), ("ps7",))
            P.op(V, lambda e: e.tensor_copy(out=convbuf[:, :, 0:3], in_=convbuf[:, :, 128:131]), ("convbuf",), ("convbuf",))
            P.op(S, lambda e: e.activation(out=convbuf[:, :, 3:131], in_=ps7[:].rearrange("p (g t) -> p g t", g=4), func=AF.Copy), ("ps7",), ("convbuf",))
            if do_nsa:
                for c in range(2):
                    for k in range(8):
                        P.op(TE, lambda e, p=p, k=k, c=c: e.matmul(ps7[0:64, c * 128:(c + 1) * 128], lhsT=Wfm[:, k, 512 + c * 64:512 + (c + 1) * 64], rhs=hT[:, p, k, :], start=(k == 0), stop=(k == 7)),
                             (HT, "W"), ("ps7",))
                P.op(V, lambda e, p=p: e.tensor_copy(out=cpre[:, p, :, 0:16], in_=cpre[:, 1 - p, :, 128:144]), ("cpre%d" % (1 - p),), ("cpre%d" % p,))
                P.op(S, lambda e, p=p: e.activation(out=cpre[:, p, :, 16:144], in_=ps7[0:64, 0:256].rearrange("p (c t) -> p c t", c=2), func=AF.Copy), ("ps7",), ("cpre%d" % p,))
            for grp in range(4):
                P.op(V, lambda e, grp=grp: e.tensor_scalar(out=cacc[:, grp, :], in0=convbuf[:, grp, 0:128], scalar1=cws[:, grp, 0:1], scalar2=cbs[:, grp:grp + 1], op0=ALU.mult, op1=ALU.add),
                     ("convbuf", "cws", "cbs"), ("cacc%d" % grp,))
                for j in range(1, 4):
                    P.op(V, lambda e, grp=grp, j=j: e.scalar_tensor_tensor(out=cacc[:, grp, :], in0=convbuf[:, grp, j:j + 128], scalar=cws[:, grp, j:j + 1], in1=cacc[:, grp, :], op0=ALU.mult, op1=ALU.add),
                         ("convbuf", "cws", "cacc%d" % grp), ("cacc%d" % grp,))
            P.op(S, lambda e, p=p: e.activation(out=qkT[:, p, :, :], in_=cacc[:], func=AF.Silu), tuple("cacc%d" % g_ for g_ in range(4)), ("qkT%d" % p,))
            for c in range(2):
                P.op(TE, lambda e, p=p, c=c: e.transpose(out=ps5[:, c * 128:(c + 1) * 128], in_=qkT[:, p, 2 + c, :], identity=identb[:]), ("qkT%d" % p, "identb"), ("ps5",))
            P.op(V, lambda e, p=p: e.tensor_copy(out=ktok[:, p, :], in_=ps5[:, 0:256]), ("ps5",), ("ktok%d" % p,))
            for (pb, pk_, c0, n) in ((ps7, "ps7", 0, TMA), (ps7, "ps7", TMA, TMB), (ps7, "ps7", TMA + TMB, TMC)):
                for k in range(8):
                    P.op(TE, lambda e, p=p, k=k, pb=pb, c0=c0, n=n: e.matmul(pb[:, 0:n], lhsT=hT[:, p, k, :], rhs=Wtm[:, k, c0:c0 + n], start=(k == 0), stop=(k == 7)),
                         (HT, "W"), (pk_,))
                if c0 == 0:
                    P.op(V, lambda e, p=p: e.tensor_copy(out=Vaug[:, p, 0:256], in_=ps7[:, 0:256]), ("ps7",), ("Vaug%d" % p,))
                    P.op(S, lambda e, p=p: e.activation(out=og[:, p, :], in_=ps7[:, 256:512], func=AF.Sigmoid), ("ps7",), ("og%d" % p,))
                elif c0 == TMA:
                    P.op(S, lambda e, p=p: e.activation(out=zg[:, p, :], in_=ps7[:, 0:256], func=AF.Silu), ("ps7",), ("zg%d" % p,))
                    P.op(G, lambda e, p=p: e.tensor_tensor(out=og[:, p, :], in0=og[:, p, :], in1=zg[:, p, :], op=ALU.mult), ("og%d" % p, "zg%d" % p), ("og%d" % p,))
                    if do_nsa:
                        P.op(V, lambda e, p=p: e.tensor_copy(out=qtok[:, p, 2:4, :], in_=ps7[:, B_QOTH:B_QOTH + 128].rearrange("p (h d) -> p h d", h=2)), ("ps7",), ("qtok%d" % p,))
                        P.op(S, lambda e, p=p: e.activation(out=nzg[:, p, :], in_=ps7[:, B_NZ:B_NZ + 128], func=AF.Silu), ("ps7",), ("nzg%d" % p,))
                else:
                    P.op(V, lambda e, p=p: e.tensor_copy(out=tmC[:, p, :], in_=ps7[:, 0:TMC]), ("ps7",), ("tmC%d" % p,))

        def emit_B(i):
            p = i % 2
            t0 = i * 128
            SM = "sm%d" % p
            smp = sm[:, p, :]
            TC = "tmC%d" % p
            P.op(V, lambda e, smp=smp, p=p: e.tensor_scalar(out=smp[:, 17:18], in0=tmC[:, p, C_F:C_F + 1], scalar1=misc[:, 0:1], scalar2=None, op0=ALU.add), (TC, "misc"), (SM + "d0",))
            P.op(S, lambda e, smp=smp: e.activation(out=smp[:, 3:4], in_=smp[:, 17:18], func=AF.Exp, scale=-1.0), (SM + "d0",), (SM + "d",))
            P.op(S, lambda e, p=p, smp=smp: e.activation(out=G2[:, p, 0:1], in_=smp[:, 3:4], func=AF.Ln, scale=1.0, bias=1.0), (SM + "d",), ("G2%d" % p,))
            P.op(V, lambda e, p=p: e.tensor_scalar(out=G2[:, p, 1:2], in0=tmC[:, p, C_I:C_I + 1], scalar1=misc[:, 1:2], scalar2=None, op0=ALU.add), (TC, "misc"), ("G2%d" % p,))
            P.op(TE, lambda e, p=p: e.matmul(ps6[:, 128:130], lhsT=Utri, rhs=G2[:, p, :], start=True, stop=True), ("G2%d" % p, "c128"), ("ps6",))
            P.op(TE, lambda e, p=p: e.matmul(ps6[:, 130:132], lhsT=ones, rhs=G2[:, p, :], start=True, stop=True), ("G2%d" % p, "c128"), ("ps6",))
            P.op(V, lambda e, smp=smp: e.tensor_copy(out=smp[:, 20:24], in_=ps6[:, 128:132]), ("ps6",), (SM + "cum",))
            P.op(V, lambda e, p=p, smp=smp: e.tensor_tensor(out=smp[:, 4:5], in0=smp[:, 20:21], in1=G2[:, p, 1:2], op=ALU.add), (SM + "cum", "G2%d" % p), (SM + "e",))
            P.op(V, lambda e, smp=smp: e.tensor_scalar(out=smp[:, 5:6], in0=smp[:, 4:5], scalar1=-LN16, scalar2=None, op0=ALU.add), (SM + "e",), (SM + "f",))
            P.op(V, lambda e, smp=smp: e.tensor_scalar(out=dbg[:, 0, :], in0=ident, scalar1=smp[:, 4:5], scalar2=None, op0=ALU.mult), (SM + "e", "c128"), ("dbg0",))
            P.op(TE, lambda e: e.matmul(ps6[:, 0:128], lhsT=ones, rhs=dbg[:, 0, :], start=True, stop=True), ("dbg0", "c128"), ("ps6",))
            P.op(V, lambda e: e.tensor_tensor(out=Bm[:], in0=ps6[:, 0:128], in1=cnegB, op=ALU.add), ("ps6", "c128"), ("Bm",))
            P.op(V, lambda e, smp=smp: e.reduce_max(out=smp[:, 6:7], in_=Bm[:], axis=AX.X), ("Bm",), (SM + "g",))
            P.op(V, lambda e, smp=smp: e.reduce_max(out=smp[:, 7:8], in_=ps6[:, 0:128], axis=AX.X), ("ps6",), (SM + "h",))
            P.op(V, lambda e, smp=smp: e.tensor_tensor(out=smp[:, 8:9], in0=smp[:, 6:7], in1=mcol[:], op=ALU.max), (SM + "g", "m"), (SM + "i",))
            P.op(V, lambda e, smp=smp: e.tensor_tensor(out=smp[:, 9:10], in0=smp[:, 7:8], in1=mcol[:], op=ALU.max), (SM + "h", "m"), (SM + "j",))
            P.op(V, lambda e, smp=smp: e.tensor_scalar(out=dbg[:, 1, :], in0=ident, scalar1=smp[:, 8:9], scalar2=None, op0=ALU.mult), (SM + "i", "c128"), ("dbg1",))
            P.op(TE, lambda e: e.matmul(ps6[:, 0:128], lhsT=ones, rhs=dbg[:, 1, :], start=True, stop=True), ("dbg1", "c128"), ("ps6",))
            P.op(S, lambda e, smp=smp: e.activation(out=DT[:], in_=ps6[:, 0:128], func=AF.Exp, scale=-1.0, bias=smp[:, 5:6]), ("ps6", SM + "f"), ("DT",))
            P.op(G, lambda e: e.tensor_tensor(out=DTm[:], in0=DT[:], in1=maskT01, op=ALU.mult), ("DT", "c128"), ("DTm",))
            P.op(S, lambda e: e.activation(out=Wm[:], in_=ps6[:, 0:128], func=AF.Exp, scale=-1.0, bias=mcol[:]), ("ps6", "m"), ("Wm",))
            for c in range(2):
                P.op(G, lambda e, p=p, c=c: e.tensor_tensor(out=qwT[:, c, :], in0=qkT[:, p, c, :], in1=Wm[:], op=ALU.mult), ("qkT%d" % p, "Wm"), ("qwT",))
            for c in range(2):
                P.op(TE, lambda e, p=p, c=c: e.matmul(ps6[:, 0:128], lhsT=qkT[:, p, 2 + c, :], rhs=qkT[:, p, c, :], start=(c == 0), stop=(c == 1)), ("qkT%d" % p,), ("ps6",))
            P.op(V, lambda e: e.tensor_tensor(out=smT[:], in0=ps6[:, 0:128], in1=DTm[:], op=ALU.mult), ("ps6", "DTm"), ("smT",))
            for c in range(2):
                P.op(TE, lambda e, c=c: e.matmul(ps6[:, 132:389], lhsT=qwT[:, c, :], rhs=Cb[:, c, :], start=(c == 0), stop=False), ("qwT", "Cb"), ("ps6",))
            P.op(TE, lambda e, p=p: e.matmul(ps6[:, 132:389], lhsT=smT[:], rhs=Vaug[:, p, :], start=False, stop=True), ("smT", "Vaug%d" % p), ("ps6",))
            P.op(V, lambda e, smp=smp: e.tensor_tensor(out=smp[:, 10:11], in0=smp[:, 20:21], in1=smp[:, 8:9], op=ALU.subtract), (SM + "cum", SM + "i"), (SM + "k",))
            P.op(S, lambda e, smp=smp: e.activation(out=smp[:, 11:12], in_=smp[:, 10:11], func=AF.Exp), (SM + "k",), (SM + "l",))
            P.op(S, lambda e, smp=smp: e.activation(out=smp[:, 19:20], in_=ps6[:, 388:389], func=AF.Abs), ("ps6",), (SM + "m0",))
            P.op(V, lambda e, smp=smp: e.tensor_tensor(out=smp[:, 12:13], in0=smp[:, 19:20], in1=smp[:, 11:12], op=ALU.max), (SM + "m0", SM + "l"), (SM + "m",))
            P.op(V, lambda e, smp=smp: e.reciprocal(out=smp[:, 13:14], in_=smp[:, 12:13]), (SM + "m",), (SM + "n",))
            P.op(V, lambda e, smp=smp: e.tensor_scalar(out=hbuf[:], in0=ps6[:, 132:388], scalar1=smp[:, 13:14], scalar2=None, op0=ALU.mult), ("ps6", SM + "n"), ("hbuf",))
            P.op(S, lambda e, smp=smp: e.activation(out=smp[:, 15:16], in_=smp[:, 9:10], func=AF.Exp, scale=-1.0, bias=smp[:, 5:6]), (SM + "j", SM + "f"), (SM + "p",))
            P.op(S, lambda e, smp=smp: e.activation(out=smp[:, 16:17], in_=smp[:, 9:10], func=AF.Exp, scale=-1.0, bias=mcol[:]), (SM + "j", "m"), (SM + "q",))
            P.op(V, lambda e, smp=smp, p=p: e.tensor_scalar(out=kwt[:], in0=ktok[:, p, :], scalar1=smp[:, 15:16], scalar2=None, op0=ALU.mult), ("ktok%d" % p, SM + "p"), ("kwt",))
            for c in range(2):
                P.op(TE, lambda e, p=p, c=c: e.matmul(ps6[:, 132:389], lhsT=kwt[:, c * 128:(c + 1) * 128], rhs=Vaug[:, p, :], start=True, stop=True), ("kwt", "Vaug%d" % p), ("ps6",))
                P.op(V, lambda e, c=c, smp=smp: e.scalar_tensor_tensor(out=Cst[:, c, :], in0=Cst[:, c, :], scalar=smp[:, 16:17], in1=ps6[:, 132:389], op0=ALU.mult, op1=ALU.add),
                     ("Cst", SM + "q", "ps6"), ("Cst",))
                P.op(S, lambda e, c=c: e.activation(out=Cb[:, c, :], in_=Cst[:, c, :], func=AF.Copy), ("Cst",), ("Cb",))
            P.op(V, lambda e, smp=smp: e.tensor_tensor(out=mcol[:], in0=smp[:, 9:10], in1=smp[:, 22:23], op=ALU.subtract), (SM + "j", SM + "cum"), ("m",))
            P.op(V, lambda e: e.bn_stats(out=bnst[:, 0:6], in_=hbuf[:]), ("hbuf",), ("bnst",))
            P.op(V, lambda e: e.bn_aggr(out=bnst[:, 6:8], in_=bnst[:, 0:6]), ("bnst",), ("bnst",))
            P.op(S, lambda e, smp=smp: e.activation(out=smp[:, 18:19], in_=bnst[:, 7:8], func=AF.Sqrt, scale=1.0, bias=epsc[:]), ("bnst", "epsc"), (SM + "o0",))
            P.op(V, lambda e, smp=smp: e.reciprocal(out=smp[:, 14:15], in_=smp[:, 18:19]), (SM + "o0",), (SM + "o",))
            P.op(V, lambda e, smp=smp: e.tensor_scalar(out=hbuf2[:], in0=hbuf[:], scalar1=bnst[:, 6:7], scalar2=smp[:, 14:15], op0=ALU.subtract, op1=ALU.mult), ("hbuf", "bnst", SM + "o"), ("hbuf2",))
            P.op(G, lambda e: e.tensor_tensor(out=hbuf2[:], in0=hbuf2[:], in1=mhg[:], op=ALU.mult), ("hbuf2", "mhg"), ("hbuf2",))
            P.op(G, lambda e, p=p: e.tensor_tensor(out=yat[:, p, :], in0=hbuf2[:], in1=og[:, p, :], op=ALU.mult), ("hbuf2", "og%d" % p), ("yat%d" % p,))
            P.dma(SY, lambda e, p=p, t0=t0: e.dma_start(out=ya_d[t0:t0 + 128, :], in_=yat[:, p, :]), "ya%d" % p, ("yat%d" % p,), ("ya_out",))

        def emit_C(i):
            p = i % 2
            t0 = i * 128
            TC = "tmC%d" % p
            if do_nsa:
                psS = (ps[0], ps[1])
                psO = (ps[2], ps[3])
                QT = "qtok%d" % p
                P.op(S, lambda e, p=p: e.activation(out=qtok[:, p, 0:2, :], in_=tmC[:, p, 0:128].rearrange("p (h d) -> p h d", h=2), func=AF.Copy), (TC,), (QT,))
                P.op(S, lambda e, p=p: e.activation(out=rp[:], in_=tmC[:, p, 0:256].rearrange("p (h d) -> p h d", h=4), func=AF.Copy), (TC,), ("rp",))
                src = tmC[:, p, 0:256].rearrange("p (h d) -> p h d", h=4)
                cosb = cossin[:, i, 0:8].unsqueeze(1).to_broadcast([128, 4, 8])
                sinb = cossin[:, i, 8:16].unsqueeze(1).to_broadcast([128, 4, 8])
                P.op(V, lambda e, src=src, cosb=cosb: e.tensor_tensor(out=rtmp[:, 0], in0=src[:, :, 0:8], in1=cosb, op=ALU.mult), (TC, "cossin"), ("rtmp0",))
                P.op(V, lambda e, src=src, sinb=sinb: e.tensor_tensor(out=rtmp[:, 1], in0=src[:, :, 8:16], in1=sinb, op=ALU.mult), (TC, "cossin"), ("rtmp1",))
                P.op(V, lambda e, src=src, cosb=cosb: e.tensor_tensor(out=rtmp[:, 2], in0=src[:, :, 8:16], in1=cosb, op=ALU.mult), (TC, "cossin"), ("rtmp2",))
                P.op(V, lambda e, src=src, sinb=sinb: e.tensor_tensor(out=rtmp[:, 3], in0=src[:, :, 0:8], in1=sinb, op=ALU.mult), (TC, "cossin"), ("rtmp3",))
                P.op(V, lambda e: e.tensor_tensor(out=rp[:, :, 0:8], in0=rtmp[:, 0], in1=rtmp[:, 1], op=ALU.subtract), ("rtmp0", "rtmp1", "rp"), ("rp",))
                P.op(V, lambda e: e.tensor_tensor(out=rp[:, :, 8:16], in0=rtmp[:, 2], in1=rtmp[:, 3], op=ALU.add), ("rtmp2", "rtmp3", "rp"), ("rp",))
                P.op(V, lambda e, i=i, p=p: e.tensor_copy(out=Vs[:, i, 0:64], in_=tmC[:, p, C_VS:C_VS + 64]), (TC,), ("Vs%d" % i,))
                P.op(V, lambda e, i=i, p=p: e.tensor_copy(out=Vw[:, i % 8, 0:64], in_=tmC[:, p, C_VW:C_VW + 64]), (TC,), ("Vw%d" % (i % 8),))
                P.op(V, lambda e, p=p: e.tensor_tensor(out=gsig[:, 0:6], in0=tmC[:, p, C_G:C_G + 6], in1=misc[:, 2:8], op=ALU.add), (TC, "misc"), ("gsig",))
                P.op(S, lambda e: e.activation(out=gsig[:, 0:6], in_=gsig[:, 0:6], func=AF.Sigmoid), ("gsig",), ("gsig",))
                for h in range(4):
                    P.op(TE, lambda e, h=h, p=p: e.transpose(out=ps[0][0:64, h * 128:(h + 1) * 128], in_=qtok[:, p, h, :], identity=ident), (QT, "c128"), ("ps0",))
                for h in range(4):
                    P.op(TE, lambda e, h=h: e.transpose(out=ps[1][0:64, h * 128:(h + 1) * 128], in_=rp[:, h, :], identity=ident), ("rp", "c128"), ("ps1",))
                P.op(V, lambda e: e.tensor_copy(out=qrawT[:], in_=ps[0][0:64, 0:512]), ("ps0",), ("qrawT",))
                P.op(S, lambda e: e.activation(out=qropT[:], in_=ps[1][0:64, 0:256], func=AF.Copy), ("ps1",), ("qropT",))
                P.op(V, lambda e, t0=t0: e.tensor_copy(out=ksT[:, t0:t0 + 128], in_=ps[1][0:64, 256:384]), ("ps1",), ("ksT%d" % i,))
                P.op(S, lambda e, i=i: e.activation(out=kwT[:, (i % 8) * 128:(i % 8 + 1) * 128], in_=ps[1][0:64, 384:512], func=AF.Copy), ("ps1",), ("kwT%d" % (i % 8),))
                for kv in range(2):
                    for l in range(32):
                        P.op(TE, lambda e, kv=kv, l=l, p=p: e.matmul(ps[4][:, 256 + kv * 8:264 + kv * 8], lhsT=W1b[:, kv, l, :], rhs=cpre[:, p, kv, l:l + 113:16], start=(l == 0), stop=(l == 31)),
                             ("W1b", "cpre%d" % p), ("ps4",))
                hid = ps[4][:, 256:272].rearrange("p (k n) -> p k n", k=2)
                P.op(V, lambda e, hid=hid: e.tensor_tensor(out=hx[:, 0, :].rearrange("p (k n) -> p k n", k=2), in0=hid, in1=biash[:].unsqueeze(2).to_broadcast([128, 2, 8]), op=ALU.add), ("ps4", "biash"), ("hx0",))
                P.op(V, lambda e: e.tensor_tensor(out=hx[:, 1, :], in0=hx[:, 0, :], in1=hx[:, 0, :], op=ALU.mult), ("hx0",), ("hx1",))
                P.op(V, lambda e: e.tensor_scalar(out=hx[:, 1, :], in0=hx[:, 1, :], scalar1=0.044715, scalar2=1.0, op0=ALU.mult, op1=ALU.add), ("hx1",), ("hx1",))
                P.op(V, lambda e: e.tensor_tensor(out=hx[:, 2, :], in0=hx[:, 1, :], in1=hx[:, 0, :], op=ALU.mult), ("hx1", "hx0"), ("hx2",))
                P.op(S, lambda e: e.activation(out=hx[:, 3, :], in_=hx[:, 2, :], func=AF.Sigmoid, scale=1.5957691216057308), ("hx2",), ("hx3",))
                P.op(V, lambda e: e.tensor_tensor(out=gl[:], in0=hx[:, 3, :], in1=hx[:, 0, :], op=ALU.mult), ("hx3", "hx0"), ("gl",))
                P.op(TE, lambda e: e.matmul(ps[4][0:64, 320:328], lhsT=W2b[:, 0, :], rhs=gl[:, 0:8], start=True, stop=True), ("W2b", "gl"), ("ps4",))
                P.op(V, lambda e, i=i: e.tensor_copy(out=kcT[:, 8 * i:8 * i + 8], in_=ps[4][0:64, 320:328]), ("ps4",), ("kcT",))
                P.op(V, lambda e, i=i: e.tensor_copy(out=GV[:, 8 * i:8 * i + 8], in_=gl[:, 8:16]), ("gl",), ("GV",))
                if i == 0:
                    P.op(V, lambda e: e.memset(GV[:, 0:1], 0.0), (), ("GV",))
                cc_cur = i // 16
                P.op(TE, lambda e, cc_cur=cc_cur: e.matmul(ps[4][:, 336:400], lhsT=GV[:, cc_cur * 128:(cc_cur + 1) * 128], rhs=W2b[:, 1, :], start=True, stop=True), ("GV", "W2b"), ("ps4",))
                P.op(V, lambda e, cc_cur=cc_cur: e.tensor_copy(out=Raug[:, cc_cur, 128:192], in_=ps[4][:, 336:400]), ("ps4",), ("Raug",))
                nch = cc_cur + 1
                for cc in range(nch):
                    P.op(TE, lambda e, cc=cc: e.matmul(psS[cc % 2][:, 0:512], lhsT=kcT[:, cc * 128:(cc + 1) * 128], rhs=qrawT[:], start=True, stop=True), ("kcT", "qrawT"), ("ps%d" % (cc % 2),))
                    P.op(S, lambda e, cc=cc: e.activation(out=PcT[:, cc, :], in_=psS[cc % 2][:, 0:512], func=AF.Exp, scale=0.125), ("ps%d" % (cc % 2),), ("PcT%d" % cc,))
                delta = 128.0 * (i % 16) - 15.0
                P.op(V, lambda e, delta=delta: e.tensor_scalar(out=cmask[:], in0=c128[:, 5, :], scalar1=delta, scalar2=None, op0=ALU.is_le), ("c128",), ("cmask",))
                P.op(G, lambda e, cc_cur=cc_cur: e.tensor_tensor(out=PcT[:, cc_cur, :].rearrange("p (h t) -> p h t", h=4), in0=PcT[:, cc_cur, :].rearrange("p (h t) -> p h t", h=4),
                                                                  in1=cmask[:].unsqueeze(1).to_broadcast([128, 4, 128]), op=ALU.mult), ("PcT%d" % cc_cur, "cmask"), ("PcT%d" % cc_cur,))
                for h in range(4):
                    po = psO[h % 2]
                    pk = "ps%d" % (2 + h % 2)
                    s0 = (h // 2) * 256
                    for cc in range(nch):
                        P.op(TE, lambda e, h=h, cc=cc, po=po, s0=s0: e.matmul(po[:, s0:s0 + 193], lhsT=PcT[:, cc, h * 128:(h + 1) * 128], rhs=Raug[:, cc, :], start=(cc == 0), stop=(cc == nch - 1)),
                             ("PcT%d" % cc, "Raug"), (pk,))
                for b_ in range(2):
                    P.op(V, lambda e, b_=b_: e.tensor_scalar(out=rr[:, b_:4:2], in0=psO[b_][:, 192:449:256], scalar1=1e-30, scalar2=None, op0=ALU.max), ("ps%d" % (2 + b_),), ("rrm%d" % b_,))
                P.op(V, lambda e: e.reciprocal(out=rr[:, 0:4], in_=rr[:, 0:4]), ("rrm0", "rrm1"), ("rr",))
                for h in range(4):
                    po = psO[h % 2]
                    pk = "ps%d" % (2 + h % 2)
                    s0 = (h // 2) * 256
                    if h == 0:
                        P.op(V, lambda e, po=po: e.tensor_scalar(out=imp[:], in0=po[:, 0:128], scalar1=rr[:, 0:1], scalar2=None, op0=ALU.mult), (pk, "rr"), ("imp",))
                    else:
                        P.op(V, lambda e, h=h, po=po, s0=s0: e.scalar_tensor_tensor(out=imp[:], in0=po[:, s0:s0 + 128], scalar=rr[:, h:h + 1], in1=imp[:], op0=ALU.mult, op1=ALU.add), (pk, "rr", "imp"), ("imp",))
                P.op(V, lambda e: e.tensor_tensor(out=rr[:, 4:6], in0=rr[:, 0:2], in1=gsig[:, 0:6:3], op=ALU.mult), ("rr", "gsig"), ("rg",))
                for h in range(2):
                    P.op(S, lambda e, h=h: e.activation(out=ynacc[:, h, :], in_=psO[h][:, 128:192], func=AF.Copy, scale=rr[:, 4 + h:5 + h]), ("ps%d" % (2 + h), "rg"), ("ynacc%d" % h,))
                w0 = 128 - 2 * i
                P.op(V, lambda e, w0=w0: e.scalar_tensor_tensor(out=sc[:, 0, :], in0=imp[:], scalar=1.0, in1=selc[:, 0, w0:w0 + 128], op0=ALU.add, op1=ALU.mult), ("imp", "selc"), ("sc0",))
                P.op(V, lambda e, w0=w0: e.tensor_tensor(out=sc[:, 0, :], in0=sc[:, 0, :], in1=selc[:, 2, w0:w0 + 128], op=ALU.max), ("sc0", "selc"), ("sc0",))
                P.op(V, lambda e: e.tensor_tensor(out=sc[:, 0, :], in0=sc[:, 0, :], in1=selc[:, 3, 0:128], op=ALU.max), ("sc0", "selc"), ("sc0",))
                P.op(V, lambda e: e.max(out=m8[:, 0:8], in_=sc[:, 0, :]), ("sc0",), ("m8a",))
                P.op(V, lambda e: e.match_replace(out=sc[:, 1, :], in_to_replace=m8[:, 0:8], in_values=sc[:, 0, :], imm_value=-1e30), ("sc0", "m8a"), ("sc1",))
                P.op(V, lambda e: e.max(out=m8[:, 8:16], in_=sc[:, 1, :]), ("sc1",), ("m8b",))
                P.op(V, lambda e: e.tensor_scalar(out=NegSel[:], in0=sc[:, 0, :], scalar1=m8[:, 15:16], scalar2=MASKNEG, op0=ALU.is_lt, op1=ALU.mult), ("sc0", "m8b"), ("NegSel",))
                P.op(TE, lambda e: e.transpose(out=ps[0][:, 0:128], in_=NegSel[:], identity=ident), ("NegSel", "c128"), ("ps0",))
                P.op(V, lambda e: e.tensor_copy(out=NegSelT[:, 0:128], in_=ps[0][:, 0:128]), ("ps0",), ("NegSelT",))
                P.op(S, lambda e: e.activation(out=NegSelT[:, 128:256], in_=ps[0][:, 0:128], func=AF.Copy), ("ps0",), ("NegSelT",))
                def attn_branch(kbs, kT_, kkey, Vt, vkey, gate_col, use_sel, ring=None):
                    sl = (lambda kb_: kb_ % ring) if ring else (lambda kb_: kb_)
                    nk = len(kbs)
                    chunks = [kbs[c_:c_ + 2] for c_ in range(0, nk, 2)]
                    done = 0
                    for n_, ch in enumerate(chunks):
                        pS = psS[n_ % 2]
                        sk_ = "ps%d" % (n_ % 2)
                        for u, kb in enumerate(ch):
                            extra = []
                            if use_sel:
                                extra.append((Eb[:, kb * 128:(kb + 1) * 128], NegSelT[:], ("Eb", "NegSelT")))
                            if kb == i:
                                extra.append((identb[:], negm[:, 0, :], ("identb", "negm")))
                            if (not use_sel) and kb == i - 4:
                                extra.append((identb[:], negm[:, 1, :], ("identb", "negm")))
                            P.op(TE, lambda e, kb=kb, pS=pS, u=u, ne=len(extra): e.matmul(pS[:, u * 256:(u + 1) * 256], lhsT=kT_[:, sl(kb) * 128:(sl(kb) + 1) * 128], rhs=qropT[:], start=True, stop=(ne == 0)),
                                 ("%s%d" % (kkey, sl(kb)), "qropT"), (sk_,))
                            for xi, (l_, r_, keys) in enumerate(extra):
                                P.op(TE, lambda e, l_=l_, r_=r_, pS=pS, u=u, last=(xi == len(extra) - 1): e.matmul(pS[:, u * 256:(u + 1) * 256], lhsT=l_, rhs=r_, start=False, stop=last), keys, (sk_,))
                        w_ = 256 * len(ch)
                        P.op(S, lambda e, pS=pS, n_=n_, w_=w_: e.activation(out=PT[:, n_ % 2, 0:w_], in_=pS[:, 0:w_], func=AF.Exp, scale=0.125), (sk_,), ("PT%d" % (n_ % 2),))
                        for u, kb in enumerate(ch):
                            for h in range(2):
                                P.op(TE, lambda e, h=h, kb=kb, n_=n_, u=u, first=(done == 0), last=(done == nk - 1): e.matmul(psO[h][:, 0:65], lhsT=PT[:, n_ % 2, u * 256 + h * 128:u * 256 + (h + 1) * 128], rhs=Vt[:, sl(kb), :], start=first, stop=last),
                                     ("PT%d" % (n_ % 2), "%s%d" % (vkey, sl(kb))), ("ps%d" % (2 + h),))
                            done += 1
                        if H3_TEST and use_sel and n_ == 0 and nk > 2:
                            P.op(TE, lambda e: e.transpose(out=ps5[:, 0:128], in_=identb[:], identity=identb[:]), ("identb",), ("ps5",))
                    for h in range(2):
                        pk = "ps%d" % (2 + h)
                        P.op(V, lambda e, h=h: e.reciprocal(out=rr[:, 8 + h:9 + h], in_=psO[h][:, 64:65]), (pk,), ("rs%d" % h,))
                        P.op(V, lambda e, h=h: e.tensor_tensor(out=rr[:, 8 + h:9 + h], in0=rr[:, 8 + h:9 + h], in1=gsig[:, 3 * h + gate_col:3 * h + gate_col + 1], op=ALU.mult), ("rs%d" % h, "gsig"), ("rs%d" % h,))
                        P.op(V, lambda e, h=h: e.scalar_tensor_tensor(out=ynacc[:, h, :], in0=psO[h][:, 0:64], scalar=rr[:, 8 + h:9 + h], in1=ynacc[:, h, :], op0=ALU.mult, op1=ALU.add),
                             (pk, "rs%d" % h, "ynacc%d" % h), ("ynacc%d" % h,))

                marks.append(len(P.cap) if P.cap is not None else 0)
                attn_branch(list(range(i + 1)), ksT, "ksT", Vs, "Vs", 1, True)
                attn_branch(list(range(max(0, i - 4), i + 1)), kwT, "kwT", Vw, "Vw", 2, False, ring=8)
                P.op(G, lambda e, p=p: e.tensor_tensor(out=ybt[:, p, :], in0=ynacc[:].rearrange("p h d -> p (h d)"), in1=nzg[:, p, :], op=ALU.mult), ("ynacc0", "ynacc1", "nzg%d" % p), ("ybt%d" % p,))
                P.dma(SY, lambda e, p=p, t0=t0: e.dma_start(out=yb_d[t0:t0 + 128, :], in_=ybt[:, p, :]), "yb%d" % p, ("ybt%d" % p,), ("yb_out",))

        marks = []
        emit_A(0)
        for i in range(NT):
            if MERGE_MODE == 0:
                emit_B(i)
                if do_nsa:
                    emit_C(i)
                if i + 1 < NT:
                    emit_A(i + 1)
                continue
            P.cap = []
            emit_B(i)
            LB = P.cap
            P.cap = []
            if i + 1 < NT:
                emit_A(i + 1)
            LA = P.cap
            P.cap = []
            del marks[:]
            emit_C(i)
            LY = P.cap
            P.cap = None
            mk = marks[0]
            for kind, args in LY[:mk]:
                (P.op if kind == "op" else P.dma)(*args)
            P.replay_merged(LB, LA, LY[mk:])

    for j in range(4):
        wfm_d, wtm_d, cw_d, cb_d, misc_d, mhg_d = per[j]
        run_pass(wfm_d, wtm_d, cw_d, cb_d, misc_d, mhg_d, ya_s[:, 256 * j:256 * (j + 1)], yb_s[:, 128 * j:128 * (j + 1)])
    P.barrier()
    P.emit(es)
    es.close()
    P2 = Prog(nc)
    es2 = ExitStack()
    keys = emit_tail(nc, P2, es2, NT // 4, xq_d, None, None, wg_d, wa_d, wb_d, wo_d, ng_d, fg_d, c128_d, out_d,
                     sel4=(oh_d, lambda q_, i: ya_s[q_ * TQ + i * 128:q_ * TQ + (i + 1) * 128, :],
                           lambda q_, i: yb_s[q_ * TQ + i * 128:q_ * TQ + (i + 1) * 128, :]))
    P2.final_wait("sync", keys)
    P2.emit(es2)
    es2.close()
    return nc


def fused_inputs(inp, b, T):
    m0 = mixer_inputs(inp, b, 0, T)
    m = {k: m0[k] for k in ("x", "pos", "ng", "c128", "cw1", "cw2", "cpos", "raug", "E", "selc", "negm")}
    for j in range(4):
        mj = mixer_inputs(inp, b, j, T) if j else m0
        for k in ("wfm", "wtm", "convw", "convb", "misc", "mhg"):
            m["%s_%d" % (k, j)] = mj[k]
    w = inp["w_in"][0]
    m["wg"] = np.ascontiguousarray(w[:, 8992 - 2048:])
    m["wa"] = np.ascontiguousarray(inp["w_branch_a"][0])
    m["wb"] = np.ascontiguousarray(inp["w_branch_b"][0])
    m["wo"] = np.ascontiguousarray(inp["w_out"][0])
    m["fg"] = np.ascontiguousarray(np.broadcast_to(inp["final_norm_g"][None, :], (128, 1024)))
    return m


def kernel_fused(**inputs):
    inp = {k: np.asarray(v) for k, v in inputs.items()}
    T = inp["x"].shape[1]
    if ("fused", T) not in _CACHE:
        _CACHE[("fused", T)] = build_fused(T)
    nc = _CACHE[("fused", T)]
    per_b = [fused_inputs(inp, b, T) for b in range(2)]
    TQ = T // 4
    maps = []
    for c in range(8):
        m = dict(per_b[c // 4])
        q = c % 4
        m["xq"] = np.ascontiguousarray(inp["x"][c // 4, q * TQ:(q + 1) * TQ])
        oh = np.zeros((128, 4), np.float32)
        oh[:, q] = 1.0
        m["onehot"] = oh
        maps.append(m)
    res = run_bass_kernel_spmd(nc, maps, core_ids=list(range(8)))
    out = np.stack([np.concatenate([np.asarray(res.results[4 * b + q]["out"]) for q in range(4)], axis=0) for b in range(2)], axis=0)
    return out.astype(np.float32)


def kernel(**inputs):
    return kernel_fused(**inputs)
```

```python
import math
from contextlib import ExitStack

import numpy as np
import concourse.bass as bass
import concourse.mybir as mybir
from concourse.bass_utils import run_bass_kernel_spmd

F32 = mybir.dt.float32
BF16 = mybir.dt.bfloat16
I32 = mybir.dt.int32
AF = mybir.ActivationFunctionType
ALU = mybir.AluOpType
AX = mybir.AxisListType

D = 1024
NEGB = -1.0e30
MASKNEG = -16384.0
LN16 = math.log(16.0)
EPS = 1e-6
ENGS = ("tensor", "vector", "scalar", "gpsimd", "sync")
SEM_ROLL = 12000
import os
SEQ_MERGE = False
MERGE_MODE = int(os.environ.get('MERGE_MODE', '3'))
H3_TEST = 0
TAIL_MERGE = bool(int(os.environ.get('TAIL_MERGE', '1')))


class Prog:
    def __init__(self, nc):
        self.nc = nc
        self.q = {e: [] for e in ENGS}
        self.cnt = {e: 0 for e in ENGS}
        self.gen = {e: 0 for e in ENGS}
        self.last_w = {}
        self.readers = {}
        self.seen = {e: {} for e in ENGS}
        self.dma_cnt = {}
        self.semnames = []
        self.cap = None

    def _deps(self, eng, reads, writes):
        deps = []
        for r in reads:
            if r in self.last_w:
                deps.append(self.last_w[r])
        for w in writes:
            if w in self.last_w:
                deps.append(self.last_w[w])
            deps += self.readers.get(w, [])
        waits = {}
        for sk, v in deps:
            if v > waits.get(sk, 0):
                waits[sk] = v
        out = []
        for sk, v in waits.items():
            if self.seen[eng].get(sk, 0) >= v:
                continue
            self.seen[eng][sk] = v
            out.append((sk, v))
        return out

    def _commit(self, token, reads, writes):
        for r in reads:
            self.readers.setdefault(r, []).append(token)
        for w in writes:
            self.last_w[w] = token
            self.readers[w] = []

    def _semkey(self, name):
        if name not in self.semnames:
            self.semnames.append(name)
        return name

    def replay_merged(self, *lists):
        lists = [l for l in lists if l]
        pos = [0] * len(lists)
        while True:
            best, bf = -1, 2.0
            for j, l in enumerate(lists):
                if pos[j] < len(l):
                    f = pos[j] / float(len(l))
                    if f < bf:
                        best, bf = j, f
            if best < 0:
                break
            kind, args = lists[best][pos[best]]
            pos[best] += 1
            (self.op if kind == "op" else self.dma)(*args)

    def op(self, eng, fn, reads=(), writes=()):
        if self.cap is not None:
            self.cap.append(("op", (eng, fn, reads, writes)))
            return
        waits = self._deps(eng, reads, writes)
        self.cnt[eng] += 1
        idx = self.cnt[eng]
        if eng == "tensor":
            self.seen[eng][eng] = idx
        self.q[eng].append((fn, waits, eng, 1, idx))
        self._commit((eng, idx), reads, writes)

    def dma(self, eng, fn, sem, reads=(), writes=()):
        if self.cap is not None:
            self.cap.append(("dma", (eng, fn, sem, reads, writes)))
            return
        waits = self._deps(eng, reads, writes)
        sk = self._semkey("d_" + sem)
        self.dma_cnt[sk] = self.dma_cnt.get(sk, 0) + 16
        token = (sk, self.dma_cnt[sk])
        self.q[eng].append((fn, waits, sk, 16, 0))
        self._commit(token, reads, writes)

    def final_wait(self, eng, keys):
        waits = self._deps(eng, keys, ())
        have = dict(waits)
        for sk, cnt in self.dma_cnt.items():
            if have.get(sk, 0) < cnt:
                have[sk] = cnt
        self.q[eng].append((None, list(have.items()), None, 0, 0))

    def plan(self):
        ms = {e: set() for e in ENGS}
        for e in ENGS:
            for fn, waits, sk, inc, idx in self.q[e]:
                for wk, v in waits:
                    if wk in ms:
                        ms[wk].add(v)
        rank = {}
        for e in ENGS:
            rank[e] = {idx: r for r, idx in enumerate(sorted(ms[e]))}
        return rank

    def emit(self, es):
        nc = self.nc
        rank = self.plan()
        names = list(self.semnames)
        for e in ENGS:
            ngen = (len(rank[e]) + SEM_ROLL - 1) // SEM_ROLL
            for g in range(ngen):
                names.append("e_%s_%d" % (e, g))
        sems = {}
        for name in names:
            sems[name] = es.enter_context(nc.semaphore(name))
        block = es.enter_context(nc.Block())
        q = self.q

        def esem(eng_name, idx):
            r = rank[eng_name][idx]
            return sems["e_%s_%d" % (eng_name, r // SEM_ROLL)], r % SEM_ROLL + 1

        def run(eng_name, e):
            for fn, waits, sk, inc, idx in q[eng_name]:
                for wk, v in waits:
                    if wk in rank:
                        sm_, val = esem(wk, v)
                        e.wait_ge(sm_, val)
                    else:
                        e.wait_ge(sems[wk], v)
                if fn is not None:
                    ins = fn(e)
                    if sk in rank:
                        if idx in rank[sk]:
                            ins.then_inc(esem(sk, idx)[0], 1)
                    else:
                        ins.then_inc(sems[sk], inc)

        @block.tensor
        def _(e):
            run("tensor", e)

        @block.vector
        def _(e):
            run("vector", e)

        @block.scalar
        def _(e):
            run("scalar", e)

        @block.gpsimd
        def _(e):
            run("gpsimd", e)

        @block.sync
        def _(e):
            run("sync", e)


NFM = 640
NTM = 1416
TMA, TMB, TMC = 512, 512, 392
C_QOWN, C_KS, C_KW, C_VS, C_VW, C_I, C_F, C_G = 0, 128, 192, 256, 320, 384, 385, 386
B_Z, B_QOTH, B_NZ = 0, 256, 384


def build_mixer(T, do_nsa=True):
    NT = T // 128
    nc = bass.Bass("TRN2", target_bir_lowering=False)
    P = Prog(nc)
    es = ExitStack()

    def din(name, shape, dt=F32):
        return nc.dram_tensor(name, list(shape), dt, kind="ExternalInput").ap()

    x_d = din("x", [T, D])
    pos_d = din("pos", [128, NT], I32)
    wfm_d = din("wfm", [D, NFM])
    wtm_d = din("wtm", [D, NTM])
    ng_d = din("ng", [128, 8])
    cw_d = din("convw", [128, 4, 4])
    cb_d = din("convb", [128, 4])
    misc_d = din("misc", [128, 16])
    mhg_d = din("mhg", [128, 256])
    c128_d = din("c128", [128, 6, 128])
    ya_d = nc.dram_tensor("ya", [T, 256], BF16, kind="ExternalOutput").ap()
    yb_d = nc.dram_tensor("yb", [T, 128], BF16, kind="ExternalOutput").ap()
    if do_nsa:
        w1_d = din("cw1", [64, 2, 32, 128])
        w2_d = din("cw2", [128, 2, 64])
        cpos_d = din("cpos", [64, 2, 32])
        raug_d = din("raug", [128, 4, 193])
        E_d = din("E", [128, T])
        sel_d = din("selc", [128, 4, 256])
        nm_d = din("negm", [128, 2, 256])

    def sb(name, shape, dt=F32):
        return es.enter_context(nc.sbuf_tensor(name, list(shape), dt))

    def pst(name, shape, dt=F32):
        return es.enter_context(nc.psum_tensor(name, list(shape), dt))

    Wfm = sb("Wfm", [128, 8, NFM], BF16)
    Wtm = sb("Wtm", [128, 8, NTM], BF16)
    stg = sb("stg", [128, 2, NTM], F32)
    ngs = sb("ngs", [128, 8])
    cws = sb("cws", [128, 4, 4])
    cbs = sb("cbs", [128, 4])
    misc = sb("miscs", [128, 16])
    mhg = sb("mhgs", [128, 256])
    c128 = sb("c128s", [128, 6, 128])
    identb = sb("identb", [128, 128], BF16)
    ident = c128[:, 0, :]
    Utri = c128[:, 1, :]
    ones = c128[:, 2, :]
    cnegB = c128[:, 3, :]
    maskT01 = c128[:, 4, :]
    xt = sb("xt", [128, 2, D])
    junk = sb("junk", [128, D], BF16)
    hn = sb("hn", [128, 2, D], BF16)
    hT = sb("hT", [128, 2, 8, 128], BF16)
    sm = sb("sm", [128, 2, 32])
    convbuf = sb("convbuf", [128, 4, 131])
    cacc = sb("cacc", [128, 4, 128])
    qkT = sb("qkT", [128, 2, 4, 128], BF16)
    Vaug = sb("Vaug", [128, 2, 257], BF16)
    og = sb("og", [128, 2, 256])
    zg = sb("zg", [128, 2, 256])
    G2 = sb("G2", [128, 2, 2])
    dbg = sb("dbg", [128, 2, 128])
    Bm = sb("Bm", [128, 128])
    DT = sb("DT", [128, 128])
    DTm = sb("DTm", [128, 128], BF16)
    Wm = sb("Wm", [128, 128])
    qwT = sb("qwT", [128, 2, 128], BF16)
    smT = sb("smT", [128, 128], BF16)
    hbuf = sb("hbuf", [128, 256])
    hbuf2 = sb("hbuf2", [128, 256])
    bnst = sb("bnst", [128, 8])
    kwt = sb("kwt", [128, 256], BF16)
    ktok = sb("ktok", [128, 2, 256], BF16)
    Cst = sb("Cst", [128, 2, 257])
    Cb = sb("Cb", [128, 2, 257], BF16)
    mcol = sb("mcol", [128, 1])
    epsc = sb("epsc", [128, 1])
    negpi = sb("negpi", [128, 1])
    tmC = sb("tmC", [128, 2, TMC])
    yat = sb("yat", [128, 2, 256], BF16)
    ybt = sb("ybt", [128, 2, 128], BF16)

    if do_nsa:
        W1b = sb("W1b", [64, 2, 32, 128], BF16)
        W2b = sb("W2b", [128, 2, 64], BF16)
        posT = sb("posT", [64, 2, 32], BF16)
        biash = sb("biash", [128, 2])
        cpre = sb("cpre", [64, 2, 2, 144], BF16)
        kcT = sb("kcT", [64, 512], BF16)
        GV = sb("GV", [128, 512], BF16)
        Raug = sb("Raug", [128, 4, 193], BF16)
        ksT = sb("ksT", [64, T], BF16)
        TLO = min(T, 4096)
        KE2 = sb("KE2", [128, TLO], BF16)
        QN2 = sb("QN2", [128, 256], BF16)
        rp2 = sb("rp2", [128, 4, 128])
        kwT = sb("kwT", [64, 8 * 128], BF16)
        Vs = sb("Vs", [128, NT, 65], BF16)
        Vw = sb("Vw", [128, 8, 65], BF16)
        Eb = sb("Eb", [128, T], BF16)
        selc = sb("selcs", [128, 4, 256])
        negm = sb("negms", [128, 2, 256], BF16)
        posi = sb("posi", [128, NT], I32)
        posf = sb("posf", [128, NT])
        ang = sb("ang", [128, NT, 8])
        ang2 = sb("ang2", [128, NT, 8])
        ang3 = sb("ang3", [128, NT, 8])
        angi = sb("angi", [128, NT, 8], I32)
        cossin = sb("cossin", [128, NT, 16])
        qtok = sb("qtok", [128, 2, 4, 64])
        rp = sb("rp", [128, 4, 64])
        rtmp = sb("rtmp", [128, 4, 4, 8])
        qrawT = sb("qrawT", [64, 512], BF16)
        qropT = sb("qropT", [64, 256], BF16)
        PcT = sb("PcT", [128, 4, 512], BF16)
        PT = sb("PT", [128, 2, 512], BF16)
        cmask = sb("cmask", [128, 128], BF16)
        hx = sb("hx", [128, 4, 16])
        gl = sb("gl", [128, 16], BF16)
        imp = sb("imp", [128, 128])
        sc = sb("sc", [128, 2, 128])
        m8 = sb("m8", [128, 16])
        sel01 = sb("sel01", [128, 128])
        NegSel = sb("NegSel", [128, 128])
        NegSelT = sb("NegSelT", [128, 256], BF16)
        gsig = sb("gsig", [128, 8])
        nzg = sb("nzg", [128, 2, 128])
        ynacc = sb("ynacc", [128, 2, 64])
        rr = sb("rr", [128, 16])

    ps = [pst("ps%d" % i, [128, 512]) for i in range(5)]
    ps5 = pst("ps5", [128, 1024], BF16)
    ps6 = pst("ps6", [128, 512])
    ps7 = pst("ps7", [128, 512])

    V, S, G, TE, SY = "vector", "scalar", "gpsimd", "tensor", "sync"

    P.dma(SY, lambda e: e.dma_start(out=ngs[:], in_=ng_d), "c0", (), ("ngs",))
    P.dma(SY, lambda e: e.dma_start(out=cws[:], in_=cw_d), "c1", (), ("cws",))
    P.dma(SY, lambda e: e.dma_start(out=cbs[:], in_=cb_d), "c2", (), ("cbs",))
    P.dma(SY, lambda e: e.dma_start(out=misc[:], in_=misc_d), "c3", (), ("misc",))
    P.dma(SY, lambda e: e.dma_start(out=mhg[:], in_=mhg_d), "c4", (), ("mhg",))
    P.dma(SY, lambda e: e.dma_start(out=c128[:], in_=c128_d), "c5", (), ("c128",))
    P.dma(G, lambda e: e.dma_start(out=identb[:], in_=c128_d[:, 0, :]), "c6", (), ("identb",))
    for k in range(8):
        sp_ = k % 2
        for (wd, Wt, ncol, nm) in ((wfm_d, Wfm, NFM, "f"), (wtm_d, Wtm, NTM, "t")):
            P.dma(SY, lambda e, wd=wd, ncol=ncol, k=k, sp_=sp_: e.dma_start(out=stg[:, sp_, 0:ncol], in_=wd[k * 128:(k + 1) * 128, :]),
                  "w%d" % sp_, (), ("stg%d" % sp_,))
            P.op(V, lambda e, Wt=Wt, ncol=ncol, k=k, sp_=sp_: e.tensor_scalar(out=Wt[:, k, :], in0=stg[:, sp_, 0:ncol], scalar1=ngs[:, k:k + 1], scalar2=None, op0=ALU.mult),
                 ("stg%d" % sp_, "ngs"), ("W",))
    P.op(V, lambda e: e.memset(epsc[:], EPS), (), ("epsc",))
    P.op(V, lambda e: e.memset(negpi[:], -math.pi), (), ("negpi",))
    P.op(V, lambda e: e.memset(Cst[:], 0.0), (), ("Cst",))
    P.op(V, lambda e: e.memset(Cb[:], 0.0), (), ("Cb",))
    P.op(V, lambda e: e.memset(mcol[:], 0.0), (), ("m",))
    P.op(V, lambda e: e.memset(convbuf[:], 0.0), (), ("convbuf",))
    P.op(V, lambda e: e.memset(Vaug[:], 1.0), (), ("Vaug0", "Vaug1"))

    if do_nsa:
        P.dma(G, lambda e: e.dma_start(out=W1b[:], in_=w1_d), "n0", (), ("W1b",))
        P.dma(G, lambda e: e.dma_start(out=W2b[:], in_=w2_d), "n1", (), ("W2b",))
        P.dma(G, lambda e: e.dma_start(out=posT[:], in_=cpos_d), "n2", (), ("posT",))
        P.dma(G, lambda e: e.dma_start(out=Raug[:], in_=raug_d), "n3", (), ("Raug",))
        P.dma(G, lambda e: e.dma_start(out=negm[:], in_=nm_d), "n4", (), ("negm",))
        P.dma(SY, lambda e: e.dma_start(out=selc[:], in_=sel_d), "n5", (), ("selc",))
        P.dma(SY, lambda e: e.dma_start(out=posi[:], in_=pos_d), "n6", (), ("posi",))
        for c0 in range(0, T, 2048):
            c1 = min(T, c0 + 2048)
            P.dma(G, lambda e, c0=c0, c1=c1: e.dma_start(out=Eb[:, c0:c1], in_=E_d[:, c0:c1]), "n7", (), ("Eb",))
        for c0 in range(0, TLO, 2048):
            c1 = min(TLO, c0 + 2048)
            P.dma(G, lambda e, c0=c0, c1=c1: e.dma_start(out=KE2[0:64, c0:c1], in_=E_d[0:64, c0:c1]), "n8", (), ("KE2e",))
        P.op(V, lambda e: e.memset(cpre[:], 0.0), (), ("cpre0", "cpre1"))
        P.op(V, lambda e: e.memset(kcT[:], 0.0), (), ("kcT",))
        P.op(V, lambda e: e.memset(GV[:], 0.0), (), ("GV",))
        P.op(G, lambda e: e.memset(Vs[:], 1.0), (), tuple("Vs%d" % t_ for t_ in range(NT)))
        P.op(G, lambda e: e.memset(Vw[:], 1.0), (), tuple("Vw%d" % t_ for t_ in range(8)))
        P.op(V, lambda e: e.tensor_copy(out=posf[:], in_=posi[:]), ("posi",), ("posf",))
        for f in range(8):
            P.op(V, lambda e, f=f: e.tensor_scalar(out=ang[:, :, f], in0=posf[:], scalar1=misc[:, 8 + f:9 + f], scalar2=None, op0=ALU.mult), ("posf", "misc"), ("ang",))
        for (off_, c0_) in ((0.75, 0), (0.5, 8)):
            P.op(V, lambda e, off_=off_: e.tensor_scalar(out=ang2[:], in0=ang[:], scalar1=1.0 / (2.0 * math.pi), scalar2=off_, op0=ALU.mult, op1=ALU.add), ("ang", "cossin"), ("ang2",))
            P.op(V, lambda e: e.tensor_copy(out=angi[:], in_=ang2[:]), ("ang2",), ("angi",))
            P.op(V, lambda e: e.tensor_copy(out=ang3[:], in_=angi[:]), ("angi",), ("ang3",))
            P.op(V, lambda e: e.tensor_tensor(out=ang2[:], in0=ang2[:], in1=ang3[:], op=ALU.subtract), ("ang2", "ang3"), ("ang2",))
            P.op(V, lambda e: e.tensor_scalar(out=ang3[:], in0=ang2[:], scalar1=0.0, scalar2=None, op0=ALU.is_lt), ("ang2",), ("ang3",))
            P.op(V, lambda e: e.tensor_tensor(out=ang2[:], in0=ang2[:], in1=ang3[:], op=ALU.add), ("ang2", "ang3"), ("ang2",))
            P.op(S, lambda e, c0_=c0_: e.activation(out=cossin[:, :, c0_:c0_ + 8], in_=ang2[:], func=AF.Sin, scale=2.0 * math.pi, bias=negpi[:]), ("ang2", "negpi"), ("cossin",))
        for kv in range(2):
            for l in range(32):
                P.op(TE, lambda e, kv=kv, l=l: e.matmul(ps[4][:, 300 + kv:301 + kv], lhsT=W1b[:, kv, l, :], rhs=posT[:, kv, l:l + 1], start=(l == 0), stop=(l == 31)),
                     ("W1b", "posT"), ("ps4",))
        P.op(V, lambda e: e.tensor_copy(out=biash[:, 0:2], in_=ps[4][:, 300:302]), ("ps4",), ("biash",))

    def emit_A(i):
        p = i % 2
        t0 = i * 128
        X, HN, HT, SM = "xt%d" % p, "hn%d" % p, "hT%d" % p, "sm%d" % p
        smp = sm[:, p, :]
        P.dma(SY, lambda e, p=p, t0=t0: e.dma_start(out=xt[:, p, :], in_=x_d[t0:t0 + 128, :]), "x%d" % p, (), (X,))
        P.op(S, lambda e, p=p, smp=smp: e.activation(out=junk[:], in_=xt[:, p, :], func=AF.Square, accum_out=smp[:, 0:1]), (X,), ("junk", SM + "a"))
        P.op(S, lambda e, smp=smp: e.activation(out=smp[:, 1:2], in_=smp[:, 0:1], func=AF.Sqrt, scale=1.0 / D, bias=epsc[:]), (SM + "a", "epsc"), (SM + "b",))
        P.op(V, lambda e, smp=smp: e.reciprocal(out=smp[:, 2:3], in_=smp[:, 1:2]), (SM + "b",), (SM + "c",))
        P.op(S, lambda e, p=p, smp=smp: e.activation(out=hn[:, p, :], in_=xt[:, p, :], func=AF.Copy, scale=smp[:, 2:3]), (X, SM + "c"), (HN,))
        for k in range(8):
            P.op(TE, lambda e, p=p, k=k: e.transpose(out=ps5[:, k * 128:(k + 1) * 128], in_=hn[:, p, k * 128:(k + 1) * 128], identity=identb[:]),
                 (HN, "identb"), ("ps5",))
        P.op(V, lambda e, p=p: e.tensor_copy(out=hT[:, p, :, :], in_=ps5[:].rearrange("p (k t) -> p k t", k=8)), ("ps5",), (HT,))
        for grp in range(4):
            for k in range(8):
                P.op(TE, lambda e, p=p, k=k, grp=grp: e.matmul(ps7[:, grp * 128:(grp + 1) * 128], lhsT=Wfm[:, k, grp * 128:(grp + 1) * 128], rhs=hT[:, p, k, :], start=(k == 0), stop=(k == 7)),
                     (HT, "W"), ("ps7",))
        P.op(V, lambda e: e.tensor_copy(out=convbuf[:, :, 0:3], in_=convbuf[:, :, 128:131]), ("convbuf",), ("convbuf",))
        P.op(S, lambda e: e.activation(out=convbuf[:, :, 3:131], in_=ps7[:].rearrange("p (g t) -> p g t", g=4), func=AF.Copy), ("ps7",), ("convbuf",))
        if do_nsa:
            for c in range(2):
                for k in range(8):
                    P.op(TE, lambda e, p=p, k=k, c=c: e.matmul(ps7[0:64, c * 128:(c + 1) * 128], lhsT=Wfm[:, k, 512 + c * 64:512 + (c + 1) * 64], rhs=hT[:, p, k, :], start=(k == 0), stop=(k == 7)),
                         (HT, "W"), ("ps7",))
            P.op(V, lambda e, p=p: e.tensor_copy(out=cpre[:, p, :, 0:16], in_=cpre[:, 1 - p, :, 128:144]), ("cpre%d" % (1 - p),), ("cpre%d" % p,))
            P.op(S, lambda e, p=p: e.activation(out=cpre[:, p, :, 16:144], in_=ps7[0:64, 0:256].rearrange("p (c t) -> p c t", c=2), func=AF.Copy), ("ps7",), ("cpre%d" % p,))
        for grp in range(4):
            P.op(V, lambda e, grp=grp: e.tensor_scalar(out=cacc[:, grp, :], in0=convbuf[:, grp, 0:128], scalar1=cws[:, grp, 0:1], scalar2=cbs[:, grp:grp + 1], op0=ALU.mult, op1=ALU.add),
                 ("convbuf", "cws", "cbs"), ("cacc%d" % grp,))
            for j in range(1, 4):
                P.op(V, lambda e, grp=grp, j=j: e.scalar_tensor_tensor(out=cacc[:, grp, :], in0=convbuf[:, grp, j:j + 128], scalar=cws[:, grp, j:j + 1], in1=cacc[:, grp, :], op0=ALU.mult, op1=ALU.add),
                     ("convbuf", "cws", "cacc%d" % grp), ("cacc%d" % grp,))
        P.op(S, lambda e, p=p: e.activation(out=qkT[:, p, :, :], in_=cacc[:], func=AF.Silu), tuple("cacc%d" % g_ for g_ in range(4)), ("qkT%d" % p,))
        for c in range(2):
            P.op(TE, lambda e, p=p, c=c: e.transpose(out=ps5[:, c * 128:(c + 1) * 128], in_=qkT[:, p, 2 + c, :], identity=identb[:]), ("qkT%d" % p, "identb"), ("ps5",))
        P.op(V, lambda e, p=p: e.tensor_copy(out=ktok[:, p, :], in_=ps5[:, 0:256]), ("ps5",), ("ktok%d" % p,))
        for (pb, pk_, c0, n) in ((ps7, "ps7", 0, TMA), (ps7, "ps7", TMA, TMB), (ps7, "ps7", TMA + TMB, TMC)):
            for k in range(8):
                P.op(TE, lambda e, p=p, k=k, pb=pb, c0=c0, n=n: e.matmul(pb[:, 0:n], lhsT=hT[:, p, k, :], rhs=Wtm[:, k, c0:c0 + n], start=(k == 0), stop=(k == 7)),
                     (HT, "W"), (pk_,))
            if c0 == 0:
                P.op(V, lambda e, p=p: e.tensor_copy(out=Vaug[:, p, 0:256], in_=ps7[:, 0:256]), ("ps7",), ("Vaug%d" % p,))
                P.op(S, lambda e, p=p: e.activation(out=og[:, p, :], in_=ps7[:, 256:512], func=AF.Sigmoid), ("ps7",), ("og%d" % p,))
            elif c0 == TMA:
                P.op(S, lambda e, p=p: e.activation(out=zg[:, p, :], in_=ps7[:, 0:256], func=AF.Silu), ("ps7",), ("zg%d" % p,))
                P.op(G, lambda e, p=p: e.tensor_tensor(out=og[:, p, :], in0=og[:, p, :], in1=zg[:, p, :], op=ALU.mult), ("og%d" % p, "zg%d" % p), ("og%d" % p,))
                if do_nsa:
                    P.op(V, lambda e, p=p: e.tensor_copy(out=qtok[:, p, 2:4, :], in_=ps7[:, B_QOTH:B_QOTH + 128].rearrange("p (h d) -> p h d", h=2)), ("ps7",), ("qtok%d" % p,))
                    P.op(S, lambda e, p=p: e.activation(out=nzg[:, p, :], in_=ps7[:, B_NZ:B_NZ + 128], func=AF.Silu), ("ps7",), ("nzg%d" % p,))
            else:
                P.op(V, lambda e, p=p: e.tensor_copy(out=tmC[:, p, :], in_=ps7[:, 0:TMC]), ("ps7",), ("tmC%d" % p,))

    def emit_B(i):
        p = i % 2
        t0 = i * 128
        SM = "sm%d" % p
        smp = sm[:, p, :]
        TC = "tmC%d" % p
        P.op(V, lambda e, smp=smp, p=p: e.tensor_scalar(out=smp[:, 17:18], in0=tmC[:, p, C_F:C_F + 1], scalar1=misc[:, 0:1], scalar2=None, op0=ALU.add), (TC, "misc"), (SM + "d0",))
        P.op(S, lambda e, smp=smp: e.activation(out=smp[:, 3:4], in_=smp[:, 17:18], func=AF.Exp, scale=-1.0), (SM + "d0",), (SM + "d",))
        P.op(S, lambda e, p=p, smp=smp: e.activation(out=G2[:, p, 0:1], in_=smp[:, 3:4], func=AF.Ln, scale=1.0, bias=1.0), (SM + "d",), ("G2%d" % p,))
        P.op(V, lambda e, p=p: e.tensor_scalar(out=G2[:, p, 1:2], in0=tmC[:, p, C_I:C_I + 1], scalar1=misc[:, 1:2], scalar2=None, op0=ALU.add), (TC, "misc"), ("G2%d" % p,))
        P.op(TE, lambda e, p=p: e.matmul(ps6[:, 128:130], lhsT=Utri, rhs=G2[:, p, :], start=True, stop=True), ("G2%d" % p, "c128"), ("ps6",))
        P.op(TE, lambda e, p=p: e.matmul(ps6[:, 130:132], lhsT=ones, rhs=G2[:, p, :], start=True, stop=True), ("G2%d" % p, "c128"), ("ps6",))
        P.op(V, lambda e, smp=smp: e.tensor_copy(out=smp[:, 20:24], in_=ps6[:, 128:132]), ("ps6",), (SM + "cum",))
        P.op(V, lambda e, p=p, smp=smp: e.tensor_tensor(out=smp[:, 4:5], in0=smp[:, 20:21], in1=G2[:, p, 1:2], op=ALU.add), (SM + "cum", "G2%d" % p), (SM + "e",))
        P.op(V, lambda e, smp=smp: e.tensor_scalar(out=smp[:, 5:6], in0=smp[:, 4:5], scalar1=-LN16, scalar2=None, op0=ALU.add), (SM + "e",), (SM + "f",))
        P.op(V, lambda e, smp=smp: e.tensor_scalar(out=dbg[:, 0, :], in0=ident, scalar1=smp[:, 4:5], scalar2=None, op0=ALU.mult), (SM + "e", "c128"), ("dbg0",))
        P.op(TE, lambda e: e.matmul(ps6[:, 0:128], lhsT=ones, rhs=dbg[:, 0, :], start=True, stop=True), ("dbg0", "c128"), ("ps6",))
        P.op(V, lambda e: e.tensor_tensor(out=Bm[:], in0=ps6[:, 0:128], in1=cnegB, op=ALU.add), ("ps6", "c128"), ("Bm",))
        P.op(V, lambda e, smp=smp: e.reduce_max(out=smp[:, 6:7], in_=Bm[:], axis=AX.X), ("Bm",), (SM + "g",))
        P.op(V, lambda e, smp=smp: e.reduce_max(out=smp[:, 7:8], in_=ps6[:, 0:128], axis=AX.X), ("ps6",), (SM + "h",))
        P.op(V, lambda e, smp=smp: e.tensor_tensor(out=smp[:, 8:9], in0=smp[:, 6:7], in1=mcol[:], op=ALU.max), (SM + "g", "m"), (SM + "i",))
        P.op(V, lambda e, smp=smp: e.tensor_tensor(out=smp[:, 9:10], in0=smp[:, 7:8], in1=mcol[:], op=ALU.max), (SM + "h", "m"), (SM + "j",))
        P.op(V, lambda e, smp=smp: e.tensor_scalar(out=dbg[:, 1, :], in0=ident, scalar1=smp[:, 8:9], scalar2=None, op0=ALU.mult), (SM + "i", "c128"), ("dbg1",))
        P.op(TE, lambda e: e.matmul(ps6[:, 0:128], lhsT=ones, rhs=dbg[:, 1, :], start=True, stop=True), ("dbg1", "c128"), ("ps6",))
        P.op(S, lambda e, smp=smp: e.activation(out=DT[:], in_=ps6[:, 0:128], func=AF.Exp, scale=-1.0, bias=smp[:, 5:6]), ("ps6", SM + "f"), ("DT",))
        P.op(G, lambda e: e.tensor_tensor(out=DTm[:], in0=DT[:], in1=maskT01, op=ALU.mult), ("DT", "c128"), ("DTm",))
        P.op(S, lambda e: e.activation(out=Wm[:], in_=ps6[:, 0:128], func=AF.Exp, scale=-1.0, bias=mcol[:]), ("ps6", "m"), ("Wm",))
        for c in range(2):
            P.op(G, lambda e, p=p, c=c: e.tensor_tensor(out=qwT[:, c, :], in0=qkT[:, p, c, :], in1=Wm[:], op=ALU.mult), ("qkT%d" % p, "Wm"), ("qwT",))
        for c in range(2):
            P.op(TE, lambda e, p=p, c=c: e.matmul(ps6[:, 0:128], lhsT=qkT[:, p, 2 + c, :], rhs=qkT[:, p, c, :], start=(c == 0), stop=(c == 1)), ("qkT%d" % p,), ("ps6",))
        P.op(V, lambda e: e.tensor_tensor(out=smT[:], in0=ps6[:, 0:128], in1=DTm[:], op=ALU.mult), ("ps6", "DTm"), ("smT",))
        for c in range(2):
            P.op(TE, lambda e, c=c: e.matmul(ps6[:, 132:389], lhsT=qwT[:, c, :], rhs=Cb[:, c, :], start=(c == 0), stop=False), ("qwT", "Cb"), ("ps6",))
        P.op(TE, lambda e, p=p: e.matmul(ps6[:, 132:389], lhsT=smT[:], rhs=Vaug[:, p, :], start=False, stop=True), ("smT", "Vaug%d" % p), ("ps6",))
        P.op(V, lambda e, smp=smp: e.tensor_tensor(out=smp[:, 10:11], in0=smp[:, 20:21], in1=smp[:, 8:9], op=ALU.subtract), (SM + "cum", SM + "i"), (SM + "k",))
        P.op(S, lambda e, smp=smp: e.activation(out=smp[:, 11:12], in_=smp[:, 10:11], func=AF.Exp), (SM + "k",), (SM + "l",))
        P.op(S, lambda e, smp=smp: e.activation(out=smp[:, 19:20], in_=ps6[:, 388:389], func=AF.Abs), ("ps6",), (SM + "m0",))
        P.op(V, lambda e, smp=smp: e.tensor_tensor(out=smp[:, 12:13], in0=smp[:, 19:20], in1=smp[:, 11:12], op=ALU.max), (SM + "m0", SM + "l"), (SM + "m",))
        P.op(V, lambda e, smp=smp: e.reciprocal(out=smp[:, 13:14], in_=smp[:, 12:13]), (SM + "m",), (SM + "n",))
        P.op(V, lambda e, smp=smp: e.tensor_scalar(out=hbuf[:], in0=ps6[:, 132:388], scalar1=smp[:, 13:14], scalar2=None, op0=ALU.mult), ("ps6", SM + "n"), ("hbuf",))
        P.op(S, lambda e, smp=smp: e.activation(out=smp[:, 15:16], in_=smp[:, 9:10], func=AF.Exp, scale=-1.0, bias=smp[:, 5:6]), (SM + "j", SM + "f"), (SM + "p",))
        P.op(S, lambda e, smp=smp: e.activation(out=smp[:, 16:17], in_=smp[:, 9:10], func=AF.Exp, scale=-1.0, bias=mcol[:]), (SM + "j", "m"), (SM + "q",))
        P.op(V, lambda e, smp=smp, p=p: e.tensor_scalar(out=kwt[:], in0=ktok[:, p, :], scalar1=smp[:, 15:16], scalar2=None, op0=ALU.mult), ("ktok%d" % p, SM + "p"), ("kwt",))
        for c in range(2):
            P.op(TE, lambda e, p=p, c=c: e.matmul(ps6[:, 132:389], lhsT=kwt[:, c * 128:(c + 1) * 128], rhs=Vaug[:, p, :], start=True, stop=True), ("kwt", "Vaug%d" % p), ("ps6",))
            P.op(V, lambda e, c=c, smp=smp: e.scalar_tensor_tensor(out=Cst[:, c, :], in0=Cst[:, c, :], scalar=smp[:, 16:17], in1=ps6[:, 132:389], op0=ALU.mult, op1=ALU.add),
                 ("Cst", SM + "q", "ps6"), ("Cst",))
            P.op(S, lambda e, c=c: e.activation(out=Cb[:, c, :], in_=Cst[:, c, :], func=AF.Copy), ("Cst",), ("Cb",))
        P.op(V, lambda e, smp=smp: e.tensor_tensor(out=mcol[:], in0=smp[:, 9:10], in1=smp[:, 22:23], op=ALU.subtract), (SM + "j", SM + "cum"), ("m",))
        P.op(V, lambda e: e.bn_stats(out=bnst[:, 0:6], in_=hbuf[:]), ("hbuf",), ("bnst",))
        P.op(V, lambda e: e.bn_aggr(out=bnst[:, 6:8], in_=bnst[:, 0:6]), ("bnst",), ("bnst",))
        P.op(S, lambda e, smp=smp: e.activation(out=smp[:, 18:19], in_=bnst[:, 7:8], func=AF.Sqrt, scale=1.0, bias=epsc[:]), ("bnst", "epsc"), (SM + "o0",))
        P.op(V, lambda e, smp=smp: e.reciprocal(out=smp[:, 14:15], in_=smp[:, 18:19]), (SM + "o0",), (SM + "o",))
        P.op(V, lambda e, smp=smp: e.tensor_scalar(out=hbuf2[:], in0=hbuf[:], scalar1=bnst[:, 6:7], scalar2=smp[:, 14:15], op0=ALU.subtract, op1=ALU.mult), ("hbuf", "bnst", SM + "o"), ("hbuf2",))
        P.op(G, lambda e: e.tensor_tensor(out=hbuf2[:], in0=hbuf2[:], in1=mhg[:], op=ALU.mult), ("hbuf2", "mhg"), ("hbuf2",))
        P.op(G, lambda e, p=p: e.tensor_tensor(out=yat[:, p, :], in0=hbuf2[:], in1=og[:, p, :], op=ALU.mult), ("hbuf2", "og%d" % p), ("yat%d" % p,))
        P.dma(SY, lambda e, p=p, t0=t0: e.dma_start(out=ya_d[t0:t0 + 128, :], in_=yat[:, p, :]), "ya%d" % p, ("yat%d" % p,), ("ya_out",))

    def emit_C(i):
        p = i % 2
        t0 = i * 128
        TC = "tmC%d" % p
        if do_nsa:
            psS = (ps[0], ps[1])
            psO = (ps[2], ps[3])
            QT = "qtok%d" % p
            P.op(S, lambda e, p=p: e.activation(out=qtok[:, p, 0:2, :], in_=tmC[:, p, 0:128].rearrange("p (h d) -> p h d", h=2), func=AF.Copy), (TC,), (QT,))
            P.op(S, lambda e, p=p: e.activation(out=rp[:], in_=tmC[:, p, 0:256].rearrange("p (h d) -> p h d", h=4), func=AF.Copy), (TC,), ("rp",))
            src = tmC[:, p, 0:256].rearrange("p (h d) -> p h d", h=4)
            cosb = cossin[:, i, 0:8].unsqueeze(1).to_broadcast([128, 4, 8])
            sinb = cossin[:, i, 8:16].unsqueeze(1).to_broadcast([128, 4, 8])
            P.op(V, lambda e, src=src, cosb=cosb: e.tensor_tensor(out=rtmp[:, 0], in0=src[:, :, 0:8], in1=cosb, op=ALU.mult), (TC, "cossin"), ("rtmp0",))
            P.op(V, lambda e, src=src, sinb=sinb: e.tensor_tensor(out=rtmp[:, 1], in0=src[:, :, 8:16], in1=sinb, op=ALU.mult), (TC, "cossin"), ("rtmp1",))
            P.op(V, lambda e, src=src, cosb=cosb: e.tensor_tensor(out=rtmp[:, 2], in0=src[:, :, 8:16], in1=cosb, op=ALU.mult), (TC, "cossin"), ("rtmp2",))
            P.op(V, lambda e, src=src, sinb=sinb: e.tensor_tensor(out=rtmp[:, 3], in0=src[:, :, 0:8], in1=sinb, op=ALU.mult), (TC, "cossin"), ("rtmp3",))
            P.op(V, lambda e: e.tensor_tensor(out=rp[:, :, 0:8], in0=rtmp[:, 0], in1=rtmp[:, 1], op=ALU.subtract), ("rtmp0", "rtmp1", "rp"), ("rp",))
            P.op(V, lambda e: e.tensor_tensor(out=rp[:, :, 8:16], in0=rtmp[:, 2], in1=rtmp[:, 3], op=ALU.add), ("rtmp2", "rtmp3", "rp"), ("rp",))
            P.op(V, lambda e, i=i, p=p: e.tensor_copy(out=Vs[:, i, 0:64], in_=tmC[:, p, C_VS:C_VS + 64]), (TC,), ("Vs%d" % i,))
            P.op(V, lambda e, i=i, p=p: e.tensor_copy(out=Vw[:, i % 8, 0:64], in_=tmC[:, p, C_VW:C_VW + 64]), (TC,), ("Vw%d" % (i % 8),))
            P.op(V, lambda e, p=p: e.tensor_tensor(out=gsig[:, 0:6], in0=tmC[:, p, C_G:C_G + 6], in1=misc[:, 2:8], op=ALU.add), (TC, "misc"), ("gsig",))
            P.op(S, lambda e: e.activation(out=gsig[:, 0:6], in_=gsig[:, 0:6], func=AF.Sigmoid), ("gsig",), ("gsig",))
            for h in range(4):
                P.op(TE, lambda e, h=h, p=p: e.transpose(out=ps[0][0:64, h * 128:(h + 1) * 128], in_=qtok[:, p, h, :], identity=ident), (QT, "c128"), ("ps0",))
            if i < 32:
                P.op(S, lambda e: e.activation(out=rp2[:, :, 0:64], in_=rp[:], func=AF.Copy), ("rp",), ("rp2a",))
                P.op(V, lambda e: e.tensor_copy(out=rp2[:, :, 64:128], in_=rp[:]), ("rp",), ("rp2b",))
                for h in range(4):
                    P.op(TE, lambda e, h=h: e.transpose(out=ps[1][:, h * 128:(h + 1) * 128], in_=rp2[:, h, :], identity=ident), ("rp2a", "rp2b", "c128"), ("ps1",))
                P.op(V, lambda e: e.tensor_copy(out=QN2[64:128, :], in_=ps[1][64:128, 0:256]), ("ps1",), ("QN2q",))
                P.op(V, lambda e, t0=t0: e.tensor_copy(out=KE2[64:128, t0:t0 + 128], in_=ps[1][64:128, 256:384]), ("ps1",), ("KE2k%d" % i,))
            else:
                for h in range(4):
                    P.op(TE, lambda e, h=h: e.transpose(out=ps[1][0:64, h * 128:(h + 1) * 128], in_=rp[:, h, :], identity=ident), ("rp", "c128"), ("ps1",))
            P.op(V, lambda e: e.tensor_copy(out=qrawT[:], in_=ps[0][0:64, 0:512]), ("ps0",), ("qrawT",))
            P.op(S, lambda e: e.activation(out=qropT[:], in_=ps[1][0:64, 0:256], func=AF.Copy), ("ps1",), ("qropT",))
            P.op(V, lambda e, t0=t0: e.tensor_copy(out=ksT[:, t0:t0 + 128], in_=ps[1][0:64, 256:384]), ("ps1",), ("ksT%d" % i,))
            P.op(S, lambda e, i=i: e.activation(out=kwT[:, (i % 8) * 128:(i % 8 + 1) * 128], in_=ps[1][0:64, 384:512], func=AF.Copy), ("ps1",), ("kwT%d" % (i % 8),))
            for kv in range(2):
                for l in range(32):
                    P.op(TE, lambda e, kv=kv, l=l, p=p: e.matmul(ps[4][:, 256 + kv * 8:264 + kv * 8], lhsT=W1b[:, kv, l, :], rhs=cpre[:, p, kv, l:l + 113:16], start=(l == 0), stop=(l == 31)),
                         ("W1b", "cpre%d" % p), ("ps4",))
            hid = ps[4][:, 256:272].rearrange("p (k n) -> p k n", k=2)
            P.op(V, lambda e, hid=hid: e.tensor_tensor(out=hx[:, 0, :].rearrange("p (k n) -> p k n", k=2), in0=hid, in1=biash[:].unsqueeze(2).to_broadcast([128, 2, 8]), op=ALU.add), ("ps4", "biash"), ("hx0",))
            P.op(V, lambda e: e.tensor_tensor(out=hx[:, 1, :], in0=hx[:, 0, :], in1=hx[:, 0, :], op=ALU.mult), ("hx0",), ("hx1",))
            P.op(V, lambda e: e.tensor_scalar(out=hx[:, 1, :], in0=hx[:, 1, :], scalar1=0.044715, scalar2=1.0, op0=ALU.mult, op1=ALU.add), ("hx1",), ("hx1",))
            P.op(V, lambda e: e.tensor_tensor(out=hx[:, 2, :], in0=hx[:, 1, :], in1=hx[:, 0, :], op=ALU.mult), ("hx1", "hx0"), ("hx2",))
            P.op(S, lambda e: e.activation(out=hx[:, 3, :], in_=hx[:, 2, :], func=AF.Sigmoid, scale=1.5957691216057308), ("hx2",), ("hx3",))
            P.op(V, lambda e: e.tensor_tensor(out=gl[:], in0=hx[:, 3, :], in1=hx[:, 0, :], op=ALU.mult), ("hx3", "hx0"), ("gl",))
            P.op(TE, lambda e: e.matmul(ps[4][0:64, 320:328], lhsT=W2b[:, 0, :], rhs=gl[:, 0:8], start=True, stop=True), ("W2b", "gl"), ("ps4",))
            P.op(V, lambda e, i=i: e.tensor_copy(out=kcT[:, 8 * i:8 * i + 8], in_=ps[4][0:64, 320:328]), ("ps4",), ("kcT",))
            P.op(V, lambda e, i=i: e.tensor_copy(out=GV[:, 8 * i:8 * i + 8], in_=gl[:, 8:16]), ("gl",), ("GV",))
            if i == 0:
                P.op(V, lambda e: e.memset(GV[:, 0:1], 0.0), (), ("GV",))
            cc_cur = i // 16
            P.op(TE, lambda e, cc_cur=cc_cur: e.matmul(ps[4][:, 336:400], lhsT=GV[:, cc_cur * 128:(cc_cur + 1) * 128], rhs=W2b[:, 1, :], start=True, stop=True), ("GV", "W2b"), ("ps4",))
            P.op(V, lambda e, cc_cur=cc_cur: e.tensor_copy(out=Raug[:, cc_cur, 128:192], in_=ps[4][:, 336:400]), ("ps4",), ("Raug",))
            nch = cc_cur + 1
            for cc in range(nch):
                P.op(TE, lambda e, cc=cc: e.matmul(psS[cc % 2][:, 0:512], lhsT=kcT[:, cc * 128:(cc + 1) * 128], rhs=qrawT[:], start=True, stop=True), ("kcT", "qrawT"), ("ps%d" % (cc % 2),))
                P.op(S, lambda e, cc=cc: e.activation(out=PcT[:, cc, :], in_=psS[cc % 2][:, 0:512], func=AF.Exp, scale=0.125), ("ps%d" % (cc % 2),), ("PcT%d" % cc,))
            delta = 128.0 * (i % 16) - 15.0
            P.op(V, lambda e, delta=delta: e.tensor_scalar(out=cmask[:], in0=c128[:, 5, :], scalar1=delta, scalar2=None, op0=ALU.is_le), ("c128",), ("cmask",))
            P.op(G, lambda e, cc_cur=cc_cur: e.tensor_tensor(out=PcT[:, cc_cur, :].rearrange("p (h t) -> p h t", h=4), in0=PcT[:, cc_cur, :].rearrange("p (h t) -> p h t", h=4),
                                                              in1=cmask[:].unsqueeze(1).to_broadcast([128, 4, 128]), op=ALU.mult), ("PcT%d" % cc_cur, "cmask"), ("PcT%d" % cc_cur,))
            for h in range(4):
                po = psO[h % 2]
                pk = "ps%d" % (2 + h % 2)
                s0 = (h // 2) * 256
                for cc in range(nch):
                    P.op(TE, lambda e, h=h, cc=cc, po=po, s0=s0: e.matmul(po[:, s0:s0 + 193], lhsT=PcT[:, cc, h * 128:(h + 1) * 128], rhs=Raug[:, cc, :], start=(cc == 0), stop=(cc == nch - 1)),
                         ("PcT%d" % cc, "Raug"), (pk,))
            for b_ in range(2):
                P.op(V, lambda e, b_=b_: e.tensor_scalar(out=rr[:, b_:4:2], in0=psO[b_][:, 192:449:256], scalar1=1e-30, scalar2=None, op0=ALU.max), ("ps%d" % (2 + b_),), ("rrm%d" % b_,))
            P.op(V, lambda e: e.reciprocal(out=rr[:, 0:4], in_=rr[:, 0:4]), ("rrm0", "rrm1"), ("rr",))
            for h in range(4):
                po = psO[h % 2]
                pk = "ps%d" % (2 + h % 2)
                s0 = (h // 2) * 256
                if h == 0:
                    P.op(V, lambda e, po=po: e.tensor_scalar(out=imp[:], in0=po[:, 0:128], scalar1=rr[:, 0:1], scalar2=None, op0=ALU.mult), (pk, "rr"), ("imp",))
                else:
                    P.op(V, lambda e, h=h, po=po, s0=s0: e.scalar_tensor_tensor(out=imp[:], in0=po[:, s0:s0 + 128], scalar=rr[:, h:h + 1], in1=imp[:], op0=ALU.mult, op1=ALU.add), (pk, "rr", "imp"), ("imp",))
            P.op(V, lambda e: e.tensor_tensor(out=rr[:, 4:6], in0=rr[:, 0:2], in1=gsig[:, 0:6:3], op=ALU.mult), ("rr", "gsig"), ("rg",))
            for h in range(2):
                P.op(S, lambda e, h=h: e.activation(out=ynacc[:, h, :], in_=psO[h][:, 128:192], func=AF.Copy, scale=rr[:, 4 + h:5 + h]), ("ps%d" % (2 + h), "rg"), ("ynacc%d" % h,))
            w0 = 128 - 2 * i
            P.op(V, lambda e, w0=w0: e.scalar_tensor_tensor(out=sc[:, 0, :], in0=imp[:], scalar=1.0, in1=selc[:, 0, w0:w0 + 128], op0=ALU.add, op1=ALU.mult), ("imp", "selc"), ("sc0",))
            P.op(V, lambda e, w0=w0: e.tensor_tensor(out=sc[:, 0, :], in0=sc[:, 0, :], in1=selc[:, 2, w0:w0 + 128], op=ALU.max), ("sc0", "selc"), ("sc0",))
            P.op(V, lambda e: e.tensor_tensor(out=sc[:, 0, :], in0=sc[:, 0, :], in1=selc[:, 3, 0:128], op=ALU.max), ("sc0", "selc"), ("sc0",))
            P.op(V, lambda e: e.max(out=m8[:, 0:8], in_=sc[:, 0, :]), ("sc0",), ("m8a",))
            P.op(V, lambda e: e.match_replace(out=sc[:, 1, :], in_to_replace=m8[:, 0:8], in_values=sc[:, 0, :], imm_value=-1e30), ("sc0", "m8a"), ("sc1",))
            P.op(V, lambda e: e.max(out=m8[:, 8:16], in_=sc[:, 1, :]), ("sc1",), ("m8b",))
            P.op(V, lambda e: e.tensor_scalar(out=NegSel[:], in0=sc[:, 0, :], scalar1=m8[:, 15:16], scalar2=MASKNEG, op0=ALU.is_lt, op1=ALU.mult), ("sc0", "m8b"), ("NegSel",))
            P.op(TE, lambda e: e.transpose(out=ps[0][:, 0:128], in_=NegSel[:], identity=ident), ("NegSel", "c128"), ("ps0",))
            if i < 32:
                P.op(V, lambda e: e.tensor_copy(out=QN2[0:64, 0:128], in_=ps[0][0:64, 0:128]), ("ps0",), ("QN2m",))
                P.op(S, lambda e: e.activation(out=QN2[0:64, 128:256], in_=ps[0][0:64, 0:128], func=AF.Copy), ("ps0",), ("QN2m",))
            else:
                P.op(V, lambda e: e.tensor_copy(out=NegSelT[:, 0:128], in_=ps[0][:, 0:128]), ("ps0",), ("NegSelT",))
                P.op(S, lambda e: e.activation(out=NegSelT[:, 128:256], in_=ps[0][:, 0:128], func=AF.Copy), ("ps0",), ("NegSelT",))
            def attn_branch(kbs, kT_, kkey, Vt, vkey, gate_col, use_sel, ring=None):
                sl = (lambda kb_: kb_ % ring) if ring else (lambda kb_: kb_)
                nk = len(kbs)
                chunks = [kbs[c_:c_ + 2] for c_ in range(0, nk, 2)]
                done = 0
                for n_, ch in enumerate(chunks):
                    pS = psS[n_ % 2]
                    sk_ = "ps%d" % (n_ % 2)
                    for u, kb in enumerate(ch):
                        extra = []
                        stacked = use_sel and i < 32
                        if use_sel and not stacked:
                            extra.append((Eb[:, kb * 128:(kb + 1) * 128], NegSelT[:], ("Eb", "NegSelT")))
                        if kb == i:
                            extra.append((identb[:], negm[:, 0, :], ("identb", "negm")))
                        if (not use_sel) and kb == i - 4:
                            extra.append((identb[:], negm[:, 1, :], ("identb", "negm")))
                        if stacked:
                            P.op(TE, lambda e, kb=kb, pS=pS, u=u, ne=len(extra): e.matmul(pS[:, u * 256:(u + 1) * 256], lhsT=KE2[:, kb * 128:(kb + 1) * 128], rhs=QN2[:], start=True, stop=(ne == 0)),
                                 ("KE2k%d" % kb, "KE2e", "QN2q", "QN2m"), (sk_,))
                        else:
                            P.op(TE, lambda e, kb=kb, pS=pS, u=u, ne=len(extra): e.matmul(pS[:, u * 256:(u + 1) * 256], lhsT=kT_[:, sl(kb) * 128:(sl(kb) + 1) * 128], rhs=qropT[:], start=True, stop=(ne == 0)),
                                 ("%s%d" % (kkey, sl(kb)), "qropT"), (sk_,))
                        for xi, (l_, r_, keys) in enumerate(extra):
                            P.op(TE, lambda e, l_=l_, r_=r_, pS=pS, u=u, last=(xi == len(extra) - 1): e.matmul(pS[:, u * 256:(u + 1) * 256], lhsT=l_, rhs=r_, start=False, stop=last), keys, (sk_,))
                    w_ = 256 * len(ch)
                    P.op(S, lambda e, pS=pS, n_=n_, w_=w_: e.activation(out=PT[:, n_ % 2, 0:w_], in_=pS[:, 0:w_], func=AF.Exp, scale=0.125), (sk_,), ("PT%d" % (n_ % 2),))
                    for u, kb in enumerate(ch):
                        for h in range(2):
                            P.op(TE, lambda e, h=h, kb=kb, n_=n_, u=u, first=(done == 0), last=(done == nk - 1): e.matmul(psO[h][:, 0:65], lhsT=PT[:, n_ % 2, u * 256 + h * 128:u * 256 + (h + 1) * 128], rhs=Vt[:, sl(kb), :], start=first, stop=last),
                                 ("PT%d" % (n_ % 2), "%s%d" % (vkey, sl(kb))), ("ps%d" % (2 + h),))
                        done += 1
                    if H3_TEST and use_sel and n_ == 0 and nk > 2:
                        P.op(TE, lambda e: e.transpose(out=ps5[:, 0:128], in_=identb[:], identity=identb[:]), ("identb",), ("ps5",))
                for h in range(2):
                    pk = "ps%d" % (2 + h)
                    P.op(V, lambda e, h=h: e.reciprocal(out=rr[:, 8 + h:9 + h], in_=psO[h][:, 64:65]), (pk,), ("rs%d" % h,))
                    P.op(V, lambda e, h=h: e.tensor_tensor(out=rr[:, 8 + h:9 + h], in0=rr[:, 8 + h:9 + h], in1=gsig[:, 3 * h + gate_col:3 * h + gate_col + 1], op=ALU.mult), ("rs%d" % h, "gsig"), ("rs%d" % h,))
                    P.op(V, lambda e, h=h: e.scalar_tensor_tensor(out=ynacc[:, h, :], in0=psO[h][:, 0:64], scalar=rr[:, 8 + h:9 + h], in1=ynacc[:, h, :], op0=ALU.mult, op1=ALU.add),
                         (pk, "rs%d" % h, "ynacc%d" % h), ("ynacc%d" % h,))

            marks.append(len(P.cap) if P.cap is not None else 0)
            attn_branch(list(range(i + 1)), ksT, "ksT", Vs, "Vs", 1, True)
            attn_branch(list(range(max(0, i - 4), i + 1)), kwT, "kwT", Vw, "Vw", 2, False, ring=8)
            P.op(G, lambda e, p=p: e.tensor_tensor(out=ybt[:, p, :], in0=ynacc[:].rearrange("p h d -> p (h d)"), in1=nzg[:, p, :], op=ALU.mult), ("ynacc0", "ynacc1", "nzg%d" % p), ("ybt%d" % p,))
            P.dma(SY, lambda e, p=p, t0=t0: e.dma_start(out=yb_d[t0:t0 + 128, :], in_=ybt[:, p, :]), "yb%d" % p, ("ybt%d" % p,), ("yb_out",))

    marks = []
    emit_A(0)
    for i in range(NT):
        if MERGE_MODE == 0:
            emit_B(i)
            if do_nsa:
                emit_C(i)
            if i + 1 < NT:
                emit_A(i + 1)
            continue
        P.cap = []
        emit_B(i)
        LB = P.cap
        P.cap = []
        if i + 1 < NT:
            emit_A(i + 1)
        LA = P.cap
        P.cap = []
        del marks[:]
        emit_C(i)
        LY = P.cap
        P.cap = None
        mk = marks[0]
        for kind, args in LY[:mk]:
            (P.op if kind == "op" else P.dma)(*args)
        P.replay_merged(LB, LA, LY[mk:])
    P.final_wait(SY, ("ya_out",) + (("yb_out",) if do_nsa else ()))
    P.emit(es)
    es.close()
    return nc


def _c128():
    c = np.zeros((128, 6, 128), np.float32)
    r = np.arange(128)
    c[:, 0, :] = np.eye(128)
    c[:, 1, :] = (r[:, None] <= r[None, :])
    c[:, 2, :] = 1.0
    c[:, 3, :] = np.where(r[None, :] <= r[:, None], 0.0, NEGB)
    c[:, 4, :] = (r[:, None] <= r[None, :])
    c[:, 5, :] = 16.0 * r[:, None] - r[None, :]
    return c


def mixer_inputs(inp, b, j, T):
    NT = T // 128
    g, pr = j // 2, j % 2
    w = inp["w_in"][0]
    off = {}
    o = 0
    for n_, wd in (("m_q", 1024), ("m_k", 1024), ("m_v", 1024), ("m_o", 1024), ("m_z", 1024), ("m_i", 4), ("m_f", 4),
                   ("n_q", 512), ("n_kc", 128), ("n_vc", 128), ("n_ks", 128), ("n_vs", 128), ("n_kw", 128), ("n_vw", 128),
                   ("n_g", 24), ("n_z", 512), ("g_a", 1024), ("g_b", 1024)):
        off[n_] = o
        o += wd
    hs = slice(256 * j, 256 * (j + 1))

    def cols(name, lo, hi):
        return w[:, off[name] + lo: off[name] + hi]
    gq = 256 * g
    own = [2 * pr, 2 * pr + 1]
    oth = [2 * (1 - pr), 2 * (1 - pr) + 1]
    wfm = np.concatenate([cols("m_q", 256 * j, 256 * j + 256), cols("m_k", 256 * j, 256 * j + 256),
                          cols("n_kc", 64 * g, 64 * g + 64), cols("n_vc", 64 * g, 64 * g + 64)], axis=1)
    hq = lambda h: cols("n_q", gq + 64 * h, gq + 64 * h + 64)
    hg = 4 * g
    wtm = np.concatenate([
        cols("m_v", 256 * j, 256 * j + 256), cols("m_o", 256 * j, 256 * j + 256),
        cols("m_z", 256 * j, 256 * j + 256), hq(oth[0]), hq(oth[1]),
        cols("n_z", 64 * (hg + own[0]), 64 * (hg + own[0]) + 128),
        hq(own[0]), hq(own[1]), cols("n_ks", 64 * g, 64 * g + 64), cols("n_kw", 64 * g, 64 * g + 64),
        cols("n_vs", 64 * g, 64 * g + 64), cols("n_vw", 64 * g, 64 * g + 64),
        cols("m_i", j, j + 1), cols("m_f", j, j + 1), cols("n_g", 3 * (hg + own[0]), 3 * (hg + own[0]) + 6),
    ], axis=1)
    assert wfm.shape[1] == NFM and wtm.shape[1] == NTM, (wfm.shape, wtm.shape)
    convw = np.zeros((128, 4, 4), np.float32)
    convb = np.zeros((128, 4), np.float32)
    for gi, (cw, cb) in enumerate(((inp["conv_q_w"][0], inp["conv_q_b"][0]), (inp["conv_k_w"][0], inp["conv_k_b"][0]))):
        for c in range(2):
            convw[:, 2 * gi + c, :] = cw[:, 256 * j + 128 * c: 256 * j + 128 * (c + 1)].T
            convb[:, 2 * gi + c] = cb[256 * j + 128 * c: 256 * j + 128 * (c + 1)]
    misc = np.zeros((128, 16), np.float32)
    misc[:, 0] = inp["b_fgate"][0, j]
    misc[:, 1] = inp["b_igate"][0, j]
    misc[:, 2:8] = inp["b_nsa_gate"][0, 3 * (hg + own[0]): 3 * (hg + own[0]) + 6][None, :]
    misc[:, 8:16] = (np.float32(500000.0) ** (-np.arange(8, dtype=np.float32) * np.float32(2.0 / 16)))[None, :]
    m = {
        "x": np.ascontiguousarray(inp["x"][b, :T]),
        "pos": np.ascontiguousarray(inp["positions"][b, :T].reshape(NT, 128).T.astype(np.int32)),
        "wfm": np.ascontiguousarray(wfm), "wtm": np.ascontiguousarray(wtm),
        "ng": np.ascontiguousarray(inp["norm_g"][0].reshape(8, 128).T),
        "convw": convw, "convb": convb, "misc": misc,
        "mhg": np.ascontiguousarray(np.broadcast_to(inp["mh_norm_g"][0, hs][None, :], (128, 256))),
        "c128": _c128(),
    }
    r = np.arange(128)
    w1 = np.stack([inp["cmp_w1_k"][0].reshape(32, 64, 128).transpose(1, 0, 2), inp["cmp_w1_v"][0].reshape(32, 64, 128).transpose(1, 0, 2)], axis=1)
    m["cw1"] = np.ascontiguousarray(w1)
    m["cw2"] = np.ascontiguousarray(np.stack([inp["cmp_w2_k"][0], inp["cmp_w2_v"][0]], axis=1))
    m["cpos"] = np.ascontiguousarray(np.stack([inp["cmp_pos_k"][0].T, inp["cmp_pos_v"][0].T], axis=1))
    m.update(_nsa_consts(T))
    return m


_NSA_CONSTS = {}


def _nsa_consts(T):
    if T in _NSA_CONSTS:
        return _NSA_CONSTS[T]
    r = np.arange(128)
    raug = np.zeros((128, 4, 193), np.float32)
    cidx = (np.arange(4)[None, :] * 128 + r[:, None])
    n = cidx - 1
    sidx = np.arange(128)
    ov = ((16 * n[:, :, None] < 64 * sidx + 64) & (16 * n[:, :, None] + 32 > 64 * sidx)).astype(np.float32)
    ov[cidx == 0] = 0.0
    raug[:, :, 0:128] = ov
    raug[:, :, 192] = (cidx > 0)
    E = (np.arange(T)[None, :] // 64 == r[:, None]).astype(np.float32)
    selc = np.zeros((128, 4, 256), np.float32)
    rel = np.arange(256)[None, :] - 128
    hi = (r[:, None] >= 64).astype(np.int64)
    valid = (rel <= hi)
    selc[:, 0, :] = valid
    selc[:, 1, :] = valid.astype(np.float32) - 1.0
    selc[:, 2, :] = np.where((rel == hi) | (rel == hi - 1), 1e9, 0.0)
    selc[:, 3, 0] = 1e9
    negm = np.zeros((128, 2, 256), np.float32)
    for h in range(2):
        negm[:, 0, h * 128:(h + 1) * 128] = np.where(r[:, None] > r[None, :], MASKNEG, 0.0)
        negm[:, 1, h * 128:(h + 1) * 128] = np.where(r[:, None] <= r[None, :], MASKNEG, 0.0)
    out = {"raug": raug, "E": E, "selc": selc, "negm": negm}
    _NSA_CONSTS[T] = out
    return out


def emit_tail(nc, P, es, NTT, x_d, ya_tile, yb_tile, wg_d, wa_d, wb_d, wo_d, ng_d, fg_d, c128_d, out_d, pfx="t"):
    def sb(name, shape, dt=F32):
        return es.enter_context(nc.sbuf_tensor(pfx + name, list(shape), dt))

    def pst(name, shape, dt=F32):
        return es.enter_context(nc.psum_tensor(pfx + name, list(shape), dt))

    K = lambda s: pfx + s
    V, S, G, TE, SY = "vector", "scalar", "gpsimd", "tensor", "sync"
    Wg = sb("Wg", [128, 8, 2048], BF16)
    Wa = sb("Wa", [128, 8, 1024], BF16)
    Wb = sb("Wb", [128, 4, 1024], BF16)
    Wo = sb("Wo", [128, 8, 1024], BF16)
    stg = sb("stg", [128, 2, 2048])
    ngs = sb("ngs", [128, 8])
    fgs = sb("fgs", [128, 1024])
    identb = sb("identb", [128, 128], BF16)
    epsc = sb("epsc", [128, 1])
    xt = sb("xt", [128, 2, D])
    junk = sb("junk", [128, D], BF16)
    hn = sb("hn", [128, D], BF16)
    hT = sb("hT", [128, 8, 128], BF16)
    yab = sb("yab", [128, 2, 1024], BF16)
    ybb = sb("ybb", [128, 2, 512], BF16)
    yT = sb("yT", [128, 2, 12, 128], BF16)
    sg = sb("sg", [128, 2, 2048], BF16)
    junk2 = sb("junk2", [128, D], BF16)
    t1 = sb("t1", [128, 1024])
    t2 = sb("t2", [128, 1024])
    mg = sb("mg", [128, 1024], BF16)
    mT = sb("mT", [128, 8, 128], BF16)
    xo = sb("xo", [128, 1024])
    ot = sb("ot", [128, 2, 1024])
    sm = sb("sm", [128, 2, 8])
    ps = [pst("ps%d" % i, [128, 512]) for i in range(6)]
    psT = pst("psT", [128, 1024], BF16)
    psT2 = pst("psT2", [128, 1024], BF16)

    P.dma(SY, lambda e: e.dma_start(out=ngs[:], in_=ng_d), K("c0"), (), (K("ngs"),))
    P.dma(SY, lambda e: e.dma_start(out=fgs[:], in_=fg_d), K("c1"), (), (K("fgs"),))
    P.dma(G, lambda e: e.dma_start(out=identb[:], in_=c128_d[:, 0, :]), K("c2"), (), (K("identb"),))
    P.op(V, lambda e: e.memset(epsc[:], EPS), (), (K("epsc"),))
    for k in range(8):
        sp_ = k % 2
        P.dma(SY, lambda e, k=k, sp_=sp_: e.dma_start(out=stg[:, sp_, :], in_=wg_d[k * 128:(k + 1) * 128, :]), K("w%d" % sp_), (), (K("stg%d" % sp_),))
        P.op(V, lambda e, k=k, sp_=sp_: e.tensor_scalar(out=Wg[:, k, :], in0=stg[:, sp_, :], scalar1=ngs[:, k:k + 1], scalar2=None, op0=ALU.mult), (K("stg%d" % sp_), K("ngs")), (K("W"),))
    for k in range(8):
        P.dma(G, lambda e, k=k: e.dma_start(out=Wa[:, k, :], in_=wa_d[k * 128:(k + 1) * 128, :]), K("wa"), (), (K("W"),))
        P.dma(G, lambda e, k=k: e.dma_start(out=Wo[:, k, :], in_=wo_d[k * 128:(k + 1) * 128, :]), K("wo"), (), (K("W"),))
    for k in range(4):
        P.dma(G, lambda e, k=k: e.dma_start(out=Wb[:, k, :], in_=wb_d[k * 128:(k + 1) * 128, :]), K("wb"), (), (K("W"),))

    def T1(i):
        p = i % 2
        t0 = i * 128
        X, SM = K("xt%d" % p), K("sm%d" % p)
        smp = sm[:, p, :]
        P.dma(SY, lambda e, p=p, t0=t0: e.dma_start(out=xt[:, p, :], in_=x_d[t0:t0 + 128, :]), K("x%d" % p), (), (X,))
        P.dma(SY, lambda e, p=p, i=i: e.dma_start(out=yab[:, p, :], in_=ya_tile(i)), K("ya%d" % p), (), (K("yab%d" % p),))
        P.dma(SY, lambda e, p=p, i=i: e.dma_start(out=ybb[:, p, :], in_=yb_tile(i)), K("yb%d" % p), (), (K("ybb%d" % p),))
        P.op(S, lambda e, p=p, smp=smp: e.activation(out=junk[:], in_=xt[:, p, :], func=AF.Square, accum_out=smp[:, 0:1]), (X,), (K("junk"), SM + "a"))
        P.op(S, lambda e, smp=smp: e.activation(out=smp[:, 1:2], in_=smp[:, 0:1], func=AF.Sqrt, scale=1.0 / D, bias=epsc[:]), (SM + "a", K("epsc")), (SM + "b",))
        P.op(V, lambda e, smp=smp: e.reciprocal(out=smp[:, 2:3], in_=smp[:, 1:2]), (SM + "b",), (SM + "c",))
        P.op(S, lambda e, p=p, smp=smp: e.activation(out=hn[:], in_=xt[:, p, :], func=AF.Copy, scale=smp[:, 2:3]), (X, SM + "c"), (K("hn"),))
        for k in range(8):
            P.op(TE, lambda e, k=k: e.transpose(out=psT[:, k * 128:(k + 1) * 128], in_=hn[:, k * 128:(k + 1) * 128], identity=identb[:]), (K("hn"), K("identb")), (K("psT"),))
        P.op(V, lambda e: e.tensor_copy(out=hT[:], in_=psT[:].rearrange("p (k t) -> p k t", k=8)), (K("psT"),), (K("hT"),))
        for bk in range(4):
            pb, pk = ps[bk % 2], K("ps%d" % (bk % 2))
            for k in range(8):
                P.op(TE, lambda e, k=k, bk=bk, pb=pb: e.matmul(pb[:, :], lhsT=hT[:, k, :], rhs=Wg[:, k, bk * 512:(bk + 1) * 512], start=(k == 0), stop=(k == 7)), (K("hT"), K("W")), (pk,))
            P.op(S, lambda e, bk=bk, pb=pb, p=p: e.activation(out=sg[:, p, bk * 512:(bk + 1) * 512], in_=pb[:, :], func=AF.Sigmoid), (pk,), (K("sg%d_%d" % (p, bk)),))
        for k in range(8):
            P.op(TE, lambda e, k=k, p=p: e.transpose(out=psT[:, k * 128:(k + 1) * 128], in_=yab[:, p, k * 128:(k + 1) * 128], identity=identb[:]), (K("yab%d" % p), K("identb")), (K("psT"),))
        P.op(V, lambda e, p=p: e.tensor_copy(out=yT[:, p, 0:8, :], in_=psT[:].rearrange("p (k t) -> p k t", k=8)), (K("psT"),), (K("yTa%d" % p),))
        for k in range(4):
            P.op(TE, lambda e, k=k, p=p: e.transpose(out=psT[:, k * 128:(k + 1) * 128], in_=ybb[:, p, k * 128:(k + 1) * 128], identity=identb[:]), (K("ybb%d" % p), K("identb")), (K("psT"),))
        P.op(V, lambda e, p=p: e.tensor_copy(out=yT[:, p, 8:12, :], in_=psT[:, 0:512].rearrange("p (k t) -> p k t", k=4)), (K("psT"),), (K("yTb%d" % p),))

    def T2(i):
        p = i % 2
        t0 = i * 128
        X, SM = K("xt%d" % p), K("sm%d" % p)
        smp = sm[:, p, :]
        for half in range(2):
            for k in range(8):
                P.op(TE, lambda e, k=k, half=half, p=p: e.matmul(ps[2 + half][:, :], lhsT=yT[:, p, k, :], rhs=Wa[:, k, half * 512:(half + 1) * 512], start=(k == 0), stop=(k == 7)), (K("yTa%d" % p), K("W")), (K("ps%d" % (2 + half)),))
            P.op(V, lambda e, half=half, p=p: e.tensor_tensor(out=t1[:, half * 512:(half + 1) * 512], in0=ps[2 + half][:, :], in1=sg[:, p, half * 512:(half + 1) * 512], op=ALU.mult),
                 (K("ps%d" % (2 + half)), K("sg%d_%d" % (p, half))), (K("t1%d" % half),))
        for half in range(2):
            for k in range(4):
                P.op(TE, lambda e, k=k, half=half, p=p: e.matmul(ps[2 + half][:, :], lhsT=yT[:, p, 8 + k, :], rhs=Wb[:, k, half * 512:(half + 1) * 512], start=(k == 0), stop=(k == 3)), (K("yTb%d" % p), K("W")), (K("ps%d" % (2 + half)),))
            P.op(V, lambda e, half=half, p=p: e.tensor_tensor(out=t2[:, half * 512:(half + 1) * 512], in0=ps[2 + half][:, :], in1=sg[:, p, 1024 + half * 512:1024 + (half + 1) * 512], op=ALU.mult),
                 (K("ps%d" % (2 + half)), K("sg%d_%d" % (p, 2 + half))), (K("t2%d" % half),))
            P.op(G, lambda e, half=half: e.tensor_tensor(out=mg[:, half * 512:(half + 1) * 512], in0=t1[:, half * 512:(half + 1) * 512], in1=t2[:, half * 512:(half + 1) * 512], op=ALU.add),
                 (K("t1%d" % half), K("t2%d" % half)), (K("mg%d" % half),))
        for k in range(8):
            P.op(TE, lambda e, k=k: e.transpose(out=psT2[:, k * 128:(k + 1) * 128], in_=mg[:, k * 128:(k + 1) * 128], identity=identb[:]), (K("mg0"), K("mg1"), K("identb")), (K("psT2"),))
        P.op(V, lambda e: e.tensor_copy(out=mT[:], in_=psT2[:].rearrange("p (k t) -> p k t", k=8)), (K("psT2"),), (K("mT"),))
        for half in range(2):
            for k in range(8):
                P.op(TE, lambda e, k=k, half=half: e.matmul(ps[4 + half][:, :], lhsT=mT[:, k, :], rhs=Wo[:, k, half * 512:(half + 1) * 512], start=(k == 0), stop=(k == 7)), (K("mT"), K("W")), (K("ps%d" % (4 + half)),))
            P.op(V, lambda e, half=half, p=p: e.tensor_tensor(out=xo[:, half * 512:(half + 1) * 512], in0=ps[4 + half][:, :], in1=xt[:, p, half * 512:(half + 1) * 512], op=ALU.add),
                 (K("ps%d" % (4 + half)), X), (K("xo%d" % half),))
        P.op(S, lambda e, smp=smp: e.activation(out=junk2[:], in_=xo[:], func=AF.Square, accum_out=smp[:, 3:4]), (K("xo0"), K("xo1")), (K("junk2"), SM + "d"))
        P.op(S, lambda e, smp=smp: e.activation(out=smp[:, 4:5], in_=smp[:, 3:4], func=AF.Sqrt, scale=1.0 / D, bias=epsc[:]), (SM + "d", K("epsc")), (SM + "e",))
        P.op(V, lambda e, smp=smp: e.reciprocal(out=smp[:, 5:6], in_=smp[:, 4:5]), (SM + "e",), (SM + "f",))
        P.op(V, lambda e, smp=smp, p=p: e.scalar_tensor_tensor(out=ot[:, p, :], in0=xo[:], scalar=smp[:, 5:6], in1=fgs[:], op0=ALU.mult, op1=ALU.mult), (K("xo0"), K("xo1"), SM + "f", K("fgs")), (K("ot%d" % p),))
        P.dma(SY, lambda e, p=p, t0=t0: e.dma_start(out=out_d[t0:t0 + 128, :], in_=ot[:, p, :]), K("o%d" % p), (K("ot%d" % p),), (K("out"),))

    T1(0)
    for i in range(NTT):
        P.cap = []
        T2(i)
        L2 = P.cap
        P.cap = []
        if i + 1 < NTT:
            T1(i + 1)
        L1 = P.cap
        P.cap = None
        if TAIL_MERGE:
            P.replay_merged(L2, L1)
        else:
            P.replay_merged(L2, [])
            P.replay_merged(L1, [])
    return (K("out"),)


def build_tail(NTT):
    nc = bass.Bass("TRN2", target_bir_lowering=False)
    P = Prog(nc)
    es = ExitStack()
    TT = NTT * 128

    def din(name, shape, dt=F32):
        return nc.dram_tensor(name, list(shape), dt, kind="ExternalInput").ap()

    x_d = din("x", [TT, D])
    ya_d = din("ya", [TT, 1024], BF16)
    yb_d = din("yb", [TT, 512], BF16)
    wg_d = din("wg", [D, 2048])
    wa_d = din("wa", [1024, 1024])
    wb_d = din("wb", [512, 1024])
    wo_d = din("wo", [1024, 1024])
    ng_d = din("ng", [128, 8])
    fg_d = din("fg", [128, 1024])
    c128_d = din("c128", [128, 6, 128])
    out_d = nc.dram_tensor("out", [TT, D], F32, kind="ExternalOutput").ap()
    keys = emit_tail(nc, P, es, NTT, x_d, lambda i: ya_d[i * 128:(i + 1) * 128, :], lambda i: yb_d[i * 128:(i + 1) * 128, :],
                     wg_d, wa_d, wb_d, wo_d, ng_d, fg_d, c128_d, out_d)
    P.final_wait("sync", keys)
    P.emit(es)
    es.close()
    return nc


def tail_inputs(inp, c, NTT, ya_full, yb_full, T):
    TT = NTT * 128
    b, q = c // 4, c % 4
    w = inp["w_in"][0]
    sl = slice(q * TT, (q + 1) * TT)
    return {
        "x": np.ascontiguousarray(inp["x"][b, :T][sl]),
        "ya": np.ascontiguousarray(ya_full[b][sl]),
        "yb": np.ascontiguousarray(yb_full[b][sl]),
        "wg": np.ascontiguousarray(w[:, 8992 - 2048:]),
        "wa": np.ascontiguousarray(inp["w_branch_a"][0]),
        "wb": np.ascontiguousarray(inp["w_branch_b"][0]),
        "wo": np.ascontiguousarray(inp["w_out"][0]),
        "ng": np.ascontiguousarray(inp["norm_g"][0].reshape(8, 128).T),
        "fg": np.ascontiguousarray(np.broadcast_to(inp["final_norm_g"][None, :], (128, 1024))),
        "c128": _c128(),
    }


_CACHE = {}


def kernel(**inputs):
    inp = {k: np.asarray(v) for k, v in inputs.items()}
    T = inp["x"].shape[1]
    if ("mix", T) not in _CACHE:
        _CACHE[("mix", T)] = build_mixer(T, do_nsa=True)
    nc1 = _CACHE[("mix", T)]
    maps = [mixer_inputs(inp, c // 4, c % 4, T) for c in range(8)]
    res = run_bass_kernel_spmd(nc1, maps, core_ids=list(range(8)))
    ya_full = [np.concatenate([np.asarray(res.results[4 * b + j]["ya"]) for j in range(4)], axis=1) for b in range(2)]
    yb_full = [np.concatenate([np.asarray(res.results[4 * b + j]["yb"]) for j in range(4)], axis=1) for b in range(2)]
    NTT = T // 128 // 4
    if ("tail", NTT) not in _CACHE:
        _CACHE[("tail", NTT)] = build_tail(NTT)
    nc2 = _CACHE[("tail", NTT)]
    maps2 = [tail_inputs(inp, c, NTT, ya_full, yb_full, T) for c in range(8)]
    res2 = run_bass_kernel_spmd(nc2, maps2, core_ids=list(range(8)))
    out = np.stack([np.concatenate([np.asarray(res2.results[4 * b + q]["out"]) for q in range(4)], axis=0) for b in range(2)], axis=0)
    return out.astype(np.float32)
```
